# Optimizing a Trainium2 kernel written in Bass

```python
import jax, jax.numpy as jnp
from jax import lax
import numpy as np

D_MODEL = 1024
BATCH = 16
SEQ = 2048
DEPTH = 1

HEAD_DIM = 64
ROPE_DIM = HEAD_DIM // 4
ROPE_THETA = 500000.0
NORM_EPS = 1e-6
ATTN_SCALE = HEAD_DIM ** -0.5

NSA_HEADS = 8
NSA_GROUPS = 2
NSA_REP = NSA_HEADS // NSA_GROUPS
CMP_LEN = 32
CMP_STRIDE = 16
CMP_HIDDEN = 256
SEL_BLOCK = 64
SEL_TOPN = 8
WINDOW = 512
WIN_QBLOCK = 128
NSA_QCHUNK = 64

MOBA_HEADS = 8
MOBA_BLOCK = 256
MOBA_TOPK = 3
MOBA_QCHUNK = 16

PAD_MULT = 256
D_FF = ((-(-8 * D_MODEL // 3)) + 255) // 256 * 256
PLE_DIM = 256

IN_SPLITS = ((NSA_HEADS * HEAD_DIM,) + (NSA_GROUPS * HEAD_DIM,) * 6 + (3 * NSA_HEADS,)
             + (MOBA_HEADS * HEAD_DIM,) * 3 + (D_MODEL, D_MODEL))
IN_COLS = sum(IN_SPLITS)
IN_CUTS = tuple(int(c) for c in np.cumsum(IN_SPLITS)[:-1])

kernel_name = 'hybrid_nsa_moba_swiglu_ple'


def rms_norm(x, g):
    xf = x.astype(jnp.float32)
    y = xf * lax.rsqrt(jnp.mean(xf * xf, axis=-1, keepdims=True) + NORM_EPS)
    return (y * g.astype(jnp.float32)).astype(x.dtype)


def partial_rope(t, pos):
    half = ROPE_DIM // 2
    inv_freq = ROPE_THETA ** (-jnp.arange(half, dtype=jnp.float32) / half)
    ang = pos.astype(jnp.float32)[..., None] * inv_freq
    cos, sin = jnp.cos(ang), jnp.sin(ang)
    tr = t[..., :ROPE_DIM].astype(jnp.float32)
    t1, t2 = tr[..., :half], tr[..., half:]
    rot = jnp.concatenate([t1 * cos - t2 * sin, t2 * cos + t1 * sin], axis=-1)
    return jnp.concatenate([rot.astype(t.dtype), t[..., ROPE_DIM:]], axis=-1)


def masked_softmax(scores, mask):
    s = jnp.where(mask, scores.astype(jnp.float32), -jnp.inf)
    m = jnp.max(s, axis=-1, keepdims=True)
    m = jnp.where(jnp.isfinite(m), m, 0.0)
    e = jnp.where(mask, jnp.exp(s - m), 0.0)
    return e / jnp.maximum(jnp.sum(e, axis=-1, keepdims=True), 1e-30)


def to_heads(t, n):
    b, s, _ = t.shape
    return t.reshape(b, s, n, HEAD_DIM).transpose(0, 2, 1, 3)


def from_heads(t):
    b, n, s, d = t.shape
    return t.transpose(0, 2, 1, 3).reshape(b, s, n * d)


def compress_blocks(t, pe, w1, w2):
    b, g, sp, d = t.shape
    n_sub, per = sp // CMP_STRIDE, CMP_LEN // CMP_STRIDE
    sub = t.reshape(b, g, n_sub, CMP_STRIDE, d)
    nc = n_sub - per + 1
    blocks = jnp.concatenate([sub[:, :, i:i + nc] for i in range(per)], axis=3)
    flat = (blocks + pe).reshape(b, g, nc, CMP_LEN * d)
    return jax.nn.silu(flat @ w1) @ w2


def nsa_mixer(q_in, kc_in, vc_in, ks_in, vs_in, kw_in, vw_in, gate_in, pos_pad,
              q_gain, kc_gain, ks_gain, kw_gain, pe_k, pe_v, ck_w1, ck_w2, cv_w1, cv_w2):
    b, sp, _ = q_in.shape
    G, R, dh = NSA_GROUPS, NSA_REP, HEAD_DIM
    pos_h = pos_pad[:, None, :]
    t_idx = jnp.arange(sp)
    q = partial_rope(rms_norm(to_heads(q_in, NSA_HEADS), q_gain), pos_h).reshape(b, G, R, sp, dh)

    kc = rms_norm(compress_blocks(to_heads(kc_in, G), pe_k, ck_w1, ck_w2), kc_gain)
    vc = compress_blocks(to_heads(vc_in, G), pe_v, cv_w1, cv_w2)
    nc = kc.shape[2]
    cmp_start = jnp.arange(nc) * CMP_STRIDE
    cmp_end = cmp_start + CMP_LEN - 1
    kc = partial_rope(kc, pos_pad[:, cmp_end][:, None, :])
    s_c = jnp.einsum('bgrtd,bgcd->bgrtc', q, kc) * ATTN_SCALE
    p_c = masked_softmax(s_c, cmp_end[None, :] <= t_idx[:, None])
    o_c = jnp.einsum('bgrtc,bgcd->bgrtd', p_c.astype(vc.dtype), vc)

    ns = sp // SEL_BLOCK
    j = jnp.arange(ns)
    overlap = ((cmp_start[:, None] <= j[None, :] * SEL_BLOCK + SEL_BLOCK - 1)
               & (cmp_end[:, None] >= j[None, :] * SEL_BLOCK)).astype(jnp.float32)
    imp = jnp.einsum('bgrtc,cj->bgtj', p_c, overlap)
    cur = (t_idx // SEL_BLOCK)[:, None]
    forced = (j[None, :] == 0) | (j[None, :] == cur) | (j[None, :] == cur - 1)
    imp = jnp.where(forced, jnp.inf, jnp.where(j[None, :] > cur, -jnp.inf, imp))
    n_top = min(SEL_TOPN, ns)
    _, sel_idx = lax.top_k(imp, n_top)

    ks = partial_rope(rms_norm(to_heads(ks_in, G), ks_gain), pos_h)
    vs = to_heads(vs_in, G)
    ks_blk = ks.reshape(b, G, ns, SEL_BLOCK, dh)
    vs_blk = vs.reshape(b, G, ns, SEL_BLOCK, dh)
    nq = sp // NSA_QCHUNK
    bi = jnp.arange(b)[:, None, None, None]
    gi = jnp.arange(G)[None, :, None, None]
    m_sel = n_top * SEL_BLOCK

    def sel_chunk(args):
        qc, ic, tc = args
        kg = ks_blk[bi, gi, ic].reshape(b, G, NSA_QCHUNK, m_sel, dh)
        vg = vs_blk[bi, gi, ic].reshape(b, G, NSA_QCHUNK, m_sel, dh)
        kpos = (ic[..., None] * SEL_BLOCK + jnp.arange(SEL_BLOCK)).reshape(b, G, NSA_QCHUNK, m_sel)
        mask = (kpos <= tc[None, None, :, None])[:, :, None]
        s = jnp.einsum('bgrcd,bgcmd->bgrcm', qc, kg) * ATTN_SCALE
        pr = masked_softmax(s, mask)
        return jnp.einsum('bgrcm,bgcmd->bgrcd', pr.astype(vg.dtype), vg)

    q_ch = q.reshape(b, G, R, nq, NSA_QCHUNK, dh).transpose(3, 0, 1, 2, 4, 5)
    idx_ch = sel_idx.reshape(b, G, nq, NSA_QCHUNK, n_top).transpose(2, 0, 1, 3, 4)
    o_s = lax.map(sel_chunk, (q_ch, idx_ch, t_idx.reshape(nq, NSA_QCHUNK)))
    o_s = o_s.transpose(1, 2, 3, 0, 4, 5).reshape(b, G, R, sp, dh)

    kw = partial_rope(rms_norm(to_heads(kw_in, G), kw_gain), pos_h)
    vw = to_heads(vw_in, G)
    nw, nprev = sp // WIN_QBLOCK, WINDOW // WIN_QBLOCK
    band = (nprev + 1) * WIN_QBLOCK

    def banded(t):
        tb = jnp.pad(t.reshape(b, G, nw, WIN_QBLOCK, dh), ((0, 0), (0, 0), (nprev, 0), (0, 0), (0, 0)))
        return jnp.concatenate([tb[:, :, i:i + nw] for i in range(nprev + 1)], axis=3).transpose(2, 0, 1, 3, 4)

    def win_block(args):
        qb, kb, vb, w = args
        qpos = w * WIN_QBLOCK + jnp.arange(WIN_QBLOCK)
        kpos = (w - nprev) * WIN_QBLOCK + jnp.arange(band)
        diff = qpos[:, None] - kpos[None, :]
        mask = (diff >= 0) & (diff < WINDOW) & (kpos[None, :] >= 0)
        s = jnp.einsum('bgrqd,bgkd->bgrqk', qb, kb) * ATTN_SCALE
        pr = masked_softmax(s, mask)
        return jnp.einsum('bgrqk,bgkd->bgrqd', pr.astype(vb.dtype), vb)

    q_w = q.reshape(b, G, R, nw, WIN_QBLOCK, dh).transpose(3, 0, 1, 2, 4, 5)
    o_w = lax.map(win_block, (q_w, banded(kw), banded(vw), jnp.arange(nw)))
    o_w = o_w.transpose(1, 2, 3, 0, 4, 5).reshape(b, G, R, sp, dh)

    g = jax.nn.sigmoid(gate_in.reshape(b, sp, NSA_HEADS, 3)).transpose(0, 2, 1, 3).reshape(b, G, R, sp, 3)
    o = g[..., 0:1] * o_c + g[..., 1:2] * o_s + g[..., 2:3] * o_w
    return from_heads(o.reshape(b, NSA_HEADS, sp, dh))


def moba_mixer(q_in, k_in, v_in, pos_pad, q_gain, k_gain):
    b, sp, _ = q_in.shape
    H, dh, BS, C = MOBA_HEADS, HEAD_DIM, MOBA_BLOCK, MOBA_QCHUNK
    pos_h = pos_pad[:, None, :]
    t_idx = jnp.arange(sp)
    q = partial_rope(rms_norm(to_heads(q_in, H), q_gain), pos_h)
    k = partial_rope(rms_norm(to_heads(k_in, H), k_gain), pos_h)
    v = to_heads(v_in, H)
    nb = sp // BS
    k_blk = k.reshape(b, H, nb, BS, dh)
    v_blk = v.reshape(b, H, nb, BS, dh)
    k_mean = jnp.mean(k_blk.astype(jnp.float32), axis=3)
    score = jnp.einsum('bhtd,bhnd->bhtn', q.astype(jnp.float32), k_mean)
    own = (t_idx // BS)[:, None]
    score = jnp.where(jnp.arange(nb)[None, :] < own, score, -jnp.inf)
    n_top = min(MOBA_TOPK, nb)
    _, sel_idx = lax.top_k(score, n_top)
    bi = jnp.arange(b)[:, None, None, None]
    hi = jnp.arange(H)[None, :, None, None]
    nq = sp // C
    m_sel = n_top * BS

    def chunk(args):
        qc, ic, c = args
        start = c * C
        blk = start // BS
        k_own = lax.dynamic_slice_in_dim(k, blk * BS, BS, axis=2)
        v_own = lax.dynamic_slice_in_dim(v, blk * BS, BS, axis=2)
        qpos = start + jnp.arange(C)
        m_own = jnp.broadcast_to(blk * BS + jnp.arange(BS)[None, :] <= qpos[:, None], (b, H, C, BS))
        m_sel_mask = jnp.broadcast_to((ic < blk)[..., None], (b, H, C, n_top, BS)).reshape(b, H, C, m_sel)
        kg = k_blk[bi, hi, ic].reshape(b, H, C, m_sel, dh)
        vg = v_blk[bi, hi, ic].reshape(b, H, C, m_sel, dh)
        s = jnp.concatenate([jnp.einsum('bhcd,bhpd->bhcp', qc, k_own),
                             jnp.einsum('bhcd,bhcmd->bhcm', qc, kg)], axis=-1) * ATTN_SCALE
        pr = masked_softmax(s, jnp.concatenate([m_own, m_sel_mask], axis=-1)).astype(v.dtype)
        return (jnp.einsum('bhcp,bhpd->bhcd', pr[..., :BS], v_own)
                + jnp.einsum('bhcm,bhcmd->bhcd', pr[..., BS:], vg))

    q_ch = q.reshape(b, H, nq, C, dh).transpose(2, 0, 1, 3, 4)
    idx_ch = sel_idx.reshape(b, H, nq, C, n_top).transpose(2, 0, 1, 3, 4)
    o = lax.map(chunk, (q_ch, idx_ch, jnp.arange(nq)))
    o = o.transpose(1, 2, 0, 3, 4).reshape(b, H, sp, dh)
    return from_heads(o)


def setup_inputs(seed: int = 0) -> dict:
    key = jax.random.key(seed)
    ks = jax.random.split(key, 25)

    def normal(k, shape, scale):
        return jax.random.normal(k, shape, jnp.float32) * scale

    def gain(k, n):
        return 1.0 + 0.1 * jax.random.normal(k, (DEPTH, n), jnp.float32)

    L, dh = DEPTH, HEAD_DIM
    nsa_w, moba_w = NSA_HEADS * dh, MOBA_HEADS * dh
    return {
        'x': normal(ks[0], (BATCH, SEQ, D_MODEL), 1.0),
        'p': normal(ks[1], (DEPTH, BATCH, SEQ, PLE_DIM), 1.0),
        'positions': jnp.tile(jnp.arange(SEQ, dtype=jnp.int32)[None, :], (BATCH, 1)),
        'g_mix': gain(ks[2], D_MODEL),
        'w_in': normal(ks[3], (L, D_MODEL, IN_COLS), D_MODEL ** -0.5),
        'nsa_q_gain': gain(ks[4], dh),
        'nsa_kc_gain': gain(ks[5], dh),
        'nsa_ks_gain': gain(ks[6], dh),
        'nsa_kw_gain': gain(ks[7], dh),
        'nsa_pe_k': normal(ks[8], (L, CMP_LEN, dh), 0.1),
        'nsa_pe_v': normal(ks[9], (L, CMP_LEN, dh), 0.1),
        'nsa_ck_w1': normal(ks[10], (L, CMP_LEN * dh, CMP_HIDDEN), (CMP_LEN * dh) ** -0.5),
        'nsa_ck_w2': normal(ks[11], (L, CMP_HIDDEN, dh), CMP_HIDDEN ** -0.5),
        'nsa_cv_w1': normal(ks[12], (L, CMP_LEN * dh, CMP_HIDDEN), (CMP_LEN * dh) ** -0.5),
        'nsa_cv_w2': normal(ks[13], (L, CMP_HIDDEN, dh), CMP_HIDDEN ** -0.5),
        'moba_q_gain': gain(ks[14], dh),
        'moba_k_gain': gain(ks[15], dh),
        'w_up_nsa': normal(ks[16], (L, nsa_w, D_MODEL), nsa_w ** -0.5),
        'w_up_moba': normal(ks[17], (L, moba_w, D_MODEL), moba_w ** -0.5),
        'w_out': normal(ks[18], (L, D_MODEL, D_MODEL), D_MODEL ** -0.5),
        'g_ffn': gain(ks[19], D_MODEL),
        'w_ffn_in': normal(ks[20], (L, D_MODEL, 2 * D_FF), D_MODEL ** -0.5),
        'w_ffn_out': normal(ks[21], (L, D_FF, D_MODEL), D_FF ** -0.5),
        'g_ple': gain(ks[22], D_MODEL),
        'w_ple_gate': normal(ks[23], (L, D_MODEL, D_MODEL), D_MODEL ** -0.5),
        'w_ple_proj': normal(ks[24], (L, PLE_DIM, D_MODEL), PLE_DIM ** -0.5),
    }


def reference(x, p, positions, g_mix, w_in, nsa_q_gain, nsa_kc_gain, nsa_ks_gain, nsa_kw_gain,
              nsa_pe_k, nsa_pe_v, nsa_ck_w1, nsa_ck_w2, nsa_cv_w1, nsa_cv_w2,
              moba_q_gain, moba_k_gain, w_up_nsa, w_up_moba, w_out,
              g_ffn, w_ffn_in, w_ffn_out, g_ple, w_ple_gate, w_ple_proj):
    b, s, _ = x.shape
    sp = -(-s // PAD_MULT) * PAD_MULT
    extra = jnp.arange(1, sp - s + 1, dtype=positions.dtype)
    pos_pad = jnp.concatenate([positions, positions[:, -1:] + extra[None, :]], axis=1)
    for i in range(DEPTH):
        proj = rms_norm(x, g_mix[i]) @ w_in[i]
        proj = jnp.pad(proj, ((0, 0), (0, sp - s), (0, 0)))
        (q_n, kc_n, vc_n, ks_n, vs_n, kw_n, vw_n, gate_n,
         q_m, k_m, v_m, gate_a, gate_b) = jnp.split(proj, IN_CUTS, axis=-1)
        y_nsa = nsa_mixer(q_n, kc_n, vc_n, ks_n, vs_n, kw_n, vw_n, gate_n, pos_pad,
                          nsa_q_gain[i], nsa_kc_gain[i], nsa_ks_gain[i], nsa_kw_gain[i],
                          nsa_pe_k[i], nsa_pe_v[i], nsa_ck_w1[i], nsa_ck_w2[i],
                          nsa_cv_w1[i], nsa_cv_w2[i])[:, :s]
        y_moba = moba_mixer(q_m, k_m, v_m, pos_pad, moba_q_gain[i], moba_k_gain[i])[:, :s]
        merged = (jax.nn.sigmoid(gate_a[:, :s]) * (y_nsa @ w_up_nsa[i])
                  + jax.nn.sigmoid(gate_b[:, :s]) * (y_moba @ w_up_moba[i]))
        x = x + merged @ w_out[i]
        gate, up = jnp.split(rms_norm(x, g_ffn[i]) @ w_ffn_in[i], 2, axis=-1)
        x = x + (jax.nn.silu(gate) * up) @ w_ffn_out[i]
        ple_gate = jax.nn.sigmoid(rms_norm(x, g_ple[i]) @ w_ple_gate[i])
        x = x + ple_gate * (p[i] @ w_ple_proj[i])
    return x
```

```python
import math
from contextlib import ExitStack
import numpy as np
import concourse.bass as bass
import concourse.mybir as mybir
from concourse.bass_utils import run_bass_kernel_spmd

F32 = mybir.dt.float32
BF16 = mybir.dt.bfloat16
I32 = mybir.dt.int32
AF = mybir.ActivationFunctionType
ALU = mybir.AluOpType
AX = mybir.AxisListType

NB = 2
S = 2048
D = 1024
NT = S // 128
DFF = 2816
NFC = DFF // 128
EPS = 1e-6
NEG = -30000.0
PI = math.pi


class Sched:
    ENG = ('pe', 'act', 'dve', 'pool', 'sp')

    def __init__(self):
        self.ops = {e: [] for e in self.ENG}
        self.cnt = {e: 0 for e in self.ENG}
        self.seen = {e: {} for e in self.ENG}
        self.res = {}
        self.dma_cnt = {}

    def _deps(self, eng, reads, writes):
        deps = {}

        def add(tok):
            if tok is None:
                return
            k, v = tok
            if deps.get(k, 0) < v:
                deps[k] = v
        for key in reads:
            r = self.res.get(key)
            if r is not None:
                add(r[0])
        for key in writes:
            r = self.res.get(key)
            if r is not None:
                add(r[0])
                for k, v in r[1].items():
                    add((k, v))
        out = []
        for k, v in deps.items():
            if eng == 'pe' and k == 'pe':
                continue
            if self.seen[eng].get(k, 0) >= v:
                continue
            self.seen[eng][k] = v
            out.append((k, v))
        return out

    def _commit(self, tok, reads, writes):
        k, v = tok
        for key in reads:
            r = self.res.setdefault(key, [None, {}])
            if r[1].get(k, 0) < v:
                r[1][k] = v
        for key in writes:
            self.res[key] = [tok, {}]

    def op(self, eng, fn, reads=(), writes=()):
        waits = self._deps(eng, reads, writes)
        self.cnt[eng] += 1
        tok = (eng, self.cnt[eng])
        self.ops[eng].append((waits, fn, (eng, 1)))
        self._commit(tok, reads, writes)
        return tok

    def dma(self, eng, stream, fn, reads=(), writes=()):
        waits = self._deps(eng, reads, writes)
        self.dma_cnt[stream] = self.dma_cnt.get(stream, 0) + 16
        tok = ('dma:' + stream, self.dma_cnt[stream])
        self.ops[eng].append((waits, fn, ('dma:' + stream, 16)))
        self._commit(tok, reads, writes)
        return tok

    def barrier(self):
        toks = [(e, self.cnt[e]) for e in self.ENG if self.cnt[e] > 0]
        toks += [('dma:' + s, v) for s, v in self.dma_cnt.items()]
        for e in self.ENG:
            waits = []
            for k, v in toks:
                if k == e and e == 'pe':
                    continue
                if self.seen[e].get(k, 0) >= v:
                    continue
                self.seen[e][k] = v
                waits.append((k, v))
            if waits:
                self.ops[e].append((waits, None, None))
        self.res = {}

    def sem_keys(self):
        ks = set(self.ENG)
        ks.update('dma:' + s for s in self.dma_cnt)
        return sorted(ks)

    def runner(self, sems):
        def run(eng_name):
            def body(engine):
                for waits, fn, inc in self.ops[eng_name]:
                    for k, v in waits:
                        engine.wait_ge(sems[k], v)
                    if fn is not None:
                        ins = fn(engine)
                        ins.then_inc(sems[inc[0]], inc[1])
            return body
        return run


class Arena:
    def __init__(self, t, nbytes):
        self.t = t
        self.off = 0
        self.nbytes = nbytes
        self.peak = 0

    def alloc(self, parts, shape, dtype):
        n = 1
        for s in shape:
            n *= s
        esz = 2 if dtype == BF16 else 4
        nb = (n * esz + 3) // 4 * 4
        assert self.off + nb <= self.nbytes, ("arena overflow", self.off, nb, self.nbytes)
        ap = self.t[0:parts, self.off // 2:(self.off + n * esz) // 2]
        self.off += nb
        self.peak = max(self.peak, self.off)
        if dtype != BF16:
            ap = ap.bitcast(dtype)
        if len(shape) == 2:
            ap = ap.rearrange("p (a b) -> p a b", a=shape[0])
        elif len(shape) == 3:
            ap = ap.rearrange("p (a b c) -> p a b c", a=shape[0], b=shape[1])
        return ap

    def mark(self):
        return self.off

    def release(self, m):
        self.off = m


def bc(ap, shape):
    return ap.broadcast_to(list(shape))


class _Stop(Exception):
    pass


def build_program(stop=None, nb=NB, qts=None):
    nc = bass.Bass("TRN2", target_bir_lowering=False)
    dbg_outs = {}

    def din(name, shape, dt=F32):
        return nc.dram_tensor(name, list(shape), dt, kind="ExternalInput").ap()

    x_d = din("x", [NB, S, D])
    p_d = din("p", [NB, S, 256])
    pos_d = din("pos", [NB, NT, 128], I32)
    gmix_d = din("g_mix", [8, 128])
    gffn_d = din("g_ffn", [8, 128])
    gple_d = din("g_ple", [8, 128])
    win_d = din("w_in", [D, 4888])
    gains_d = {n: din(n, [1, 64]) for n in ("nsa_q_gain", "nsa_kc_gain", "nsa_ks_gain", "nsa_kw_gain",
                                            "moba_q_gain", "moba_k_gain")}
    pek_d = din("nsa_pe_k", [32, 64])
    pev_d = din("nsa_pe_v", [32, 64])
    ckw1_d = din("nsa_ck_w1", [2048, 256])
    ckw2_d = din("nsa_ck_w2", [256, 64])
    cvw1_d = din("nsa_cv_w1", [2048, 256])
    cvw2_d = din("nsa_cv_w2", [256, 64])
    wupn_d = din("w_up_nsa", [512, D])
    wupm_d = din("w_up_moba", [512, D])
    wout_d = din("w_out", [D, D])
    wfi_d = din("w_ffn_in", [D, 2 * DFF])
    wfo_d = din("w_ffn_out", [DFF, D])
    wpg_d = din("w_ple_gate", [D, D])
    wpp_d = din("w_ple_proj", [256, D])
    ident_d = din("c_ident", [128, 128])
    tri_d = din("c_tri", [128, 128])
    anti_d = din("c_anti", [128, 128])
    cmask_d = din("c_cmask", [127, S])
    fbs_d = din("c_fbs", [128, NT * 32])
    fbm_d = din("c_fbm", [128, NT * 8])
    ovl_d = din("c_ovl", [127, 33])
    invf_d = din("c_invf", [1, 8])
    emat_d = din("c_emat", [32, S])
    out_d = nc.dram_tensor("out", [NB, S, D], F32, kind="ExternalOutput").ap()

    Sx = Sched()
    ARENA_BYTES = 206 * 1024
    with ExitStack() as es:
        arena_t = es.enter_context(nc.sbuf_tensor("arena", [128, ARENA_BYTES // 2], BF16))
        psum_t = es.enter_context(nc.psum_tensor("psum", [128, 8, 512], F32))
        AR = Arena(arena_t, ARENA_BYTES)

        def bank(i):
            return psum_t[:, i, :]

        def bank_bf(i):
            return psum_t[:, i, :].bitcast(BF16)

        def dma_sp(stream, out, in_, reads=(), writes=(), slow=False):
            if slow:
                return Sx.dma('sp', stream, lambda e: e.dma_start(out=out, in_=in_, allow_slow_non_contiguous=True), reads, writes)
            return Sx.dma('sp', stream, lambda e: e.dma_start(out=out, in_=in_), reads, writes)

        def dma_cast(stream, out, in_, reads=(), writes=()):
            return Sx.dma('pool', stream, lambda e: e.dma_start(out=out, in_=in_), reads, writes)

        def mm(out, lhsT, rhs, start, stop, reads, writes, skip=False):
            return Sx.op('pe', lambda e: e.matmul(out, lhsT=lhsT, rhs=rhs, start=start, stop=stop,
                                                  skip_group_check=skip), reads, writes)

        def tr(out, in_, ident, reads, writes):
            return Sx.op('pe', lambda e: e.transpose(out=out, in_=in_, identity=ident), reads, writes)

        def act(out, in_, func, reads, writes, scale=None, bias=None):
            kw = {}
            if scale is not None:
                kw['scale'] = scale
            if bias is not None:
                kw['bias'] = bias
            return Sx.op('act', lambda e: e.activation(out=out, in_=in_, func=func, **kw), reads, writes)

        def tt(eng, out, in0, in1, op, reads, writes):
            return Sx.op(eng, lambda e: e.tensor_tensor(out=out, in0=in0, in1=in1, op=op), reads, writes)

        def ts(eng, out, in0, s1, s2, op0, op1, reads, writes):
            if op1 is None:
                return Sx.op(eng, lambda e: e.tensor_scalar(out=out, in0=in0, scalar1=s1, scalar2=None, op0=op0),
                             reads, writes)
            return Sx.op(eng, lambda e: e.tensor_scalar(out=out, in0=in0, scalar1=s1, scalar2=s2, op0=op0, op1=op1),
                         reads, writes)

        def stt(out, in0, scalar, in1, op0, op1, reads, writes):
            return Sx.op('dve', lambda e: e.scalar_tensor_tensor(out=out, in0=in0, scalar=scalar, in1=in1,
                                                                 op0=op0, op1=op1), reads, writes)

        def cp(eng, out, in_, reads, writes):
            if eng == 'act':
                return Sx.op(eng, lambda e: e.copy(out=out, in_=in_), reads, writes)
            return Sx.op(eng, lambda e: e.tensor_copy(out=out, in_=in_), reads, writes)

        def recip(out, in_, reads, writes):
            return Sx.op('dve', lambda e: e.reciprocal(out=out, in_=in_), reads, writes)

        def memset(eng, ap, val, writes):
            return Sx.op(eng, lambda e: e.memset(ap, val), (), writes)

        def checkpoint(name, dumps):
            if stop != name:
                return
            Sx.barrier()
            for label, ap in dumps:
                shp = list(ap.shape)
                dt_ = ap.dtype
                d = nc.dram_tensor("dbg_" + label, shp, dt_, kind="ExternalOutput").ap()
                dbg_outs[label] = d
                Sx.dma('sp', 'dbg_' + label, lambda e, d=d, ap=ap: e.dma_start(out=d, in_=ap), (), ())
            raise _Stop()

        identf = AR.alloc(128, [128], F32)
        identb = AR.alloc(128, [128], BF16)
        trib = AR.alloc(128, [128], BF16)
        antib = AR.alloc(128, [128], BF16)
        cmask = AR.alloc(127, [NT, 128], BF16)
        fbs = AR.alloc(128, [NT, 32], F32)
        fbm = AR.alloc(128, [NT, 8], F32)
        invf = AR.alloc(128, [8], F32)
        gtile = {n: AR.alloc(128, [64], F32) for n in gains_d}
        gq8 = AR.alloc(128, [64], F32)
        gm8 = AR.alloc(128, [64], F32)
        gvec = {n: AR.alloc(128, [8], F32) for n in ("mix", "ffn", "ple")}
        gstage = AR.alloc(8, [128], F32)
        pestage = AR.alloc(32, [2, 64], F32)
        pestb = AR.alloc(32, [2, 64], BF16)
        peT = AR.alloc(64, [2, 32], BF16)
        cos_t = AR.alloc(128, [NT, 8], F32)
        sin_t = AR.alloc(128, [NT, 8], F32)
        cosc = AR.alloc(127, [8], F32)
        sinc = AR.alloc(127, [8], F32)
        n_ss = AR.alloc(128, [8], F32)
        n_rs = AR.alloc(128, [8], F32)
        x_ss = AR.alloc(128, [1], F32)
        x_rs = AR.alloc(128, [1], F32)

        dma_sp('c0', identf, ident_d, writes=['identf'])
        cp('dve', identb, identf, ['identf'], ['identb'])
        dma_cast('c1', trib, tri_d, writes=['trib'])
        dma_cast('c2', antib, anti_d, writes=['antib'])
        dma_cast('c3', cmask, cmask_d.rearrange("p (a b) -> p a b", a=NT), writes=['cmask'])
        dma_sp('c4', fbs, fbs_d.rearrange("p (a b) -> p a b", a=NT), writes=['fbs'])
        dma_sp('c5', fbm, fbm_d.rearrange("p (a b) -> p a b", a=NT), writes=['fbm'])
        dma_sp('c6', invf, invf_d.partition_broadcast(128), writes=['invf'])
        for i, n in enumerate(gains_d):
            dma_sp('c7_%d' % i, gtile[n], gains_d[n].partition_broadcast(128), writes=['g_' + n])
        ts('dve', gq8, gtile["nsa_q_gain"], 0.125, None, ALU.mult, None, ['g_nsa_q_gain'], ['gq8'])
        ts('dve', gm8, gtile["moba_q_gain"], 0.125, None, ALU.mult, None, ['g_moba_q_gain'], ['gm8'])
        for n, dd in (("mix", gmix_d), ("ffn", gffn_d), ("ple", gple_d)):
            dma_sp('c8', gstage, dd, writes=['gstage'])
            tr(bank(7)[:, 0:8], gstage, identf[0:8, 0:8], ['gstage', 'identf'], ['b7'])
            cp('dve', gvec[n], bank(7)[:, 0:8], ['b7'], ['gvec' + n])
        dma_sp('c9', pestage[:, 0, :], pek_d, writes=['pestage'])
        dma_sp('c9', pestage[:, 1, :], pev_d, writes=['pestage'])
        cp('dve', pestb, pestage, ['pestage'], ['pestb'])
        for kv in range(2):
            tr(bank_bf(7)[0:64, kv * 32:(kv + 1) * 32], pestb[:, kv, :], identb[0:32, 0:32], ['pestb', 'identb'], ['b7'])
        cp('dve', peT, bank_bf(7)[0:64, 0:64].rearrange("p (a b) -> p a b", a=2), ['b7'], ['peT'])

        PERSIST = AR.mark()

        def norm_transpose(src, gain, dst, rk, wk, xn, bk):
            sq = xn_sq
            tt('dve', sq, src, src, ALU.mult, [rk], ['xn_sq'])
            Sx.op('dve', lambda e: e.tensor_reduce(out=x_ss, in_=sq, axis=AX.X, op=ALU.add), ['xn_sq'], ['x_ss'])
            act(x_rs, x_ss, AF.Ln, ['x_ss'], ['x_rs'], scale=1.0 / D, bias=EPS)
            act(x_rs, x_rs, AF.Exp, ['x_rs'], ['x_rs'], scale=-0.5)
            act(xn, src, AF.Copy, [rk, 'x_rs'], ['xn'], scale=x_rs[:, 0:1])
            pb = bank_bf(bk).rearrange("p (a b) -> p a b", a=8)
            for c in range(8):
                tr(pb[:, c, :], xn[:, c * 128:(c + 1) * 128], identb, ['xn', 'identb'], ['b%d' % bk])
            tt('dve', dst, pb, bc(gain.unsqueeze(2), [128, 8, 128]), ALU.mult, ['b%d' % bk], [wk])

        def head_norm_rope(src, H, rows, gain3, cosb, sinb, out_bf, rk, wk, extra_reads=()):
            R = rows
            qraw = hn_raw[0:R, 0:H * 64]
            sq = hn_sq[0:R, 0:H * 64]
            q3 = qraw.rearrange("p (h d) -> p h d", h=H)
            act(qraw, src, AF.Copy, [rk], ['hn_raw'])
            tt('dve', sq, qraw, qraw, ALU.mult, ['hn_raw'], ['hn_sq'])
            Sx.op('dve', lambda e: e.tensor_reduce(out=n_ss[0:R, 0:H], in_=sq.rearrange("p (h d) -> p h d", h=H),
                                                   axis=AX.X, op=ALU.add), ['hn_sq'], ['n_ss'])
            act(n_rs[0:R, 0:H], n_ss[0:R, 0:H], AF.Ln, ['n_ss'], ['n_rs'], scale=1.0 / 64, bias=EPS)
            act(n_rs[0:R, 0:H], n_rs[0:R, 0:H], AF.Exp, ['n_rs'], ['n_rs'], scale=-0.5)
            tt('dve', q3, q3, bc(n_rs[0:R, 0:H].unsqueeze(2), [R, H, 64]), ALU.mult, ['hn_raw', 'n_rs'], ['hn_raw'])
            tt('dve', q3, q3, gain3, ALU.mult, ['hn_raw'] + list(extra_reads), ['hn_raw'])
            cp('pool', out_bf[:, :, 16:64], q3[:, :, 16:64], ['hn_raw'], [wk])
            x1 = q3[:, :, 0:8]
            x2 = q3[:, :, 8:16]
            c3 = bc(cosb.unsqueeze(1), [R, H, 8])
            s3 = bc(sinb.unsqueeze(1), [R, H, 8])
            ra = hn_r[0:R, 0, 0:H, :]
            rb = hn_r[0:R, 1, 0:H, :]
            rc = hn_r[0:R, 2, 0:H, :]
            rd = hn_r[0:R, 3, 0:H, :]
            tt('dve', ra, x1, c3, ALU.mult, ['hn_raw', 'rope'], ['hn_ra'])
            tt('dve', rb, x2, s3, ALU.mult, ['hn_raw', 'rope'], ['hn_rb'])
            tt('dve', rc, x2, c3, ALU.mult, ['hn_raw', 'rope'], ['hn_rc'])
            tt('dve', rd, x1, s3, ALU.mult, ['hn_raw', 'rope'], ['hn_rd'])
            tt('dve', out_bf[:, :, 0:8], ra, rb, ALU.subtract, ['hn_ra', 'hn_rb'], [wk])
            tt('dve', out_bf[:, :, 8:16], rc, rd, ALU.add, ['hn_rc', 'hn_rd'], [wk])

        pjc = [0]

        def next_pj():
            pjc[0] ^= 1
            return pjc[0]

        try:
          main_body = True
          for b in range(nb):
            AR.release(PERSIST)
            Sx.barrier()
            hT = AR.alloc(128, [8, S], BF16)
            yTn = AR.alloc(128, [4, S], BF16)
            yTm = AR.alloc(128, [4, S], BF16)
            ATT = AR.mark()
            ksA = AR.alloc(96, [2, S], BF16)
            kwT = AR.alloc(64, [2, S], BF16)
            kmT = AR.alloc(128, [4, S], BF16)
            VS = AR.alloc(128, [NT, 2, 65], BF16)
            VW = AR.alloc(128, [NT, 2, 65], BF16)
            VM = AR.alloc(128, [NT, 8, 65], BF16)
            kcT = AR.alloc(64, [2, 127], BF16)
            VC = AR.alloc(127, [2, 97], BF16)
            kmean = AR.alloc(128, [4, 8], BF16)
            kmean_f = AR.alloc(128, [4, 8], F32)
            kmean_z = AR.alloc(128, [8, 8], BF16)
            hn_raw = AR.alloc(128, [512], F32)
            hn_sq = AR.alloc(128, [512], F32)
            hn_r = AR.alloc(128, [4, 8, 8], F32)
            xn_sq = AR.alloc(128, [1024], F32)
            xn_b = AR.alloc(128, [1024], BF16)
            BC = AR.mark()
            xs = [AR.alloc(128, [1024], F32) for _ in range(2)]
            wb = [AR.alloc(128, [8, 512], BF16)]
            ks_b = AR.alloc(128, [2, 64], BF16)
            kvT = AR.alloc(64, [4, S], BF16)
            w1sb = AR.alloc(64, [32, 256], BF16)
            w2sb = AR.alloc(128, [2, 2, 64], BF16)
            hsb = AR.alloc(128, [2, 127], BF16)
            hb = AR.alloc(128, [2], F32)
            tm_b = AR.alloc(128, [512], BF16)
            posi = AR.alloc(16, [128], I32)
            posf16 = AR.alloc(16, [128], F32)
            posf = AR.alloc(128, [NT], F32)
            ang = AR.alloc(128, [NT, 8], F32)
            ang2 = AR.alloc(128, [NT, 8], F32)
            angi = AR.alloc(128, [NT, 8], I32)
            angn = AR.alloc(128, [NT, 8], F32)
            posci = AR.alloc(127, [1], I32)
            poscf = AR.alloc(127, [1], F32)

            def sincos(angv, outv, R, shape, shift):
                a2 = ang2[0:R] if len(shape) == 2 else ang2[0:R, 0, :]
                ai = angi[0:R] if len(shape) == 2 else angi[0:R, 0, :]
                an = angn[0:R] if len(shape) == 2 else angn[0:R, 0, :]
                ts('dve', a2, angv, shift, 1.0 / (2 * PI), ALU.add, ALU.mult, ['ang'], ['ang2'])
                cp('dve', ai, a2, ['ang2'], ['angi'])
                cp('dve', an, ai, ['angi'], ['angn'])
                ts('dve', an, an, -2 * PI, None, ALU.mult, None, ['angn'], ['angn'])
                tt('dve', a2, an, angv, ALU.add, ['angn', 'ang'], ['ang2'])
                ts('dve', a2, a2, shift, None, ALU.add, None, ['ang2'], ['ang2'])
                ts('dve', an, a2, PI, -2 * PI, ALU.is_gt, ALU.mult, ['ang2'], ['angn'])
                tt('dve', a2, a2, an, ALU.add, ['ang2', 'angn'], ['ang2'])
                ts('dve', an, a2, -PI, 2 * PI, ALU.is_lt, ALU.mult, ['ang2'], ['angn'])
                tt('dve', a2, a2, an, ALU.add, ['ang2', 'angn'], ['ang2'])
                ts('dve', a2, a2, PI, -PI, ALU.min, ALU.max, ['ang2'], ['ang2'])
                act(outv, a2, AF.Sin, ['ang2'], ['rope'])

            dma_sp('pos', posi, pos_d[b], writes=['posi'])
            cp('dve', posf16, posi, ['posi'], ['posf16'])
            tr(bank(7)[:, 0:16], posf16, identf[0:16, 0:16], ['posf16', 'identf'], ['b7'])
            cp('dve', posf, bank(7)[:, 0:16], ['b7'], ['posf'])
            tt('dve', ang, bc(posf.unsqueeze(2), [128, NT, 8]), bc(invf.unsqueeze(1), [128, NT, 8]), ALU.mult,
               ['posf', 'invf'], ['ang'])
            sincos(ang, sin_t, 128, [NT, 8], 0.0)
            sincos(ang, cos_t, 128, [NT, 8], PI / 2)
            posflat = pos_d[b].rearrange("a b -> (a b)")
            dma_sp('posc', posci, posflat[16:S].rearrange("(c s) -> c s", s=16)[:, 15:16], writes=['posci'], slow=True)
            cp('dve', poscf, posci, ['posci'], ['poscf'])
            angc = ang[0:127, 0, :]
            ts('dve', angc, invf[0:127, :], poscf[:, 0:1], None, ALU.mult, None, ['poscf', 'invf', 'rope'], ['ang'])
            sincos(angc, sinc, 127, [8], 0.0)
            sincos(angc, cosc, 127, [8], PI / 2)

            for t in range(NT):
                xb = xs[t % 2]
                dma_sp('xs%d' % (t % 2), xb, x_d[b, t * 128:(t + 1) * 128, :], writes=['xs%d' % (t % 2)])
                norm_transpose(xb, gvec["mix"], hT[:, :, t * 128:(t + 1) * 128], 'xs%d' % (t % 2), 'hT%d' % t, xn_b, 2)

            checkpoint('A', [('hT', hT), ('cos', cos_t), ('sin', sin_t), ('cosc', cosc), ('sinc', sinc)])
            memset('pool', VS[:, :, :, 64:65], 1.0, ['VS'])
            memset('pool', VW[:, :, :, 64:65], 1.0, ['VW'])
            memset('pool', VM[:, :, :, 64:65], 1.0, ['VM'])
            for g in range(2):
                dma_cast('emat', ksA[64:96, g, :], emat_d, writes=['ksA_e'])
                dma_cast('ovl', VC[:, g, 64:97], ovl_d, writes=['VC_o'])

            gks3 = bc(gtile["nsa_ks_gain"].unsqueeze(1), [128, 2, 64])
            gkw3 = bc(gtile["nsa_kw_gain"].unsqueeze(1), [128, 2, 64])
            gkm3 = bc(gtile["moba_k_gain"].unsqueeze(1), [128, 8, 64])
            chunks = [(512, 512, 'kcvcksvs'), (1024, 256, 'kwvw'), (1816, 512, 'km'), (2328, 512, 'vm')]
            hT_all = ['hT%d' % t for t in range(NT)]
            for ci, (c0, cw, kind) in enumerate(chunks):
                wbuf = wb[0]
                wk = 'wb0'
                dma_cast(wk, wbuf[:, :, 0:cw], win_d[:, c0:c0 + cw].rearrange("(c p) n -> p c n", p=128), writes=[wk])
                for t in range(NT):
                    pj = next_pj()
                    pk = 'b%d' % pj
                    tsl = slice(t * 128, (t + 1) * 128)
                    for kc in range(8):
                        mm(bank(pj)[:, 0:cw], hT[:, kc, tsl], wbuf[:, kc, 0:cw], kc == 0, kc == 7, ['hT%d' % t, wk], [pk])
                    ptb = bank_bf(2)
                    if kind == 'kcvcksvs':
                        cp('act', tm_b[:, 0:256], bank(pj)[:, 0:256], [pk], ['tm_b'])
                        for i in range(4):
                            tr(ptb[0:64, i * 128:(i + 1) * 128], tm_b[:, i * 64:(i + 1) * 64], identb, ['tm_b', 'identb'], ['b2'])
                        cp('act', kvT[:, :, tsl], ptb[0:64, 0:512].rearrange("p (a b) -> p a b", a=4), ['b2'], ['kvT'])
                        cp('act', VS[:, t, :, 0:64], bank(pj)[:, 384:512].rearrange("p (g d) -> p g d", g=2), [pk], ['VS'])
                        head_norm_rope(bank(pj)[:, 256:384], 2, 128, gks3, cos_t[:, t, :], sin_t[:, t, :], ks_b, pk, 'ks_b',
                                       ['g_nsa_ks_gain'])
                        for g in range(2):
                            tr(ptb[0:64, g * 128:(g + 1) * 128], ks_b[:, g, :], identb,
                               ['ks_b', 'identb'], ['b2'])
                        cp('act', ksA[0:64, :, tsl], ptb[0:64, 0:256].rearrange("p (a b) -> p a b", a=2), ['b2'], ['ksA'])
                    elif kind == 'kwvw':
                        cp('act', VW[:, t, :, 0:64], bank(pj)[:, 128:256].rearrange("p (g d) -> p g d", g=2), [pk], ['VW'])
                        kwb = tm_b[:, 0:128].rearrange("p (h d) -> p h d", h=2)
                        head_norm_rope(bank(pj)[:, 0:128], 2, 128, gkw3, cos_t[:, t, :], sin_t[:, t, :], kwb, pk, 'tm_b',
                                       ['g_nsa_kw_gain'])
                        for g in range(2):
                            tr(ptb[0:64, g * 128:(g + 1) * 128], tm_b[:, g * 64:(g + 1) * 64], identb, ['tm_b', 'identb'], ['b2'])
                        cp('act', kwT[0:64, :, tsl], ptb[0:64, 0:256].rearrange("p (a b) -> p a b", a=2), ['b2'], ['kwT'])
                    elif kind == 'km':
                        kmb = tm_b.rearrange("p (h d) -> p h d", h=8)
                        head_norm_rope(bank(pj), 8, 128, gkm3, cos_t[:, t, :], sin_t[:, t, :], kmb, pk, 'tm_b',
                                       ['g_moba_k_gain'])
                        for hp in range(4):
                            tr(ptb[:, hp * 128:(hp + 1) * 128], tm_b[:, hp * 128:(hp + 1) * 128], identb, ['tm_b', 'identb'], ['b2'])
                        cp('act', kmT[:, :, tsl], ptb[:, 0:512].rearrange("p (a b) -> p a b", a=4), ['b2'], ['kmT'])
                    else:
                        cp('act', VM[:, t, :, 0:64], bank(pj).rearrange("p (h d) -> p h d", h=8), [pk], ['VM'])
            Sx.op('dve', lambda e: e.tensor_reduce(out=kmean_f, in_=kmT.rearrange("p h (n k) -> p h n k", k=256),
                                                   axis=AX.X, op=ALU.add), ['kmT'], ['kmean_f'])
            ts('dve', kmean, kmean_f, 1.0 / 256, None, ALU.mult, None, ['kmean_f'], ['kmean'])
            memset('dve', kmean_z, 0.0, ['kmean_z'])
            kz4 = kmean_z.rearrange("p (hp two) n -> p hp two n", two=2)
            cp('dve', kz4[0:64, :, 0, :], kmean[0:64, :, :], ['kmean', 'kmean_z'], ['kmean_z'])
            cp('dve', kz4[64:128, :, 1, :], kmean[64:128, :, :], ['kmean', 'kmean_z'], ['kmean_z'])

            checkpoint('B', [('ksA', ksA), ('kwT', kwT), ('kmT', kmT), ('VS', VS), ('VW', VW), ('VM', VM), ('kvT', kvT), ('kmean', kmean_f)])
            for kv, (w1_d, w2_d) in enumerate(((ckw1_d, ckw2_d), (cvw1_d, cvw2_d))):
                dma_cast('w1', w1sb, w1_d.rearrange("(i d) h -> d i h", d=64), writes=['w1sb'])
                dma_cast('w2', w2sb[:, kv, :, :], w2_d.rearrange("(hh p) d -> p hh d", p=128), writes=['w2sb'])
                for hh in range(2):
                    for i in range(32):
                        mm(bank(7)[:, hh:hh + 1], w1sb[:, i, hh * 128:(hh + 1) * 128], peT[:, kv, i:i + 1], i == 0, i == 31,
                           ['w1sb', 'peT'], ['b7'])
                cp('dve', hb, bank(7)[:, 0:2], ['b7'], ['hb'])
                for g in range(2):
                    src = kvT[:, kv * 2 + g, :].rearrange("p (c s) -> p c s", s=16)
                    for hh in range(2):
                        pj = next_pj()
                        pk = 'b%d' % pj
                        for i in range(32):
                            rhs = src[:, 0:127, i] if i < 16 else src[:, 1:128, i - 16]
                            mm(bank(pj)[:, 0:127], w1sb[:, i, hh * 128:(hh + 1) * 128], rhs, i == 0, i == 31, ['w1sb', 'kvT'], [pk])
                        act(hsb[:, hh, :], bank(pj)[:, 0:127], AF.Silu, [pk, 'hb'], ['hsb'], bias=hb[:, hh:hh + 1])
                    pj = next_pj()
                    pk = 'b%d' % pj
                    for hh in range(2):
                        mm(bank(pj)[0:127, 0:64], hsb[:, hh, :], w2sb[:, kv, hh, :], hh == 0, hh == 1, ['hsb', 'w2sb'], [pk])
                    if kv == 0:
                        kcb = tm_b[0:127, 0:64].rearrange("p (h d) -> p h d", h=1)
                        head_norm_rope(bank(pj)[0:127, 0:64], 1, 127, bc(gtile["nsa_kc_gain"][0:127].unsqueeze(1), [127, 1, 64]),
                                       cosc, sinc, kcb, pk, 'tm_b', ['g_nsa_kc_gain'])
                        tr(bank_bf(2)[0:64, 0:127], tm_b[0:127, 0:64], identb[0:127, 0:127], ['tm_b', 'identb'], ['b2'])
                        cp('act', kcT[:, g, :], bank_bf(2)[0:64, 0:127], ['b2'], ['kcT'])
                    else:
                        cp('act', VC[:, g, 0:64], bank(pj)[0:127, 0:64], [pk], ['VC'])

            checkpoint('C', [('kcT', kcT), ('VC', VC)])
            AR.release(BC)
            Sx.barrier()
            wq = AR.alloc(128, [8, 1048], BF16)
            QA = AR.alloc(96, [2, 512], BF16)
            QM = AR.alloc(128, [4, 128], BF16)
            qn_b = AR.alloc(128, [8, 64], BF16)
            qm_b = AR.alloc(128, [8, 64], BF16)
            sig = AR.alloc(128, [8, 3], F32)
            Ec = AR.alloc(127, [512], BF16)
            Em = AR.alloc(127, [512], BF16)
            PT = [AR.alloc(128, [512], BF16) for _ in range(3)]
            rs4 = AR.alloc(128, [4], F32)
            coef = AR.alloc(128, [4], F32)
            impn = AR.alloc(128, [32], F32)
            top8 = AR.alloc(128, [8], F32)
            selb = AR.alloc(128, [32], F32)
            Bpad = AR.alloc(128, [96], BF16)
            yacc = AR.alloc(128, [8, 64], F32)
            ytmp = AR.alloc(128, [4, 64], F32)
            oacc = AR.alloc(128, [8, 65], F32)
            scm = AR.alloc(128, [8, 8], F32)
            topm = AR.alloc(128, [8, 8], F32)
            selm = AR.alloc(128, [8, 8], F32)
            rsm = AR.alloc(128, [8], F32)
            y_b = AR.alloc(128, [512], BF16)

            dma_cast('wq', wq[:, :, 0:512], win_d[:, 0:512].rearrange("(c p) n -> p c n", p=128), writes=['wq'])
            dma_cast('wq', wq[:, :, 512:536], win_d[:, 1280:1304].rearrange("(c p) n -> p c n", p=128), writes=['wq'])
            dma_cast('wq', wq[:, :, 536:1048], win_d[:, 1304:1816].rearrange("(c p) n -> p c n", p=128), writes=['wq'])
            memset('dve', Bpad, 0.0, ['Bpad'])
            gq3 = bc(gq8.unsqueeze(1), [128, 8, 64])
            gm3 = bc(gm8.unsqueeze(1), [128, 8, 64])
            ptc = [0]

            def next_pt():
                ptc[0] = (ptc[0] + 1) % 3
                return PT[ptc[0]], 'PT%d' % ptc[0]

            psc = [0]

            def next_ps():
                psc[0] ^= 1
                return 3 + psc[0]

            for qt in (range(NT) if qts is None else qts):
                tsl = slice(qt * 128, (qt + 1) * 128)
                hk = 'hT%d' % qt
                for kc in range(8):
                    mm(bank(0), hT[:, kc, tsl], wq[:, kc, 0:512], kc == 0, kc == 7, [hk, 'wq'], ['b0'])
                for kc in range(8):
                    mm(bank(1), hT[:, kc, tsl], wq[:, kc, 536:1048], kc == 0, kc == 7, [hk, 'wq'], ['b1'])
                for kc in range(8):
                    mm(bank(7)[:, 0:24], hT[:, kc, tsl], wq[:, kc, 512:536], kc == 0, kc == 7, [hk, 'wq'], ['b7'])
                sig2 = sig.rearrange("p h c -> p (h c)")
                act(sig2, bank(7)[:, 0:24], AF.Exp, ['b7'], ['sig'], scale=-1.0)
                ts('dve', sig2, sig2, 1.0, None, ALU.add, None, ['sig'], ['sig'])
                recip(sig2, sig2, ['sig'], ['sig'])
                head_norm_rope(bank(0), 8, 128, gq3, cos_t[:, qt, :], sin_t[:, qt, :], qn_b, 'b0', 'qn_b', ['gq8'])
                ptb = bank_bf(2)
                for h in range(8):
                    tr(ptb[0:64, h * 128:(h + 1) * 128], qn_b[:, h, :], identb, ['qn_b', 'identb'], ['b2'])
                cp('act', QA[0:64, :, :], ptb[0:64, :].rearrange("p (g n) -> p g n", g=2), ['b2'], ['QA'])
                head_norm_rope(bank(1), 8, 128, gm3, cos_t[:, qt, :], sin_t[:, qt, :], qm_b, 'b1', 'qm_b', ['gm8'])
                qm2 = qm_b.rearrange("p h d -> p (h d)")
                for hp in range(4):
                    tr(ptb[:, hp * 128:(hp + 1) * 128], qm2[:, hp * 128:(hp + 1) * 128], identb, ['qm_b', 'identb'], ['b2'])
                cp('act', QM, ptb[:, 0:512].rearrange("p (a b) -> p a b", a=4), ['b2'], ['QM'])

                ncv = min(127, 8 * qt + 7)
                for g in range(2):
                    ps_i = next_ps()
                    pk = 'b%d' % ps_i
                    mm(bank(ps_i)[0:127, :], kcT[:, g, :], QA[0:64, g, :], True, True, ['kcT', 'QA'], [pk])
                    act(Ec, bank(ps_i)[0:127, :], AF.Exp, [pk], ['Ec'])
                    tt('pool', Em.rearrange("p (r t) -> p r t", r=4), Ec.rearrange("p (r t) -> p r t", r=4),
                       bc(cmask[:, qt, :].unsqueeze(1), [127, 4, 128]), ALU.mult, ['Ec', 'cmask'], ['Em'])
                    po = bank(5)[:, 0:388].rearrange("p (r c) -> p r c", r=4)
                    for r in range(4):
                        mm(po[:, r, :], Em[:, r * 128:(r + 1) * 128], VC[:, g, :], True, True, ['Em', 'VC', 'VC_o'], ['b5'])
                    ts('dve', rs4, po[:, :, 96], 1e-30, None, ALU.max, None, ['b5'], ['rs4'])
                    recip(rs4, rs4, ['rs4'], ['rs4'])
                    ts('dve', impn, po[:, 0, 64:96], rs4[:, 0:1], None, ALU.mult, None, ['b5', 'rs4'], ['impn'])
                    for r in range(1, 4):
                        stt(impn, po[:, r, 64:96], rs4[:, r:r + 1], impn, ALU.mult, ALU.add, ['b5', 'rs4', 'impn'], ['impn'])
                    tt('dve', coef, rs4, sig[:, 4 * g:4 * g + 4, 0], ALU.mult, ['rs4', 'sig'], ['coef'])
                    tt('dve', yacc[:, 4 * g:4 * g + 4, :], po[:, :, 0:64], bc(coef.unsqueeze(2), [128, 4, 64]), ALU.mult,
                       ['b5', 'coef'], ['yacc%d' % g])
                    tt('dve', impn, impn, fbs[:, qt, :], ALU.add, ['impn', 'fbs'], ['impn'])
                    Sx.op('dve', lambda e: e.max(out=top8, in_=impn), ['impn'], ['top8'])
                    ts('dve', selb, impn, top8[:, 7:8], -NEG, ALU.is_ge, ALU.mult, ['impn', 'top8'], ['selb'])
                    ts('dve', Bpad[:, 64:96], selb, NEG, None, ALU.add, None, ['selb'], ['Bpad'])
                    tr(bank_bf(2)[0:96, 0:128], Bpad, identb, ['Bpad', 'identb'], ['b2'])
                    cp('act', QA[64:96, g, :].rearrange("p (r t) -> p r t", r=4),
                       bc(bank_bf(2)[64:96, 0:128].unsqueeze(1), [32, 4, 128]), ['b2'], ['QA'])

                def nsa_branch(g, kts, K, kT, kkey, Vt, vkey, gate_idx, first):
                    po2 = bank(6 if first else 5)[:, 0:260].rearrange("p (r c) -> p r c", r=4)
                    pok = 'b6' if first else 'b5'
                    for j, (kt, mask) in enumerate(kts):
                        ps_i = next_ps()
                        pk = 'b%d' % ps_i
                        mm(bank(ps_i), kT[0:K, g, kt * 128:(kt + 1) * 128], QA[0:K, g, :], True, True, [kkey, 'QA', 'ksA_e'], [pk])
                        pt, ptk = next_pt()
                        act(pt, bank(ps_i), AF.Exp, [pk], [ptk])
                        if mask is not None:
                            p3 = pt.rearrange("p (r t) -> p r t", r=4)
                            tt('pool', p3, p3, bc(mask.unsqueeze(1), [128, 4, 128]), ALU.mult, [ptk, 'trib', 'antib'], [ptk])
                        for r in range(4):
                            mm(po2[:, r, :], pt[:, r * 128:(r + 1) * 128], Vt[:, kt, g, :], (j == 0 and r == 0), j == len(kts) - 1,
                               [ptk, vkey], [pok], skip=True)
                    ts('dve', rs4, po2[:, :, 64], 1e-30, None, ALU.max, None, [pok], ['rs4'])
                    recip(rs4, rs4, ['rs4'], ['rs4'])
                    tt('dve', coef, rs4, sig[:, 4 * g:4 * g + 4, gate_idx], ALU.mult, ['rs4', 'sig'], ['coef'])
                    tt('dve', ytmp, po2[:, :, 0:64], bc(coef.unsqueeze(2), [128, 4, 64]), ALU.mult, [pok, 'coef'], ['ytmp'])
                    tt('dve', yacc[:, 4 * g:4 * g + 4, :], yacc[:, 4 * g:4 * g + 4, :], ytmp, ALU.add,
                       ['ytmp', 'yacc%d' % g], ['yacc%d' % g])

                for g in range(2):
                    kts = [(kt, trib if kt == qt else None) for kt in range(qt + 1)]
                    nsa_branch(g, kts, 96, ksA, 'ksA', VS, 'VS', 1, True)
                    kts = []
                    for kt in range(max(0, qt - 4), qt + 1):
                        m = trib if kt == qt else (antib if kt == qt - 4 else None)
                        kts.append((kt, m))
                    nsa_branch(g, kts, 64, kwT, 'kwT', VW, 'VW', 2, False)
                cp('act', y_b, yacc.rearrange("p h d -> p (h d)"), ['yacc0', 'yacc1'], ['y_b'])
                for c in range(4):
                    tr(ptb[:, c * 128:(c + 1) * 128], y_b[:, c * 128:(c + 1) * 128], identb, ['y_b', 'identb'], ['b2'])
                cp('act', yTn[:, :, tsl], ptb[:, 0:512].rearrange("p (a b) -> p a b", a=4), ['b2'], ['yTn%d' % qt])

                own = qt // 2
                use_sel = own >= 4
                if use_sel:
                    psel = bank(7)[:, 0:64].rearrange("p (h n) -> p h n", h=8)
                    for h in range(8):
                        hp, base = h // 2, 64 * (h % 2)
                        mm(psel[:, h, :], QM[:, hp, :], kmean_z[:, h, :], True, True,
                           ['QM', 'kmean_z'], ['b7'])
                    tt('dve', scm, psel, bc(fbm[:, qt, :].unsqueeze(1), [128, 8, 8]), ALU.add, ['b7', 'fbm'], ['scm'])
                    for h in range(8):
                        Sx.op('dve', lambda e, h=h: e.max(out=topm[:, h, :], in_=scm[:, h, :]), ['scm'], ['topm'])
                    tt('dve', selm, scm, bc(topm[:, :, 2:3], [128, 8, 8]), ALU.is_ge, ['scm', 'topm'], ['selm'])
                for h in range(8):
                    hp, base = h // 2, 64 * (h % 2)
                    for n in range(own + 1):
                        ktl = [kt for kt in (2 * n, 2 * n + 1) if kt <= qt]
                        ps_i = next_ps()
                        pk = 'b%d' % ps_i
                        for j, kt in enumerate(ktl):
                            mm(bank(ps_i)[:, j * 128:(j + 1) * 128], kmT[base:base + 64, hp, kt * 128:(kt + 1) * 128],
                               QM[base:base + 64, hp, :], True, True, ['kmT', 'QM'], [pk])
                        pt, ptk = next_pt()
                        W = 128 * len(ktl)
                        act(pt[:, 0:W], bank(ps_i)[:, 0:W], AF.Exp, [pk], [ptk])
                        if ktl[-1] == qt:
                            j = len(ktl) - 1
                            tt('pool', pt[:, j * 128:(j + 1) * 128], pt[:, j * 128:(j + 1) * 128], trib, ALU.mult, [ptk, 'trib'], [ptk])
                        pm = bank(6)[:, 0:65]
                        for j, kt in enumerate(ktl):
                            mm(pm, pt[:, j * 128:(j + 1) * 128], VM[:, kt, h, :], j == 0, j == len(ktl) - 1, [ptk, 'VM'], ['b6'])
                        ok = 'oacc%d' % h
                        if n == 0:
                            if use_sel and n != own:
                                ts('dve', oacc[:, h, :], pm, selm[:, h, n:n + 1], None, ALU.mult, None, ['b6', 'selm'], [ok])
                            else:
                                cp('dve', oacc[:, h, :], pm, ['b6'], [ok])
                        else:
                            if use_sel and n != own:
                                stt(oacc[:, h, :], pm, selm[:, h, n:n + 1], oacc[:, h, :], ALU.mult, ALU.add, ['b6', 'selm', ok], [ok])
                            else:
                                tt('dve', oacc[:, h, :], pm, oacc[:, h, :], ALU.add, ['b6', ok], [ok])
                oks = ['oacc%d' % h for h in range(8)]
                ts('dve', rsm, oacc[:, :, 64], 1e-30, None, ALU.max, None, oks, ['rsm'])
                recip(rsm, rsm, ['rsm'], ['rsm'])
                tt('dve', y_b.rearrange("p (h d) -> p h d", h=8), oacc[:, :, 0:64], bc(rsm.unsqueeze(2), [128, 8, 64]), ALU.mult,
                   oks + ['rsm'], ['y_b'])
                for c in range(4):
                    tr(ptb[:, c * 128:(c + 1) * 128], y_b[:, c * 128:(c + 1) * 128], identb, ['y_b', 'identb'], ['b2'])
                cp('act', yTm[:, :, tsl], ptb[:, 0:512].rearrange("p (a b) -> p a b", a=4), ['b2'], ['yTm%d' % qt])

            checkpoint('D', [('yTn', yTn), ('yTm', yTm), ('QA', QA), ('QM', QM), ('sig', sig), ('impn', impn), ('yacc', yacc), ('oacc', oacc), ('selm', selm)])
            AR.release(ATT)
            Sx.barrier()
            xres = AR.alloc(128, [NT, D], F32)
            XE = AR.mark()
            mT = AR.alloc(128, [8, 1024], BF16)
            wE = [(AR.alloc(128, [8, 128], BF16), AR.alloc(128, [8, 128], BF16),
                   AR.alloc(128, [4, 128], BF16), AR.alloc(128, [4, 128], BF16)) for _ in range(2)]
            wo = AR.alloc(128, [8, 512], BF16)
            sga = AR.alloc(128, [512], BF16)
            sgb = AR.alloc(128, [512], BF16)
            m1 = AR.alloc(128, [512], F32)
            m2 = AR.alloc(128, [512], F32)
            xst = [AR.alloc(128, [512], F32) for _ in range(2)]
            yTn_all = ['yTn%d' % t for t in range(NT)]
            yTm_all = ['yTm%d' % t for t in range(NT)]
            for half in range(2):
                for c in range(8):
                    wa, wbb, wun, wum = wE[c % 2]
                    wk = 'wE%d' % (c % 2)
                    dma_cast(wk, wa, win_d[:, 2840 + c * 128:2840 + (c + 1) * 128].rearrange("(c p) n -> p c n", p=128), writes=[wk])
                    dma_cast(wk, wbb, win_d[:, 3864 + c * 128:3864 + (c + 1) * 128].rearrange("(c p) n -> p c n", p=128), writes=[wk])
                    dma_cast(wk, wun, wupn_d[:, c * 128:(c + 1) * 128].rearrange("(c p) n -> p c n", p=128), writes=[wk])
                    dma_cast(wk, wum, wupm_d[:, c * 128:(c + 1) * 128].rearrange("(c p) n -> p c n", p=128), writes=[wk])
                    for tg in range(2):
                        t0 = half * 1024 + tg * 512
                        cs = slice(t0, t0 + 512)
                        hk = ['hT%d' % t for t in range(t0 // 128, t0 // 128 + 4)]
                        ynk = ['yTn%d' % t for t in range(t0 // 128, t0 // 128 + 4)]
                        ymk = ['yTm%d' % t for t in range(t0 // 128, t0 // 128 + 4)]
                        for kc in range(8):
                            mm(bank(0), wa[:, kc, :], hT[:, kc, cs], kc == 0, kc == 7, hk + [wk], ['b0'])
                        for kc in range(8):
                            mm(bank(1), wbb[:, kc, :], hT[:, kc, cs], kc == 0, kc == 7, hk + [wk], ['b1'])
                        for kc in range(4):
                            mm(bank(3), wun[:, kc, :], yTn[:, kc, cs], kc == 0, kc == 3, ynk + [wk], ['b3'])
                        for kc in range(4):
                            mm(bank(4), wum[:, kc, :], yTm[:, kc, cs], kc == 0, kc == 3, ymk + [wk], ['b4'])
                        act(sga, bank(0), AF.Sigmoid, ['b0'], ['sga'])
                        act(sgb, bank(1), AF.Sigmoid, ['b1'], ['sgb'])
                        tt('dve', m1, bank(3), sga, ALU.mult, ['b3', 'sga'], ['m1'])
                        tt('dve', m2, bank(4), sgb, ALU.mult, ['b4', 'sgb'], ['m2'])
                        tt('pool', mT[:, c, tg * 512:(tg + 1) * 512], m1, m2, ALU.add, ['m1', 'm2'], ['mT'])
                for ch in range(2):
                    dma_cast('wo', wo, wout_d[:, ch * 512:(ch + 1) * 512].rearrange("(c p) n -> p c n", p=128), writes=['wo'])
                    for tl in range(8):
                        t = half * 8 + tl
                        xk = 'xst%d' % (tl % 2)
                        dma_sp(xk, xst[tl % 2], x_d[b, t * 128:(t + 1) * 128, ch * 512:(ch + 1) * 512], writes=[xk])
                        pj = next_pj()
                        pk = 'b%d' % pj
                        for kc in range(8):
                            mm(bank(pj), mT[:, kc, tl * 128:(tl + 1) * 128], wo[:, kc, :], kc == 0, kc == 7, ['mT', 'wo'], [pk])
                        tt('dve', xres[:, t, ch * 512:(ch + 1) * 512], bank(pj), xst[tl % 2], ALU.add, [pk, xk], ['xres%d' % t])

            checkpoint('E', [('xres', xres)])
            AR.release(XE)
            Sx.barrier()
            xn_sq = AR.alloc(128, [1024], F32)
            xn_b = AR.alloc(128, [1024], BF16)
            wfo = AR.alloc(128, [NFC, 512], BF16)
            wgu = [(AR.alloc(128, [8, 128], BF16), AR.alloc(128, [8, 128], BF16)) for _ in range(2)]
            sgt = AR.alloc(128, [512], BF16)
            ost = [AR.alloc(128, [512], F32) for _ in range(2)]
            pst = AR.alloc(128, [256], F32)
            pstb = AR.alloc(128, [256], BF16)
            sgp = AR.alloc(128, [512], F32)
            sv = AR.mark()
            AR.release(PERSIST)
            h2T = AR.alloc(128, [8, 1024], BF16)
            actT = AR.alloc(128, [NFC, 1024], BF16)
            pT = AR.alloc(128, [2, 1024], BF16)
            assert AR.off <= ATT, (AR.off, ATT)
            AR.off = sv
            wpg = wfo[:, 0:16, :].rearrange("p (k c) n -> p k (c n)", k=8)
            wpp = AR.alloc(128, [2, 1024], BF16)

            for half in range(2):
                for tl in range(8):
                    t = half * 8 + tl
                    norm_transpose(xres[:, t, :], gvec["ffn"], h2T[:, :, tl * 128:(tl + 1) * 128], 'xres%d' % t, 'h2T', xn_b, 2)
                for fc in range(NFC):
                    wg_, wu_ = wgu[fc % 2]
                    wk = 'wgu%d' % (fc % 2)
                    dma_cast(wk, wg_, wfi_d[:, fc * 128:(fc + 1) * 128].rearrange("(c p) n -> p c n", p=128), writes=[wk])
                    dma_cast(wk, wu_, wfi_d[:, DFF + fc * 128:DFF + (fc + 1) * 128].rearrange("(c p) n -> p c n", p=128), writes=[wk])
                    for tg in range(2):
                        cs = slice(tg * 512, (tg + 1) * 512)
                        bg, bu = (0, 1) if tg == 0 else (3, 4)
                        for kc in range(8):
                            mm(bank(bg), wg_[:, kc, :], h2T[:, kc, cs], kc == 0, kc == 7, ['h2T', wk], ['b%d' % bg])
                        for kc in range(8):
                            mm(bank(bu), wu_[:, kc, :], h2T[:, kc, cs], kc == 0, kc == 7, ['h2T', wk], ['b%d' % bu])
                        act(sgt, bank(bg), AF.Silu, ['b%d' % bg], ['sgt'])
                        tt('dve', actT[:, fc, cs], bank(bu), sgt, ALU.mult, ['b%d' % bu, 'sgt'], ['actT'])
                for ch in range(2):
                    dma_cast('wfo', wfo, wfo_d[:, ch * 512:(ch + 1) * 512].rearrange("(c p) n -> p c n", p=128), writes=['wfo'])
                    for tl in range(8):
                        t = half * 8 + tl
                        pj = 5 + (tl % 2)
                        pk = 'b%d' % pj
                        for fc in range(NFC):
                            mm(bank(pj), actT[:, fc, tl * 128:(tl + 1) * 128], wfo[:, fc, :], fc == 0, fc == NFC - 1, ['actT', 'wfo'], [pk])
                        xr = xres[:, t, ch * 512:(ch + 1) * 512]
                        tt('dve', xr, bank(pj), xr, ALU.add, [pk, 'xres%d' % t], ['xres%d' % t])
                for tl in range(8):
                    t = half * 8 + tl
                    norm_transpose(xres[:, t, :], gvec["ple"], h2T[:, :, tl * 128:(tl + 1) * 128], 'xres%d' % t, 'h2T', xn_b, 2)
                    dma_sp('pst', pst, p_d[b, t * 128:(t + 1) * 128, :], writes=['pst'])
                    cp('pool', pstb, pst, ['pst'], ['pstb'])
                    for c in range(2):
                        tr(bank_bf(7)[:, c * 128:(c + 1) * 128], pstb[:, c * 128:(c + 1) * 128], identb, ['pstb', 'identb'], ['b7'])
                    cp('act', pT[:, :, tl * 128:(tl + 1) * 128], bank_bf(7)[:, 0:256].rearrange("p (a b) -> p a b", a=2), ['b7'], ['pT'])
                dma_cast('wfo', wpg, wpg_d.rearrange("(c p) n -> p c n", p=128), writes=['wfo'])
                dma_cast('wpp', wpp, wpp_d.rearrange("(c p) n -> p c n", p=128), writes=['wpp'])
                for tl in range(8):
                    t = half * 8 + tl
                    tls = slice(tl * 128, (tl + 1) * 128)
                    for ch in range(2):
                        ccs = slice(ch * 512, (ch + 1) * 512)
                        bg, bp = (0, 1) if ch == 0 else (3, 4)
                        for kc in range(8):
                            mm(bank(bg), h2T[:, kc, tls], wpg[:, kc, ccs], kc == 0, kc == 7, ['h2T', 'wfo'], ['b%d' % bg])
                        for kc in range(2):
                            mm(bank(bp), pT[:, kc, tls], wpp[:, kc, ccs], kc == 0, kc == 1, ['pT', 'wpp'], ['b%d' % bp])
                        act(sgp, bank(bg), AF.Sigmoid, ['b%d' % bg], ['sgp'])
                        ob = ost[ch]
                        ok = 'ost%d' % ch
                        tt('dve', ob, bank(bp), sgp, ALU.mult, ['b%d' % bp, 'sgp'], [ok])
                        tt('pool', ob, ob, xres[:, t, ccs], ALU.add, [ok, 'xres%d' % t], [ok])
                        dma_sp('out%d' % ch, out_d[b, t * 128:(t + 1) * 128, ccs], ob, reads=[ok], writes=['outd'])
        except _Stop:
            pass
        Sx.barrier()

        keys = Sx.sem_keys()
        sems = {k: es.enter_context(nc.semaphore(k.replace(':', '_'))) for k in keys}
        with nc.Block() as block:
            run = Sx.runner(sems)
            block.sync(run('sp'))
            block.tensor(run('pe'))
            block.scalar(run('act'))
            block.vector(run('dve'))
            block.gpsimd(run('pool'))
    build_program.last_dbg = dbg_outs
    return nc


def _consts():
    c = {}
    c["c_ident"] = np.eye(128, dtype=np.float32)
    k = np.arange(128)[:, None]
    t = np.arange(128)[None, :]
    c["c_tri"] = (k <= t).astype(np.float32)
    c["c_anti"] = (k > t).astype(np.float32)
    cc = np.arange(127)[:, None]
    tt_ = np.arange(S)[None, :]
    c["c_cmask"] = ((16 * cc + 31) <= tt_).astype(np.float32)
    tl = np.arange(128)[:, None, None]
    qt = np.arange(NT)[None, :, None]
    j = np.arange(32)[None, None, :]
    cur = (qt * 128 + tl) // 64
    forced = (j == 0) | (j == cur) | (j == cur - 1)
    fb = np.where(forced, 1e9, np.where(j > cur, -1e9, 0.0)).astype(np.float32)
    c["c_fbs"] = np.ascontiguousarray(fb.reshape(128, NT * 32))
    n = np.arange(8)[None, None, :]
    own = (qt // 2)
    fm = np.where(n >= own, -1e9, 0.0).astype(np.float32) + np.zeros((128, 1, 1), np.float32)
    c["c_fbm"] = np.ascontiguousarray(fm.reshape(128, NT * 8))
    cs = np.arange(127) * 16
    ce = cs + 31
    jj = np.arange(32)
    ov = ((cs[:, None] <= jj[None, :] * 64 + 63) & (ce[:, None] >= jj[None, :] * 64)).astype(np.float32)
    c["c_ovl"] = np.concatenate([ov, np.ones((127, 1), np.float32)], axis=1)
    half = 8
    c["c_invf"] = (500000.0 ** (-np.arange(half, dtype=np.float32) / half)).astype(np.float32).reshape(1, 8)
    c["c_emat"] = (np.arange(S)[None, :] // 64 == np.arange(32)[:, None]).astype(np.float32)
    return c


_NC_CACHE = {}


def make_in_maps(inputs, n_cores=8):
    consts = _consts()
    f = lambda a: np.ascontiguousarray(np.asarray(a))
    shared = dict(consts)
    shared["g_mix"] = f(inputs["g_mix"][0]).reshape(8, 128)
    shared["g_ffn"] = f(inputs["g_ffn"][0]).reshape(8, 128)
    shared["g_ple"] = f(inputs["g_ple"][0]).reshape(8, 128)
    for n in ("nsa_q_gain", "nsa_kc_gain", "nsa_ks_gain", "nsa_kw_gain", "moba_q_gain", "moba_k_gain"):
        shared[n] = f(inputs[n][0]).reshape(1, 64)
    for n in ("w_in", "nsa_pe_k", "nsa_pe_v", "nsa_ck_w1", "nsa_ck_w2", "nsa_cv_w1", "nsa_cv_w2", "w_up_nsa", "w_up_moba",
              "w_out", "w_ffn_in", "w_ffn_out", "w_ple_gate", "w_ple_proj"):
        shared[n] = f(inputs[n][0])
    x = np.asarray(inputs["x"])
    p = np.asarray(inputs["p"])[0]
    pos = np.asarray(inputs["positions"]).astype(np.int32)
    in_maps = []
    for c in range(n_cores):
        m = dict(shared)
        m["x"] = f(x[c * NB:(c + 1) * NB])
        m["p"] = f(p[c * NB:(c + 1) * NB])
        m["pos"] = f(pos[c * NB:(c + 1) * NB]).reshape(NB, NT, 128)
        in_maps.append(m)
    return in_maps


def kernel(**inputs):
    n_cores = 8
    if "nc" not in _NC_CACHE:
        _NC_CACHE["nc"] = build_program()
    nc = _NC_CACHE["nc"]
    in_maps = make_in_maps(inputs, n_cores)
    res = run_bass_kernel_spmd(nc, in_maps, core_ids=list(range(n_cores)))
    out = np.concatenate([np.asarray(r["out"]) for r in res.results], axis=0)
    return out.astype(np.float32)
```

```python
import math
from contextlib import ExitStack
import numpy as np
import concourse.bass as bass
import concourse.mybir as mybir
from concourse.bass_utils import run_bass_kernel_spmd

F32 = mybir.dt.float32
BF16 = mybir.dt.bfloat16
I32 = mybir.dt.int32
AF = mybir.ActivationFunctionType
ALU = mybir.AluOpType
AX = mybir.AxisListType

NB = 2
S = 2048
D = 1024
NT = S // 128
DFF = 2816
NFC = DFF // 128
EPS = 1e-6
NEG = -30000.0
PI = math.pi


class Sched:
    ENG = ('pe', 'act', 'dve', 'pool', 'sp')

    def __init__(self):
        self.ops = {e: [] for e in self.ENG}
        self.cnt = {e: 0 for e in self.ENG}
        self.seen = {e: {} for e in self.ENG}
        self.res = {}
        self.dma_cnt = {}

    def _deps(self, eng, reads, writes):
        deps = {}

        def add(tok):
            if tok is None:
                return
            k, v = tok
            if deps.get(k, 0) < v:
                deps[k] = v
        for key in reads:
            r = self.res.get(key)
            if r is not None:
                add(r[0])
        for key in writes:
            r = self.res.get(key)
            if r is not None:
                add(r[0])
                for k, v in r[1].items():
                    add((k, v))
        out = []
        for k, v in deps.items():
            if eng == 'pe' and k == 'pe':
                continue
            if self.seen[eng].get(k, 0) >= v:
                continue
            self.seen[eng][k] = v
            out.append((k, v))
        return out

    def _commit(self, tok, reads, writes):
        k, v = tok
        for key in reads:
            r = self.res.setdefault(key, [None, {}])
            if r[1].get(k, 0) < v:
                r[1][k] = v
        for key in writes:
            self.res[key] = [tok, {}]

    def op(self, eng, fn, reads=(), writes=()):
        waits = self._deps(eng, reads, writes)
        self.cnt[eng] += 1
        tok = (eng, self.cnt[eng])
        self.ops[eng].append((waits, fn, (eng, 1)))
        self._commit(tok, reads, writes)
        return tok

    def dma(self, eng, stream, fn, reads=(), writes=()):
        waits = self._deps(eng, reads, writes)
        self.dma_cnt[stream] = self.dma_cnt.get(stream, 0) + 16
        tok = ('dma:' + stream, self.dma_cnt[stream])
        self.ops[eng].append((waits, fn, ('dma:' + stream, 16)))
        self._commit(tok, reads, writes)
        return tok

    def barrier(self):
        toks = [(e, self.cnt[e]) for e in self.ENG if self.cnt[e] > 0]
        toks += [('dma:' + s, v) for s, v in self.dma_cnt.items()]
        for e in self.ENG:
            waits = []
            for k, v in toks:
                if k == e and e == 'pe':
                    continue
                if self.seen[e].get(k, 0) >= v:
                    continue
                self.seen[e][k] = v
                waits.append((k, v))
            if waits:
                self.ops[e].append((waits, None, None))
        self.res = {}

    def sem_keys(self):
        ks = set(self.ENG)
        ks.update('dma:' + s for s in self.dma_cnt)
        return sorted(ks)

    def runner(self, sems):
        def run(eng_name):
            def body(engine):
                for waits, fn, inc in self.ops[eng_name]:
                    for k, v in waits:
                        engine.wait_ge(sems[k], v)
                    if fn is not None:
                        ins = fn(engine)
                        ins.then_inc(sems[inc[0]], inc[1])
            return body
        return run


class Arena:
    def __init__(self, t, nbytes):
        self.t = t
        self.off = 0
        self.nbytes = nbytes
        self.peak = 0

    def alloc(self, parts, shape, dtype):
        n = 1
        for s in shape:
            n *= s
        esz = 2 if dtype == BF16 else 4
        nb = (n * esz + 3) // 4 * 4
        assert self.off + nb <= self.nbytes, ("arena overflow", self.off, nb, self.nbytes)
        ap = self.t[0:parts, self.off // 2:(self.off + n * esz) // 2]
        self.off += nb
        self.peak = max(self.peak, self.off)
        if dtype != BF16:
            ap = ap.bitcast(dtype)
        if len(shape) == 2:
            ap = ap.rearrange("p (a b) -> p a b", a=shape[0])
        elif len(shape) == 3:
            ap = ap.rearrange("p (a b c) -> p a b c", a=shape[0], b=shape[1])
        return ap

    def mark(self):
        return self.off

    def release(self, m):
        self.off = m


def bc(ap, shape):
    return ap.broadcast_to(list(shape))


class _Stop(Exception):
    pass


def build_program(stop=None, nb=NB, qts=None):
    nc = bass.Bass("TRN2", target_bir_lowering=False)
    dbg_outs = {}

    def din(name, shape, dt=F32):
        return nc.dram_tensor(name, list(shape), dt, kind="ExternalInput").ap()

    x_d = din("x", [NB, S, D])
    p_d = din("p", [NB, S, 256])
    pos_d = din("pos", [NB, NT, 128], I32)
    gmix_d = din("g_mix", [8, 128])
    gffn_d = din("g_ffn", [8, 128])
    gple_d = din("g_ple", [8, 128])
    win_d = din("w_in", [D, 4888])
    gains_d = {n: din(n, [1, 64]) for n in ("nsa_q_gain", "nsa_kc_gain", "nsa_ks_gain", "nsa_kw_gain",
                                            "moba_q_gain", "moba_k_gain")}
    pek_d = din("nsa_pe_k", [32, 64])
    pev_d = din("nsa_pe_v", [32, 64])
    ckw1_d = din("nsa_ck_w1", [2048, 256])
    ckw2_d = din("nsa_ck_w2", [256, 64])
    cvw1_d = din("nsa_cv_w1", [2048, 256])
    cvw2_d = din("nsa_cv_w2", [256, 64])
    wupn_d = din("w_up_nsa", [512, D])
    wupm_d = din("w_up_moba", [512, D])
    wout_d = din("w_out", [D, D])
    wfi_d = din("w_ffn_in", [D, 2 * DFF])
    wfo_d = din("w_ffn_out", [DFF, D])
    wpg_d = din("w_ple_gate", [D, D])
    wpp_d = din("w_ple_proj", [256, D])
    ident_d = din("c_ident", [128, 128])
    tri_d = din("c_tri", [128, 128])
    anti_d = din("c_anti", [128, 128])
    cmask_d = din("c_cmask", [127, S])
    fbs_d = din("c_fbs", [128, NT * 32])
    fbm_d = din("c_fbm", [128, NT * 8])
    ovl_d = din("c_ovl", [127, 33])
    invf_d = din("c_invf", [1, 8])
    emat_d = din("c_emat", [32, S])
    out_d = nc.dram_tensor("out", [NB, S, D], F32, kind="ExternalOutput").ap()

    Sx = Sched()
    ARENA_BYTES = 206 * 1024
    with ExitStack() as es:
        arena_t = es.enter_context(nc.sbuf_tensor("arena", [128, ARENA_BYTES // 2], BF16))
        psum_t = es.enter_context(nc.psum_tensor("psum", [128, 8, 512], F32))
        AR = Arena(arena_t, ARENA_BYTES)

        def bank(i):
            return psum_t[:, i, :]

        def bank_bf(i):
            return psum_t[:, i, :].bitcast(BF16)

        def dma_sp(stream, out, in_, reads=(), writes=(), slow=False):
            if slow:
                return Sx.dma('sp', stream, lambda e: e.dma_start(out=out, in_=in_, allow_slow_non_contiguous=True), reads, writes)
            return Sx.dma('sp', stream, lambda e: e.dma_start(out=out, in_=in_), reads, writes)

        def dma_cast(stream, out, in_, reads=(), writes=()):
            return Sx.dma('pool', stream, lambda e: e.dma_start(out=out, in_=in_), reads, writes)

        def mm(out, lhsT, rhs, start, stop, reads, writes, skip=False):
            return Sx.op('pe', lambda e: e.matmul(out, lhsT=lhsT, rhs=rhs, start=start, stop=stop,
                                                  skip_group_check=skip), reads, writes)

        def tr(out, in_, ident, reads, writes):
            return Sx.op('pe', lambda e: e.transpose(out=out, in_=in_, identity=ident), reads, writes)

        def act(out, in_, func, reads, writes, scale=None, bias=None):
            kw = {}
            if scale is not None:
                kw['scale'] = scale
            if bias is not None:
                kw['bias'] = bias
            return Sx.op('act', lambda e: e.activation(out=out, in_=in_, func=func, **kw), reads, writes)

        def tt(eng, out, in0, in1, op, reads, writes):
            return Sx.op(eng, lambda e: e.tensor_tensor(out=out, in0=in0, in1=in1, op=op), reads, writes)

        def ts(eng, out, in0, s1, s2, op0, op1, reads, writes):
            if op1 is None:
                return Sx.op(eng, lambda e: e.tensor_scalar(out=out, in0=in0, scalar1=s1, scalar2=None, op0=op0),
                             reads, writes)
            return Sx.op(eng, lambda e: e.tensor_scalar(out=out, in0=in0, scalar1=s1, scalar2=s2, op0=op0, op1=op1),
                         reads, writes)

        def stt(out, in0, scalar, in1, op0, op1, reads, writes):
            return Sx.op('dve', lambda e: e.scalar_tensor_tensor(out=out, in0=in0, scalar=scalar, in1=in1,
                                                                 op0=op0, op1=op1), reads, writes)

        def cp(eng, out, in_, reads, writes):
            if eng == 'act':
                return Sx.op(eng, lambda e: e.copy(out=out, in_=in_), reads, writes)
            return Sx.op(eng, lambda e: e.tensor_copy(out=out, in_=in_), reads, writes)

        def recip(out, in_, reads, writes):
            return Sx.op('dve', lambda e: e.reciprocal(out=out, in_=in_), reads, writes)

        def memset(eng, ap, val, writes):
            return Sx.op(eng, lambda e: e.memset(ap, val), (), writes)

        def checkpoint(name, dumps):
            if stop != name:
                return
            Sx.barrier()
            for label, ap in dumps:
                shp = list(ap.shape)
                dt_ = ap.dtype
                d = nc.dram_tensor("dbg_" + label, shp, dt_, kind="ExternalOutput").ap()
                dbg_outs[label] = d
                Sx.dma('sp', 'dbg_' + label, lambda e, d=d, ap=ap: e.dma_start(out=d, in_=ap), (), ())
            raise _Stop()

        identf = AR.alloc(128, [128], F32)
        identb = AR.alloc(128, [128], BF16)
        trib = AR.alloc(128, [128], BF16)
        antib = AR.alloc(128, [128], BF16)
        cmask = AR.alloc(127, [NT, 128], BF16)
        fbs = AR.alloc(128, [NT, 32], F32)
        fbm = AR.alloc(128, [NT, 8], F32)
        invf = AR.alloc(128, [8], F32)
        gtile = {n: AR.alloc(128, [64], F32) for n in gains_d}
        gq8 = AR.alloc(128, [64], F32)
        gm8 = AR.alloc(128, [64], F32)
        gvec = {n: AR.alloc(128, [8], F32) for n in ("mix", "ffn", "ple")}
        gstage = AR.alloc(8, [128], F32)
        pestage = AR.alloc(32, [2, 64], F32)
        pestb = AR.alloc(32, [2, 64], BF16)
        peT = AR.alloc(64, [2, 32], BF16)
        cos_t = AR.alloc(128, [NT, 8], F32)
        sin_t = AR.alloc(128, [NT, 8], F32)
        cosc = AR.alloc(127, [8], F32)
        sinc = AR.alloc(127, [8], F32)
        n_ss = AR.alloc(128, [8], F32)
        n_rs = AR.alloc(128, [8], F32)
        x_ss = AR.alloc(128, [1], F32)
        x_rs = AR.alloc(128, [1], F32)

        dma_sp('c0', identf, ident_d, writes=['identf'])
        cp('dve', identb, identf, ['identf'], ['identb'])
        dma_cast('c1', trib, tri_d, writes=['trib'])
        dma_cast('c2', antib, anti_d, writes=['antib'])
        dma_cast('c3', cmask, cmask_d.rearrange("p (a b) -> p a b", a=NT), writes=['cmask'])
        dma_sp('c4', fbs, fbs_d.rearrange("p (a b) -> p a b", a=NT), writes=['fbs'])
        dma_sp('c5', fbm, fbm_d.rearrange("p (a b) -> p a b", a=NT), writes=['fbm'])
        dma_sp('c6', invf, invf_d.partition_broadcast(128), writes=['invf'])
        for i, n in enumerate(gains_d):
            dma_sp('c7_%d' % i, gtile[n], gains_d[n].partition_broadcast(128), writes=['g_' + n])
        ts('dve', gq8, gtile["nsa_q_gain"], 0.125, None, ALU.mult, None, ['g_nsa_q_gain'], ['gq8'])
        ts('dve', gm8, gtile["moba_q_gain"], 0.125, None, ALU.mult, None, ['g_moba_q_gain'], ['gm8'])
        for n, dd in (("mix", gmix_d), ("ffn", gffn_d), ("ple", gple_d)):
            dma_sp('c8', gstage, dd, writes=['gstage'])
            tr(bank(7)[:, 0:8], gstage, identf[0:8, 0:8], ['gstage', 'identf'], ['b7'])
            cp('dve', gvec[n], bank(7)[:, 0:8], ['b7'], ['gvec' + n])
        dma_sp('c9', pestage[:, 0, :], pek_d, writes=['pestage'])
        dma_sp('c9', pestage[:, 1, :], pev_d, writes=['pestage'])
        cp('dve', pestb, pestage, ['pestage'], ['pestb'])
        for kv in range(2):
            tr(bank_bf(7)[0:64, kv * 32:(kv + 1) * 32], pestb[:, kv, :], identb[0:32, 0:32], ['pestb', 'identb'], ['b7'])
        cp('dve', peT, bank_bf(7)[0:64, 0:64].rearrange("p (a b) -> p a b", a=2), ['b7'], ['peT'])

        PERSIST = AR.mark()

        def norm_transpose(src, gain, dst, rk, wk, xn, bk):
            sq = xn_sq
            tt(sq_eng[0], sq, src, src, ALU.mult, [rk], ['xn_sq'])
            Sx.op('dve', lambda e: e.tensor_reduce(out=x_ss, in_=sq, axis=AX.X, op=ALU.add), ['xn_sq'], ['x_ss'])
            act(x_rs, x_ss, AF.Ln, ['x_ss'], ['x_rs'], scale=1.0 / D, bias=EPS)
            act(x_rs, x_rs, AF.Exp, ['x_rs'], ['x_rs'], scale=-0.5)
            act(xn, src, AF.Copy, [rk, 'x_rs'], ['xn'], scale=x_rs[:, 0:1])
            pb = bank_bf(bk).rearrange("p (a b) -> p a b", a=8)
            for c in range(8):
                tr(pb[:, c, :], xn[:, c * 128:(c + 1) * 128], identb, ['xn', 'identb'], ['b%d' % bk])
            tt('dve', dst, pb, bc(gain.unsqueeze(2), [128, 8, 128]), ALU.mult, ['b%d' % bk], [wk])

        def head_norm_rope(src, H, rows, gain3, cosb, sinb, out_bf, rk, wk, extra_reads=()):
            R = rows
            qraw = hn_raw[0:R, 0:H * 64]
            sq = hn_sq[0:R, 0:H * 64]
            q3 = qraw.rearrange("p (h d) -> p h d", h=H)
            act(qraw, src, AF.Copy, [rk], ['hn_raw'])
            tt('dve', sq, qraw, qraw, ALU.mult, ['hn_raw'], ['hn_sq'])
            Sx.op('dve', lambda e: e.tensor_reduce(out=n_ss[0:R, 0:H], in_=sq.rearrange("p (h d) -> p h d", h=H),
                                                   axis=AX.X, op=ALU.add), ['hn_sq'], ['n_ss'])
            act(n_rs[0:R, 0:H], n_ss[0:R, 0:H], AF.Ln, ['n_ss'], ['n_rs'], scale=1.0 / 64, bias=EPS)
            act(n_rs[0:R, 0:H], n_rs[0:R, 0:H], AF.Exp, ['n_rs'], ['n_rs'], scale=-0.5)
            tt('dve', q3, q3, bc(n_rs[0:R, 0:H].unsqueeze(2), [R, H, 64]), ALU.mult, ['hn_raw', 'n_rs'], ['hn_raw'])
            tt('dve', q3, q3, gain3, ALU.mult, ['hn_raw'] + list(extra_reads), ['hn_raw'])
            cp('pool', out_bf[:, :, 16:64], q3[:, :, 16:64], ['hn_raw'], [wk])
            x1 = q3[:, :, 0:8]
            x2 = q3[:, :, 8:16]
            c3 = bc(cosb.unsqueeze(1), [R, H, 8])
            s3 = bc(sinb.unsqueeze(1), [R, H, 8])
            ra = hn_r[0:R, 0, 0:H, :]
            rb = hn_r[0:R, 1, 0:H, :]
            rc = hn_r[0:R, 2, 0:H, :]
            rd = hn_r[0:R, 3, 0:H, :]
            tt('dve', ra, x1, c3, ALU.mult, ['hn_raw', 'rope'], ['hn_ra'])
            tt('dve', rb, x2, s3, ALU.mult, ['hn_raw', 'rope'], ['hn_rb'])
            tt('dve', rc, x2, c3, ALU.mult, ['hn_raw', 'rope'], ['hn_rc'])
            tt('dve', rd, x1, s3, ALU.mult, ['hn_raw', 'rope'], ['hn_rd'])
            tt('dve', out_bf[:, :, 0:8], ra, rb, ALU.subtract, ['hn_ra', 'hn_rb'], [wk])
            tt('dve', out_bf[:, :, 8:16], rc, rd, ALU.add, ['hn_rc', 'hn_rd'], [wk])

        sq_eng = ['dve']
        pjc = [0]

        def next_pj():
            pjc[0] ^= 1
            return pjc[0]

        try:
          main_body = True
          for b in range(nb):
            AR.release(PERSIST)
            Sx.barrier()
            hT = AR.alloc(128, [8, S], BF16)
            yTn = AR.alloc(128, [4, S], BF16)
            yTm = AR.alloc(128, [4, S], BF16)
            ATT = AR.mark()
            ksA = AR.alloc(96, [2, S], BF16)
            kwT = AR.alloc(64, [2, S], BF16)
            kmT = AR.alloc(128, [4, S], BF16)
            VS = AR.alloc(128, [NT, 2, 65], BF16)
            VW = AR.alloc(128, [NT, 2, 65], BF16)
            VM = AR.alloc(128, [NT, 8, 65], BF16)
            kcT = AR.alloc(64, [2, 127], BF16)
            VC = AR.alloc(127, [2, 97], BF16)
            kmean = AR.alloc(128, [4, 8], BF16)
            kmean_f = AR.alloc(128, [4, 8], F32)
            kmean_z = AR.alloc(128, [8, 8], BF16)
            hn_raw = AR.alloc(128, [512], F32)
            hn_sq = AR.alloc(128, [512], F32)
            hn_r = AR.alloc(128, [4, 8, 8], F32)
            xn_sq = AR.alloc(128, [1024], F32)
            xn_b = AR.alloc(128, [1024], BF16)
            BC = AR.mark()
            xs = [AR.alloc(128, [1024], F32) for _ in range(2)]
            wb = [AR.alloc(128, [8, 512], BF16)]
            ks_b = AR.alloc(128, [2, 64], BF16)
            kvT = AR.alloc(64, [4, S], BF16)
            w1sb = AR.alloc(64, [32, 256], BF16)
            _sv = AR.mark()
            AR.release(_sv - 16 * 1024)
            wb.append(AR.alloc(128, [8, 512], BF16))
            AR.release(_sv)
            w2sb = AR.alloc(128, [2, 2, 64], BF16)
            hsb = AR.alloc(128, [2, 127], BF16)
            hb = AR.alloc(128, [2], F32)
            tm_b = AR.alloc(128, [512], BF16)
            posi = AR.alloc(16, [128], I32)
            posf16 = AR.alloc(16, [128], F32)
            posf = AR.alloc(128, [NT], F32)
            ang = AR.alloc(128, [NT, 8], F32)
            ang2 = AR.alloc(128, [NT, 8], F32)
            angi = AR.alloc(128, [NT, 8], I32)
            angn = AR.alloc(128, [NT, 8], F32)
            posci = AR.alloc(127, [1], I32)
            poscf = AR.alloc(127, [1], F32)

            def sincos(angv, outv, R, shape, shift):
                a2 = ang2[0:R] if len(shape) == 2 else ang2[0:R, 0, :]
                ai = angi[0:R] if len(shape) == 2 else angi[0:R, 0, :]
                an = angn[0:R] if len(shape) == 2 else angn[0:R, 0, :]
                ts('dve', a2, angv, shift, 1.0 / (2 * PI), ALU.add, ALU.mult, ['ang'], ['ang2'])
                cp('dve', ai, a2, ['ang2'], ['angi'])
                cp('dve', an, ai, ['angi'], ['angn'])
                ts('dve', an, an, -2 * PI, None, ALU.mult, None, ['angn'], ['angn'])
                tt('dve', a2, an, angv, ALU.add, ['angn', 'ang'], ['ang2'])
                ts('dve', a2, a2, shift, None, ALU.add, None, ['ang2'], ['ang2'])
                ts('dve', an, a2, PI, -2 * PI, ALU.is_gt, ALU.mult, ['ang2'], ['angn'])
                tt('dve', a2, a2, an, ALU.add, ['ang2', 'angn'], ['ang2'])
                ts('dve', an, a2, -PI, 2 * PI, ALU.is_lt, ALU.mult, ['ang2'], ['angn'])
                tt('dve', a2, a2, an, ALU.add, ['ang2', 'angn'], ['ang2'])
                ts('dve', a2, a2, PI, -PI, ALU.min, ALU.max, ['ang2'], ['ang2'])
                act(outv, a2, AF.Sin, ['ang2'], ['rope'])

            dma_sp('pos', posi, pos_d[b], writes=['posi'])
            cp('dve', posf16, posi, ['posi'], ['posf16'])
            tr(bank(7)[:, 0:16], posf16, identf[0:16, 0:16], ['posf16', 'identf'], ['b7'])
            cp('dve', posf, bank(7)[:, 0:16], ['b7'], ['posf'])
            tt('dve', ang, bc(posf.unsqueeze(2), [128, NT, 8]), bc(invf.unsqueeze(1), [128, NT, 8]), ALU.mult,
               ['posf', 'invf'], ['ang'])
            sincos(ang, sin_t, 128, [NT, 8], 0.0)
            sincos(ang, cos_t, 128, [NT, 8], PI / 2)
            posflat = pos_d[b].rearrange("a b -> (a b)")
            dma_sp('posc', posci, posflat[16:S].rearrange("(c s) -> c s", s=16)[:, 15:16], writes=['posci'], slow=True)
            cp('dve', poscf, posci, ['posci'], ['poscf'])
            angc = ang[0:127, 0, :]
            ts('dve', angc, invf[0:127, :], poscf[:, 0:1], None, ALU.mult, None, ['poscf', 'invf', 'rope'], ['ang'])
            sincos(angc, sinc, 127, [8], 0.0)
            sincos(angc, cosc, 127, [8], PI / 2)

            sq_eng[0] = 'pool'
            for t in range(NT):
                xb = xs[t % 2]
                dma_sp('xs%d' % (t % 2), xb, x_d[b, t * 128:(t + 1) * 128, :], writes=['xs%d' % (t % 2)])
                norm_transpose(xb, gvec["mix"], hT[:, :, t * 128:(t + 1) * 128], 'xs%d' % (t % 2), 'hT%d' % t, xn_b, 2)

            sq_eng[0] = 'dve'
            checkpoint('A', [('hT', hT), ('cos', cos_t), ('sin', sin_t), ('cosc', cosc), ('sinc', sinc)])
            memset('pool', VS[:, :, :, 64:65], 1.0, ['VS'])
            memset('pool', VW[:, :, :, 64:65], 1.0, ['VW'])
            memset('pool', VM[:, :, :, 64:65], 1.0, ['VM'])
            for g in range(2):
                dma_cast('emat', ksA[64:96, g, :], emat_d, writes=['ksA_e'])
                dma_cast('ovl', VC[:, g, 64:97], ovl_d, writes=['VC_o'])

            gks3 = bc(gtile["nsa_ks_gain"].unsqueeze(1), [128, 2, 64])
            gkw3 = bc(gtile["nsa_kw_gain"].unsqueeze(1), [128, 2, 64])
            gkm3 = bc(gtile["moba_k_gain"].unsqueeze(1), [128, 8, 64])
            chunks = [(512, 512, 'kcvcksvs'), (1024, 256, 'kwvw'), (1816, 512, 'km'), (2328, 512, 'vm')]
            hT_all = ['hT%d' % t for t in range(NT)]
            for ci, (c0, cw, kind) in enumerate(chunks):
                wbuf = wb[ci % 2]
                wk = 'wb0' if ci % 2 == 0 else 'w1sb'
                dma_cast(wk, wbuf[:, :, 0:cw], win_d[:, c0:c0 + cw].rearrange("(c p) n -> p c n", p=128), writes=[wk])

                def mmB(t, wbuf=wbuf, wk=wk, cw=cw):
                    pj = t % 2
                    tsl = slice(t * 128, (t + 1) * 128)
                    for kc in range(8):
                        mm(bank(pj)[:, 0:cw], hT[:, kc, tsl], wbuf[:, kc, 0:cw], kc == 0, kc == 7, ['hT%d' % t, wk], ['b%d' % pj])

                def postB(t, kind=kind):
                    pj = t % 2
                    pk = 'b%d' % pj
                    tsl = slice(t * 128, (t + 1) * 128)
                    ptb = bank_bf(2)
                    if kind == 'kcvcksvs':
                        cp('act', tm_b[:, 0:256], bank(pj)[:, 0:256], [pk], ['tm_b'])
                        cp('act', VS[:, t, :, 0:64], bank(pj)[:, 384:512].rearrange("p (g d) -> p g d", g=2), [pk], ['VS'])
                        head_norm_rope(bank(pj)[:, 256:384], 2, 128, gks3, cos_t[:, t, :], sin_t[:, t, :], ks_b, pk, 'ks_b',
                                       ['g_nsa_ks_gain'])
                        for i in range(4):
                            tr(ptb[0:64, i * 128:(i + 1) * 128], tm_b[:, i * 64:(i + 1) * 64], identb, ['tm_b', 'identb'], ['b2'])
                        cp('act', kvT[:, :, tsl], ptb[0:64, 0:512].rearrange("p (a b) -> p a b", a=4), ['b2'], ['kvT'])
                        for g in range(2):
                            tr(ptb[0:64, g * 128:(g + 1) * 128], ks_b[:, g, :], identb,
                               ['ks_b', 'identb'], ['b2'])
                        cp('act', ksA[0:64, :, tsl], ptb[0:64, 0:256].rearrange("p (a b) -> p a b", a=2), ['b2'], ['ksA'])
                    elif kind == 'kwvw':
                        cp('act', VW[:, t, :, 0:64], bank(pj)[:, 128:256].rearrange("p (g d) -> p g d", g=2), [pk], ['VW'])
                        kwb = tm_b[:, 0:128].rearrange("p (h d) -> p h d", h=2)
                        head_norm_rope(bank(pj)[:, 0:128], 2, 128, gkw3, cos_t[:, t, :], sin_t[:, t, :], kwb, pk, 'tm_b',
                                       ['g_nsa_kw_gain'])
                        for g in range(2):
                            tr(ptb[0:64, g * 128:(g + 1) * 128], tm_b[:, g * 64:(g + 1) * 64], identb, ['tm_b', 'identb'], ['b2'])
                        cp('act', kwT[0:64, :, tsl], ptb[0:64, 0:256].rearrange("p (a b) -> p a b", a=2), ['b2'], ['kwT'])
                    elif kind == 'km':
                        kmb = tm_b.rearrange("p (h d) -> p h d", h=8)
                        head_norm_rope(bank(pj), 8, 128, gkm3, cos_t[:, t, :], sin_t[:, t, :], kmb, pk, 'tm_b',
                                       ['g_moba_k_gain'])
                        for hp in range(4):
                            tr(ptb[:, hp * 128:(hp + 1) * 128], tm_b[:, hp * 128:(hp + 1) * 128], identb, ['tm_b', 'identb'], ['b2'])
                        cp('act', kmT[:, :, tsl], ptb[:, 0:512].rearrange("p (a b) -> p a b", a=4), ['b2'], ['kmT'])
                    else:
                        cp('act', VM[:, t, :, 0:64], bank(pj).rearrange("p (h d) -> p h d", h=8), [pk], ['VM'])

                mmB(0)
                for t in range(NT):
                    if t + 1 < NT:
                        mmB(t + 1)
                    postB(t)
            Sx.op('dve', lambda e: e.tensor_reduce(out=kmean_f, in_=kmT.rearrange("p h (n k) -> p h n k", k=256),
                                                   axis=AX.X, op=ALU.add), ['kmT'], ['kmean_f'])
            ts('dve', kmean, kmean_f, 1.0 / 256, None, ALU.mult, None, ['kmean_f'], ['kmean'])
            memset('dve', kmean_z, 0.0, ['kmean_z'])
            kz4 = kmean_z.rearrange("p (hp two) n -> p hp two n", two=2)
            cp('dve', kz4[0:64, :, 0, :], kmean[0:64, :, :], ['kmean', 'kmean_z'], ['kmean_z'])
            cp('dve', kz4[64:128, :, 1, :], kmean[64:128, :, :], ['kmean', 'kmean_z'], ['kmean_z'])

            checkpoint('B', [('ksA', ksA), ('kwT', kwT), ('kmT', kmT), ('VS', VS), ('VW', VW), ('VM', VM), ('kvT', kvT), ('kmean', kmean_f)])
            for kv, (w1_d, w2_d) in enumerate(((ckw1_d, ckw2_d), (cvw1_d, cvw2_d))):
                dma_cast('w1', w1sb, w1_d.rearrange("(i d) h -> d i h", d=64), writes=['w1sb'])
                dma_cast('w2', w2sb[:, kv, :, :], w2_d.rearrange("(hh p) d -> p hh d", p=128), writes=['w2sb'])
                for hh in range(2):
                    for i in range(32):
                        mm(bank(7)[:, hh:hh + 1], w1sb[:, i, hh * 128:(hh + 1) * 128], peT[:, kv, i:i + 1], i == 0, i == 31,
                           ['w1sb', 'peT'], ['b7'])
                cp('dve', hb, bank(7)[:, 0:2], ['b7'], ['hb'])
                for g in range(2):
                    src = kvT[:, kv * 2 + g, :].rearrange("p (c s) -> p c s", s=16)
                    for hh in range(2):
                        pj = next_pj()
                        pk = 'b%d' % pj
                        for i in range(32):
                            rhs = src[:, 0:127, i] if i < 16 else src[:, 1:128, i - 16]
                            mm(bank(pj)[:, 0:127], w1sb[:, i, hh * 128:(hh + 1) * 128], rhs, i == 0, i == 31, ['w1sb', 'kvT'], [pk])
                        act(hsb[:, hh, :], bank(pj)[:, 0:127], AF.Silu, [pk, 'hb'], ['hsb'], bias=hb[:, hh:hh + 1])
                    pj = next_pj()
                    pk = 'b%d' % pj
                    for hh in range(2):
                        mm(bank(pj)[0:127, 0:64], hsb[:, hh, :], w2sb[:, kv, hh, :], hh == 0, hh == 1, ['hsb', 'w2sb'], [pk])
                    if kv == 0:
                        kcb = tm_b[0:127, 0:64].rearrange("p (h d) -> p h d", h=1)
                        head_norm_rope(bank(pj)[0:127, 0:64], 1, 127, bc(gtile["nsa_kc_gain"][0:127].unsqueeze(1), [127, 1, 64]),
                                       cosc, sinc, kcb, pk, 'tm_b', ['g_nsa_kc_gain'])
                        tr(bank_bf(2)[0:64, 0:127], tm_b[0:127, 0:64], identb[0:127, 0:127], ['tm_b', 'identb'], ['b2'])
                        cp('act', kcT[:, g, :], bank_bf(2)[0:64, 0:127], ['b2'], ['kcT'])
                    else:
                        cp('act', VC[:, g, 0:64], bank(pj)[0:127, 0:64], [pk], ['VC'])

            checkpoint('C', [('kcT', kcT), ('VC', VC)])
            AR.release(BC)
            Sx.barrier()
            wq = AR.alloc(128, [8, 1048], BF16)
            QAs = [AR.alloc(96, [2, 512], BF16) for _ in range(2)]
            QMs = [AR.alloc(128, [4, 128], BF16) for _ in range(2)]
            qn_b = AR.alloc(128, [8, 64], BF16)
            qm_b = AR.alloc(128, [8, 64], BF16)
            sigs = [AR.alloc(128, [8, 3], F32) for _ in range(2)]
            Ec = AR.alloc(127, [512], BF16)
            Em = AR.alloc(127, [512], BF16)
            NPT = 4
            PT = [AR.alloc(128, [512], BF16) for _ in range(NPT)]
            rs4 = AR.alloc(128, [4], F32)
            coef = AR.alloc(128, [4], F32)
            impn = AR.alloc(128, [32], F32)
            top8 = AR.alloc(128, [8], F32)
            selb = AR.alloc(128, [32], F32)
            Bpad = AR.alloc(128, [96], BF16)
            yaccs = [AR.alloc(128, [8, 64], F32) for _ in range(2)]
            ytmp = AR.alloc(128, [4, 64], F32)
            oacc = AR.alloc(128, [8, 65], F32)
            scm = AR.alloc(128, [8, 8], F32)
            topm = AR.alloc(128, [8, 8], F32)
            selm = AR.alloc(128, [8, 8], F32)
            rsm = AR.alloc(128, [8], F32)
            y_b = AR.alloc(128, [512], BF16)
            y_b2 = AR.alloc(128, [512], BF16)

            dma_cast('wq', wq[:, :, 0:512], win_d[:, 0:512].rearrange("(c p) n -> p c n", p=128), writes=['wq'])
            dma_cast('wq', wq[:, :, 512:536], win_d[:, 1280:1304].rearrange("(c p) n -> p c n", p=128), writes=['wq'])
            dma_cast('wq', wq[:, :, 536:1048], win_d[:, 1304:1816].rearrange("(c p) n -> p c n", p=128), writes=['wq'])
            memset('dve', Bpad, 0.0, ['Bpad'])
            gq3 = bc(gq8.unsqueeze(1), [128, 8, 64])
            gm3 = bc(gm8.unsqueeze(1), [128, 8, 64])
            ptc = [0]

            def next_pt():
                ptc[0] = (ptc[0] + 1) % NPT
                return PT[ptc[0]], 'PT%d' % ptc[0]

            psc = [0]

            def next_ps():
                psc[0] ^= 1
                return 3 + psc[0]

            ptb = bank_bf(2)

            qlist0 = list(range(NT) if qts is None else qts)
            PAR = {q: i % 2 for i, q in enumerate(qlist0)}

            def front_a(qt):
                tsl = slice(qt * 128, (qt + 1) * 128)
                hk = 'hT%d' % qt
                for kc in range(8):
                    mm(bank(0), hT[:, kc, tsl], wq[:, kc, 0:512], kc == 0, kc == 7, [hk, 'wq'], ['b0'])
                for kc in range(8):
                    mm(bank(1), hT[:, kc, tsl], wq[:, kc, 536:1048], kc == 0, kc == 7, [hk, 'wq'], ['b1'])
                for kc in range(8):
                    mm(bank(7)[:, 0:24], hT[:, kc, tsl], wq[:, kc, 512:536], kc == 0, kc == 7, [hk, 'wq'], ['b7'])

            def front_b1(qt):
                p = PAR[qt]
                sig2 = sigs[p].rearrange("p h c -> p (h c)")
                sk = 'sig%d' % p
                act(sig2, bank(7)[:, 0:24], AF.Exp, ['b7'], [sk], scale=-1.0)
                ts('dve', sig2, sig2, 1.0, None, ALU.add, None, [sk], [sk])
                recip(sig2, sig2, [sk], [sk])
                head_norm_rope(bank(0), 8, 128, gq3, cos_t[:, qt, :], sin_t[:, qt, :], qn_b, 'b0', 'qn_b', ['gq8'])
                head_norm_rope(bank(1), 8, 128, gm3, cos_t[:, qt, :], sin_t[:, qt, :], qm_b, 'b1', 'qm_b', ['gm8'])

            def front_b2(qt):
                p = PAR[qt]
                QA, QM = QAs[p], QMs[p]
                for h in range(8):
                    tr(ptb[0:64, h * 128:(h + 1) * 128], qn_b[:, h, :], identb, ['qn_b', 'identb'], ['b2'])
                cp('act', QA[0:64, :, :], ptb[0:64, :].rearrange("p (g n) -> p g n", g=2), ['b2'], ['QA%d' % p])
                qm2 = qm_b.rearrange("p h d -> p (h d)")
                for hp in range(4):
                    tr(ptb[:, hp * 128:(hp + 1) * 128], qm2[:, hp * 128:(hp + 1) * 128], identb, ['qm_b', 'identb'], ['b2'])
                cp('act', QM, ptb[:, 0:512].rearrange("p (a b) -> p a b", a=4), ['b2'], ['QM%d' % p])

            def cmp_sel(qt):
                p = PAR[qt]
                QA, sig, yacc = QAs[p], sigs[p], yaccs[p]
                qk_, sk = 'QA%d' % p, 'sig%d' % p
                for g in range(2):
                    ps_i = next_ps()
                    pk = 'b%d' % ps_i
                    mm(bank(ps_i)[0:127, :], kcT[:, g, :], QA[0:64, g, :], True, True, ['kcT', qk_], [pk])
                    act(Ec, bank(ps_i)[0:127, :], AF.Exp, [pk], ['Ec'])
                    tt('pool', Em.rearrange("p (r t) -> p r t", r=4), Ec.rearrange("p (r t) -> p r t", r=4),
                       bc(cmask[:, qt, :].unsqueeze(1), [127, 4, 128]), ALU.mult, ['Ec', 'cmask'], ['Em'])
                    po = bank(5)[:, 0:388].rearrange("p (r c) -> p r c", r=4)
                    for r in range(4):
                        mm(po[:, r, :], Em[:, r * 128:(r + 1) * 128], VC[:, g, :], True, True, ['Em', 'VC', 'VC_o'], ['b5'])
                    ts('dve', rs4, po[:, :, 96], 1e-30, None, ALU.max, None, ['b5'], ['rs4'])
                    recip(rs4, rs4, ['rs4'], ['rs4'])
                    ts('dve', impn, po[:, 0, 64:96], rs4[:, 0:1], None, ALU.mult, None, ['b5', 'rs4'], ['impn'])
                    for r in range(1, 4):
                        stt(impn, po[:, r, 64:96], rs4[:, r:r + 1], impn, ALU.mult, ALU.add, ['b5', 'rs4', 'impn'], ['impn'])
                    tt('dve', coef, rs4, sig[:, 4 * g:4 * g + 4, 0], ALU.mult, ['rs4', sk], ['coef'])
                    tt('dve', yacc[:, 4 * g:4 * g + 4, :], po[:, :, 0:64], bc(coef.unsqueeze(2), [128, 4, 64]), ALU.mult,
                       ['b5', 'coef'], ['yacc%d_%d' % (p, g)])
                    tt('dve', impn, impn, fbs[:, qt, :], ALU.add, ['impn', 'fbs'], ['impn'])
                    Sx.op('dve', lambda e: e.max(out=top8, in_=impn), ['impn'], ['top8'])
                    ts('dve', selb, impn, top8[:, 7:8], -NEG, ALU.is_ge, ALU.mult, ['impn', 'top8'], ['selb'])
                    ts('dve', Bpad[:, 64:96], selb, NEG, None, ALU.add, None, ['selb'], ['Bpad'])
                    tr(ptb[0:96, 0:128], Bpad, identb, ['Bpad', 'identb'], ['b2'])
                    cp('act', QA[64:96, g, :].rearrange("p (r t) -> p r t", r=4),
                       bc(ptb[64:96, 0:128].unsqueeze(1), [32, 4, 128]), ['b2'], ['QAb%d' % p])

            def nsa_units(qt, g, kind):
                p = PAR[qt]
                QA, sig, yacc = QAs[p], sigs[p], yaccs[p]
                if kind == 'sel':
                    kts = [(kt, trib if kt == qt else None) for kt in range(qt + 1)]
                    K, kT, kkey, Vt, vkey, gate_idx, bk = 96, ksA, 'ksA', VS, 'VS', 1, 5 + g
                    qkeys = ['QA%d' % p, 'QAb%d' % p, 'ksA_e']
                else:
                    kts = []
                    for kt in range(max(0, qt - 4), qt + 1):
                        m = trib if kt == qt else (antib if kt == qt - 4 else None)
                        kts.append((kt, m))
                    K, kT, kkey, Vt, vkey, gate_idx, bk = 64, kwT, 'kwT', VW, 'VW', 2, 5
                    qkeys = ['QA%d' % p]
                po2 = bank(bk)[:, 0:260].rearrange("p (r c) -> p r c", r=4)
                pok = 'b%d' % bk
                units = []
                nk = len(kts)
                for j, (kt, mask) in enumerate(kts):
                    st = {}

                    def qk(j=j, kt=kt, mask=mask, st=st):
                        ps_i = next_ps()
                        pk = 'b%d' % ps_i
                        mm(bank(ps_i), kT[0:K, g, kt * 128:(kt + 1) * 128], QA[0:K, g, :], True, True, [kkey] + qkeys, [pk])
                        pt, ptk = next_pt()
                        act(pt, bank(ps_i), AF.Exp, [pk], [ptk])
                        if mask is not None:
                            p3 = pt.rearrange("p (r t) -> p r t", r=4)
                            tt('pool', p3, p3, bc(mask.unsqueeze(1), [128, 4, 128]), ALU.mult, [ptk, 'trib', 'antib'], [ptk])
                        st['pt'] = (pt, ptk)

                    def pv(j=j, kt=kt, st=st):
                        pt, ptk = st['pt']
                        for r in range(4):
                            mm(po2[:, r, :], pt[:, r * 128:(r + 1) * 128], Vt[:, kt, g, :], (j == 0 and r == 0), j == nk - 1,
                               [ptk, vkey], [pok], skip=True)
                        if j == nk - 1:
                            ts('dve', rs4, po2[:, :, 64], 1e-30, None, ALU.max, None, [pok], ['rs4'])
                            recip(rs4, rs4, ['rs4'], ['rs4'])
                            tt('dve', coef, rs4, sig[:, 4 * g:4 * g + 4, gate_idx], ALU.mult, ['rs4', 'sig%d' % p], ['coef'])
                            tt('dve', ytmp, po2[:, :, 0:64], bc(coef.unsqueeze(2), [128, 4, 64]), ALU.mult, [pok, 'coef'], ['ytmp'])
                            yk = 'yacc%d_%d' % (p, g)
                            tt('dve', yacc[:, 4 * g:4 * g + 4, :], yacc[:, 4 * g:4 * g + 4, :], ytmp, ALU.add, ['ytmp', yk], [yk])
                    units.append((qk, pv))
                return units

            def moba_sel(qt):
                p = PAR[qt]
                QM = QMs[p]
                psel = bank(7)[:, 0:64].rearrange("p (h n) -> p h n", h=8)
                for h in range(8):
                    mm(psel[:, h, :], QM[:, h // 2, :], kmean_z[:, h, :], True, True, ['QM%d' % p, 'kmean_z'], ['b7'])
                tt('dve', scm, psel, bc(fbm[:, qt, :].unsqueeze(1), [128, 8, 8]), ALU.add, ['b7', 'fbm'], ['scm'])
                for h in range(8):
                    Sx.op('dve', lambda e, h=h: e.max(out=topm[:, h, :], in_=scm[:, h, :]), ['scm'], ['topm'])
                tt('dve', selm, scm, bc(topm[:, :, 2:3], [128, 8, 8]), ALU.is_ge, ['scm', 'topm'], ['selm'])

            mbc = [0]

            def moba_units(qt):
                p = PAR[qt]
                QM = QMs[p]
                own = qt // 2
                use_sel = own >= 4
                units = []
                for h in range(8):
                    hp, base = h // 2, 64 * (h % 2)
                    for n in range(own + 1):
                        ktl = [kt for kt in (2 * n, 2 * n + 1) if kt <= qt]
                        st = {}

                        def qk(h=h, hp=hp, base=base, n=n, ktl=ktl, st=st):
                            ps_i = next_ps()
                            pk = 'b%d' % ps_i
                            for j, kt in enumerate(ktl):
                                mm(bank(ps_i)[:, j * 128:(j + 1) * 128], kmT[base:base + 64, hp, kt * 128:(kt + 1) * 128],
                                   QM[base:base + 64, hp, :], True, True, ['kmT', 'QM%d' % p], [pk])
                            pt, ptk = next_pt()
                            W = 128 * len(ktl)
                            act(pt[:, 0:W], bank(ps_i)[:, 0:W], AF.Exp, [pk], [ptk])
                            if ktl[-1] == qt:
                                j = len(ktl) - 1
                                tt('pool', pt[:, j * 128:(j + 1) * 128], pt[:, j * 128:(j + 1) * 128], trib, ALU.mult, [ptk, 'trib'], [ptk])
                            st['pt'] = (pt, ptk)

                        def pv(h=h, n=n, ktl=ktl, st=st):
                            pt, ptk = st['pt']
                            mbc[0] ^= 1
                            bk = 5 + mbc[0]
                            bkk = 'b%d' % bk
                            pm = bank(bk)[:, 0:65]
                            for j, kt in enumerate(ktl):
                                mm(pm, pt[:, j * 128:(j + 1) * 128], VM[:, kt, h, :], j == 0, j == len(ktl) - 1, [ptk, 'VM'], [bkk])
                            ok = 'oacc%d' % h
                            if n == 0:
                                if use_sel and n != own:
                                    ts('dve', oacc[:, h, :], pm, selm[:, h, n:n + 1], None, ALU.mult, None, [bkk, 'selm'], [ok])
                                else:
                                    cp('dve', oacc[:, h, :], pm, [bkk], [ok])
                            else:
                                if use_sel and n != own:
                                    stt(oacc[:, h, :], pm, selm[:, h, n:n + 1], oacc[:, h, :], ALU.mult, ALU.add, [bkk, 'selm', ok], [ok])
                                else:
                                    tt('dve', oacc[:, h, :], pm, oacc[:, h, :], ALU.add, [bkk, ok], [ok])
                        units.append((qk, pv))
                return units

            def finish_nsa(qt):
                p = PAR[qt]
                tsl = slice(qt * 128, (qt + 1) * 128)
                cp('act', y_b, yaccs[p].rearrange("p h d -> p (h d)"), ['yacc%d_0' % p, 'yacc%d_1' % p], ['y_b'])
                for c in range(4):
                    tr(ptb[:, c * 128:(c + 1) * 128], y_b[:, c * 128:(c + 1) * 128], identb, ['y_b', 'identb'], ['b2'])
                cp('act', yTn[:, :, tsl], ptb[:, 0:512].rearrange("p (a b) -> p a b", a=4), ['b2'], ['yTn%d' % qt])

            def finish_moba(qt):
                tsl = slice(qt * 128, (qt + 1) * 128)
                oks = ['oacc%d' % h for h in range(8)]
                ts('dve', rsm, oacc[:, :, 64], 1e-30, None, ALU.max, None, oks, ['rsm'])
                recip(rsm, rsm, ['rsm'], ['rsm'])
                tt('dve', y_b2.rearrange("p (h d) -> p h d", h=8), oacc[:, :, 0:64], bc(rsm.unsqueeze(2), [128, 8, 64]), ALU.mult,
                   oks + ['rsm'], ['y_b2'])
                for c in range(4):
                    tr(ptb[:, c * 128:(c + 1) * 128], y_b2[:, c * 128:(c + 1) * 128], identb, ['y_b2', 'identb'], ['b2'])
                cp('act', yTm[:, :, tsl], ptb[:, 0:512].rearrange("p (a b) -> p a b", a=4), ['b2'], ['yTm%d' % qt])

            def run_units(units, hooks):
                n = len(units)
                DEPTH = 2
                for i in range(n + DEPTH):
                    if i < n:
                        if i in hooks:
                            hooks[i]()
                        units[i][0]()
                    if i - DEPTH >= 0:
                        units[i - DEPTH][1]()

            qlist = list(range(NT) if qts is None else qts)
            front_a(qlist[0])
            front_b1(qlist[0])
            front_b2(qlist[0])
            cmp_sel(qlist[0])
            for qi, qt in enumerate(qlist):
                nxt = qlist[qi + 1] if qi + 1 < len(qlist) else None
                if qt // 2 >= 4:
                    moba_sel(qt)
                win = nsa_units(qt, 0, 'win') + nsa_units(qt, 1, 'win')
                mob = moba_units(qt)
                sel = nsa_units(qt, 0, 'sel') + nsa_units(qt, 1, 'sel')
                units = win + mob + sel
                hooks = {}
                if nxt is not None:
                    front_a(nxt)
                    front_b1(nxt)
                    hooks[len(win) + len(mob)] = (lambda nxt=nxt: front_b2(nxt))
                run_units(units, hooks)
                finish_moba(qt)
                finish_nsa(qt)
                if nxt is not None:
                    cmp_sel(nxt)
            QA, QM, sig, yacc = QAs[0], QMs[0], sigs[0], yaccs[0]

            checkpoint('D', [('yTn', yTn), ('yTm', yTm), ('QA', QA), ('QM', QM), ('sig', sig), ('impn', impn), ('yacc', yacc), ('oacc', oacc), ('selm', selm)])
            AR.release(ATT)
            Sx.barrier()
            xres = AR.alloc(128, [NT, D], F32)
            XE = AR.mark()
            mT = AR.alloc(128, [8, 1024], BF16)
            wE = [(AR.alloc(128, [8, 128], BF16), AR.alloc(128, [8, 128], BF16),
                   AR.alloc(128, [4, 128], BF16), AR.alloc(128, [4, 128], BF16)) for _ in range(2)]
            wo = AR.alloc(128, [8, 512], BF16)
            sga = AR.alloc(128, [512], BF16)
            sgb = AR.alloc(128, [512], BF16)
            m1 = AR.alloc(128, [512], F32)
            m2 = AR.alloc(128, [512], F32)
            xst = [AR.alloc(128, [512], F32) for _ in range(2)]
            yTn_all = ['yTn%d' % t for t in range(NT)]
            yTm_all = ['yTm%d' % t for t in range(NT)]
            for half in range(2):
                for c in range(8):
                    wa, wbb, wun, wum = wE[c % 2]
                    wk = 'wE%d' % (c % 2)
                    dma_cast(wk, wa, win_d[:, 2840 + c * 128:2840 + (c + 1) * 128].rearrange("(c p) n -> p c n", p=128), writes=[wk])
                    dma_cast(wk, wbb, win_d[:, 3864 + c * 128:3864 + (c + 1) * 128].rearrange("(c p) n -> p c n", p=128), writes=[wk])
                    dma_cast(wk, wun, wupn_d[:, c * 128:(c + 1) * 128].rearrange("(c p) n -> p c n", p=128), writes=[wk])
                    dma_cast(wk, wum, wupm_d[:, c * 128:(c + 1) * 128].rearrange("(c p) n -> p c n", p=128), writes=[wk])
                    for tg in range(2):
                        t0 = half * 1024 + tg * 512
                        cs = slice(t0, t0 + 512)
                        hk = ['hT%d' % t for t in range(t0 // 128, t0 // 128 + 4)]
                        ynk = ['yTn%d' % t for t in range(t0 // 128, t0 // 128 + 4)]
                        ymk = ['yTm%d' % t for t in range(t0 // 128, t0 // 128 + 4)]
                        for kc in range(8):
                            mm(bank(0), wa[:, kc, :], hT[:, kc, cs], kc == 0, kc == 7, hk + [wk], ['b0'])
                        for kc in range(8):
                            mm(bank(1), wbb[:, kc, :], hT[:, kc, cs], kc == 0, kc == 7, hk + [wk], ['b1'])
                        for kc in range(4):
                            mm(bank(3), wun[:, kc, :], yTn[:, kc, cs], kc == 0, kc == 3, ynk + [wk], ['b3'])
                        for kc in range(4):
                            mm(bank(4), wum[:, kc, :], yTm[:, kc, cs], kc == 0, kc == 3, ymk + [wk], ['b4'])
                        act(sga, bank(0), AF.Sigmoid, ['b0'], ['sga'])
                        act(sgb, bank(1), AF.Sigmoid, ['b1'], ['sgb'])
                        tt('dve', m1, bank(3), sga, ALU.mult, ['b3', 'sga'], ['m1'])
                        tt('dve', m2, bank(4), sgb, ALU.mult, ['b4', 'sgb'], ['m2'])
                        tt('dve', mT[:, c, tg * 512:(tg + 1) * 512], m1, m2, ALU.add, ['m1', 'm2'], ['mT'])
                for ch in range(2):
                    dma_cast('wo', wo, wout_d[:, ch * 512:(ch + 1) * 512].rearrange("(c p) n -> p c n", p=128), writes=['wo'])
                    for tl in range(8):
                        t = half * 8 + tl
                        xk = 'xst%d' % (tl % 2)
                        dma_sp(xk, xst[tl % 2], x_d[b, t * 128:(t + 1) * 128, ch * 512:(ch + 1) * 512], writes=[xk])
                        pj = next_pj()
                        pk = 'b%d' % pj
                        for kc in range(8):
                            mm(bank(pj), mT[:, kc, tl * 128:(tl + 1) * 128], wo[:, kc, :], kc == 0, kc == 7, ['mT', 'wo'], [pk])
                        tt('dve', xres[:, t, ch * 512:(ch + 1) * 512], bank(pj), xst[tl % 2], ALU.add, [pk, xk], ['xres%d' % t])

            checkpoint('E', [('xres', xres)])
            AR.release(XE)
            Sx.barrier()
            xn_sq = AR.alloc(128, [1024], F32)
            xn_b = AR.alloc(128, [1024], BF16)
            wfo = AR.alloc(128, [NFC, 512], BF16)
            wgu = [(AR.alloc(128, [8, 128], BF16), AR.alloc(128, [8, 128], BF16)) for _ in range(2)]
            sgt = AR.alloc(128, [512], BF16)
            ost = [AR.alloc(128, [512], F32) for _ in range(2)]
            pst = AR.alloc(128, [256], F32)
            pstb = AR.alloc(128, [256], BF16)
            sgp = AR.alloc(128, [512], F32)
            sv = AR.mark()
            AR.release(PERSIST)
            h2T = AR.alloc(128, [8, 1024], BF16)
            actT = AR.alloc(128, [NFC, 1024], BF16)
            pT = AR.alloc(128, [2, 1024], BF16)
            assert AR.off <= ATT, (AR.off, ATT)
            AR.off = sv
            wpg = wfo[:, 0:16, :].rearrange("p (k c) n -> p k (c n)", k=8)
            wpp = AR.alloc(128, [2, 1024], BF16)

            for half in range(2):
                for tl in range(8):
                    t = half * 8 + tl
                    norm_transpose(xres[:, t, :], gvec["ffn"], h2T[:, :, tl * 128:(tl + 1) * 128], 'xres%d' % t, 'h2T', xn_b, 2)
                for fc in range(NFC):
                    wg_, wu_ = wgu[fc % 2]
                    wk = 'wgu%d' % (fc % 2)
                    dma_cast(wk, wg_, wfi_d[:, fc * 128:(fc + 1) * 128].rearrange("(c p) n -> p c n", p=128), writes=[wk])
                    dma_cast(wk, wu_, wfi_d[:, DFF + fc * 128:DFF + (fc + 1) * 128].rearrange("(c p) n -> p c n", p=128), writes=[wk])
                    for tg in range(2):
                        cs = slice(tg * 512, (tg + 1) * 512)
                        bg, bu = (0, 1) if tg == 0 else (3, 4)
                        for kc in range(8):
                            mm(bank(bg), wg_[:, kc, :], h2T[:, kc, cs], kc == 0, kc == 7, ['h2T', wk], ['b%d' % bg])
                        for kc in range(8):
                            mm(bank(bu), wu_[:, kc, :], h2T[:, kc, cs], kc == 0, kc == 7, ['h2T', wk], ['b%d' % bu])
                        act(sgt, bank(bg), AF.Silu, ['b%d' % bg], ['sgt'])
                        tt('dve', actT[:, fc, cs], bank(bu), sgt, ALU.mult, ['b%d' % bu, 'sgt'], ['actT'])
                for ch in range(2):
                    dma_cast('wfo', wfo, wfo_d[:, ch * 512:(ch + 1) * 512].rearrange("(c p) n -> p c n", p=128), writes=['wfo'])
                    for tl in range(8):
                        t = half * 8 + tl
                        pj = 5 + (tl % 2)
                        pk = 'b%d' % pj
                        for fc in range(NFC):
                            mm(bank(pj), actT[:, fc, tl * 128:(tl + 1) * 128], wfo[:, fc, :], fc == 0, fc == NFC - 1, ['actT', 'wfo'], [pk])
                        xr = xres[:, t, ch * 512:(ch + 1) * 512]
                        tt('dve', xr, bank(pj), xr, ALU.add, [pk, 'xres%d' % t], ['xres%d' % t])
                for tl in range(8):
                    t = half * 8 + tl
                    norm_transpose(xres[:, t, :], gvec["ple"], h2T[:, :, tl * 128:(tl + 1) * 128], 'xres%d' % t, 'h2T', xn_b, 2)
                    dma_sp('pst', pst, p_d[b, t * 128:(t + 1) * 128, :], writes=['pst'])
                    cp('dve', pstb, pst, ['pst'], ['pstb'])
                    for c in range(2):
                        tr(bank_bf(7)[:, c * 128:(c + 1) * 128], pstb[:, c * 128:(c + 1) * 128], identb, ['pstb', 'identb'], ['b7'])
                    cp('act', pT[:, :, tl * 128:(tl + 1) * 128], bank_bf(7)[:, 0:256].rearrange("p (a b) -> p a b", a=2), ['b7'], ['pT'])
                dma_cast('wfo', wpg, wpg_d.rearrange("(c p) n -> p c n", p=128), writes=['wfo'])
                dma_cast('wpp', wpp, wpp_d.rearrange("(c p) n -> p c n", p=128), writes=['wpp'])
                for tl in range(8):
                    t = half * 8 + tl
                    tls = slice(tl * 128, (tl + 1) * 128)
                    for ch in range(2):
                        ccs = slice(ch * 512, (ch + 1) * 512)
                        bg, bp = (0, 1) if ch == 0 else (3, 4)
                        for kc in range(8):
                            mm(bank(bg), h2T[:, kc, tls], wpg[:, kc, ccs], kc == 0, kc == 7, ['h2T', 'wfo'], ['b%d' % bg])
                        for kc in range(2):
                            mm(bank(bp), pT[:, kc, tls], wpp[:, kc, ccs], kc == 0, kc == 1, ['pT', 'wpp'], ['b%d' % bp])
                        act(sgp, bank(bg), AF.Sigmoid, ['b%d' % bg], ['sgp'])
                        ob = ost[ch]
                        ok = 'ost%d' % ch
                        tt('dve', ob, bank(bp), sgp, ALU.mult, ['b%d' % bp, 'sgp'], [ok])
                        tt('dve', ob, ob, xres[:, t, ccs], ALU.add, [ok, 'xres%d' % t], [ok])
                        dma_sp('out%d' % ch, out_d[b, t * 128:(t + 1) * 128, ccs], ob, reads=[ok], writes=['outd'])
        except _Stop:
            pass
        Sx.barrier()

        keys = Sx.sem_keys()
        sems = {k: es.enter_context(nc.semaphore(k.replace(':', '_'))) for k in keys}
        with nc.Block() as block:
            run = Sx.runner(sems)
            block.sync(run('sp'))
            block.tensor(run('pe'))
            block.scalar(run('act'))
            block.vector(run('dve'))
            block.gpsimd(run('pool'))
    build_program.last_dbg = dbg_outs
    return nc


def _consts():
    c = {}
    c["c_ident"] = np.eye(128, dtype=np.float32)
    k = np.arange(128)[:, None]
    t = np.arange(128)[None, :]
    c["c_tri"] = (k <= t).astype(np.float32)
    c["c_anti"] = (k > t).astype(np.float32)
    cc = np.arange(127)[:, None]
    tt_ = np.arange(S)[None, :]
    c["c_cmask"] = ((16 * cc + 31) <= tt_).astype(np.float32)
    tl = np.arange(128)[:, None, None]
    qt = np.arange(NT)[None, :, None]
    j = np.arange(32)[None, None, :]
    cur = (qt * 128 + tl) // 64
    forced = (j == 0) | (j == cur) | (j == cur - 1)
    fb = np.where(forced, 1e9, np.where(j > cur, -1e9, 0.0)).astype(np.float32)
    c["c_fbs"] = np.ascontiguousarray(fb.reshape(128, NT * 32))
    n = np.arange(8)[None, None, :]
    own = (qt // 2)
    fm = np.where(n >= own, -1e9, 0.0).astype(np.float32) + np.zeros((128, 1, 1), np.float32)
    c["c_fbm"] = np.ascontiguousarray(fm.reshape(128, NT * 8))
    cs = np.arange(127) * 16
    ce = cs + 31
    jj = np.arange(32)
    ov = ((cs[:, None] <= jj[None, :] * 64 + 63) & (ce[:, None] >= jj[None, :] * 64)).astype(np.float32)
    c["c_ovl"] = np.concatenate([ov, np.ones((127, 1), np.float32)], axis=1)
    half = 8
    c["c_invf"] = (500000.0 ** (-np.arange(half, dtype=np.float32) / half)).astype(np.float32).reshape(1, 8)
    c["c_emat"] = (np.arange(S)[None, :] // 64 == np.arange(32)[:, None]).astype(np.float32)
    return c


_NC_CACHE = {}


def make_in_maps(inputs, n_cores=8):
    consts = _consts()
    f = lambda a: np.ascontiguousarray(np.asarray(a))
    shared = dict(consts)
    shared["g_mix"] = f(inputs["g_mix"][0]).reshape(8, 128)
    shared["g_ffn"] = f(inputs["g_ffn"][0]).reshape(8, 128)
    shared["g_ple"] = f(inputs["g_ple"][0]).reshape(8, 128)
    for n in ("nsa_q_gain", "nsa_kc_gain", "nsa_ks_gain", "nsa_kw_gain", "moba_q_gain", "moba_k_gain"):
        shared[n] = f(inputs[n][0]).reshape(1, 64)
    for n in ("w_in", "nsa_pe_k", "nsa_pe_v", "nsa_ck_w1", "nsa_ck_w2", "nsa_cv_w1", "nsa_cv_w2", "w_up_nsa", "w_up_moba",
              "w_out", "w_ffn_in", "w_ffn_out", "w_ple_gate", "w_ple_proj"):
        shared[n] = f(inputs[n][0])
    x = np.asarray(inputs["x"])
    p = np.asarray(inputs["p"])[0]
    pos = np.asarray(inputs["positions"]).astype(np.int32)
    in_maps = []
    for c in range(n_cores):
        m = dict(shared)
        m["x"] = f(x[c * NB:(c + 1) * NB])
        m["p"] = f(p[c * NB:(c + 1) * NB])
        m["pos"] = f(pos[c * NB:(c + 1) * NB]).reshape(NB, NT, 128)
        in_maps.append(m)
    return in_maps


def kernel(**inputs):
    n_cores = 8
    if "nc" not in _NC_CACHE:
        _NC_CACHE["nc"] = build_program()
    nc = _NC_CACHE["nc"]
    in_maps = make_in_maps(inputs, n_cores)
    res = run_bass_kernel_spmd(nc, in_maps, core_ids=list(range(n_cores)))
    out = np.concatenate([np.asarray(r["out"]) for r in res.results], axis=0)
    return out.astype(np.float32)
```

```python
import math
from contextlib import ExitStack
import numpy as np
import concourse.bass as bass
import concourse.mybir as mybir
from concourse.bass_utils import run_bass_kernel_spmd

F32 = mybir.dt.float32
BF16 = mybir.dt.bfloat16
I32 = mybir.dt.int32
AF = mybir.ActivationFunctionType
ALU = mybir.AluOpType
AX = mybir.AxisListType

NB = 2
S = 2048
D = 1024
NT = S // 128
DFF = 2816
NFC = DFF // 128
EPS = 1e-6
NEG = -30000.0
PI = math.pi


class Sched:
    ENG = ('pe', 'act', 'dve', 'pool', 'sp')

    def __init__(self):
        self.ops = {e: [] for e in self.ENG}
        self.cnt = {e: 0 for e in self.ENG}
        self.seen = {e: {} for e in self.ENG}
        self.res = {}
        self.dma_cnt = {}

    def _deps(self, eng, reads, writes):
        deps = {}

        def add(tok):
            if tok is None:
                return
            k, v = tok
            if deps.get(k, 0) < v:
                deps[k] = v
        for key in reads:
            r = self.res.get(key)
            if r is not None:
                add(r[0])
        for key in writes:
            r = self.res.get(key)
            if r is not None:
                add(r[0])
                for k, v in r[1].items():
                    add((k, v))
        out = []
        for k, v in deps.items():
            if eng == 'pe' and k == 'pe':
                continue
            if self.seen[eng].get(k, 0) >= v:
                continue
            self.seen[eng][k] = v
            out.append((k, v))
        return out

    def _commit(self, tok, reads, writes):
        k, v = tok
        for key in reads:
            r = self.res.setdefault(key, [None, {}])
            if r[1].get(k, 0) < v:
                r[1][k] = v
        for key in writes:
            self.res[key] = [tok, {}]

    def op(self, eng, fn, reads=(), writes=()):
        waits = self._deps(eng, reads, writes)
        self.cnt[eng] += 1
        tok = (eng, self.cnt[eng])
        self.ops[eng].append((waits, fn, (eng, 1)))
        self._commit(tok, reads, writes)
        return tok

    def dma(self, eng, stream, fn, reads=(), writes=()):
        waits = self._deps(eng, reads, writes)
        self.dma_cnt[stream] = self.dma_cnt.get(stream, 0) + 16
        tok = ('dma:' + stream, self.dma_cnt[stream])
        self.ops[eng].append((waits, fn, ('dma:' + stream, 16)))
        self._commit(tok, reads, writes)
        return tok

    def barrier(self):
        toks = [(e, self.cnt[e]) for e in self.ENG if self.cnt[e] > 0]
        toks += [('dma:' + s, v) for s, v in self.dma_cnt.items()]
        for e in self.ENG:
            waits = []
            for k, v in toks:
                if k == e and e == 'pe':
                    continue
                if self.seen[e].get(k, 0) >= v:
                    continue
                self.seen[e][k] = v
                waits.append((k, v))
            if waits:
                self.ops[e].append((waits, None, None))
        self.res = {}

    def sem_keys(self):
        ks = set(self.ENG)
        ks.update('dma:' + s for s in self.dma_cnt)
        return sorted(ks)

    def runner(self, sems):
        def run(eng_name):
            def body(engine):
                for waits, fn, inc in self.ops[eng_name]:
                    for k, v in waits:
                        engine.wait_ge(sems[k], v)
                    if fn is not None:
                        ins = fn(engine)
                        ins.then_inc(sems[inc[0]], inc[1])
            return body
        return run


class Arena:
    def __init__(self, t, nbytes):
        self.t = t
        self.off = 0
        self.nbytes = nbytes
        self.peak = 0

    def alloc(self, parts, shape, dtype):
        n = 1
        for s in shape:
            n *= s
        esz = 2 if dtype == BF16 else 4
        nb = (n * esz + 3) // 4 * 4
        assert self.off + nb <= self.nbytes, ("arena overflow", self.off, nb, self.nbytes)
        ap = self.t[0:parts, self.off // 2:(self.off + n * esz) // 2]
        self.off += nb
        self.peak = max(self.peak, self.off)
        if dtype != BF16:
            ap = ap.bitcast(dtype)
        if len(shape) == 2:
            ap = ap.rearrange("p (a b) -> p a b", a=shape[0])
        elif len(shape) == 3:
            ap = ap.rearrange("p (a b c) -> p a b c", a=shape[0], b=shape[1])
        return ap

    def mark(self):
        return self.off

    def release(self, m):
        self.off = m


def bc(ap, shape):
    return ap.broadcast_to(list(shape))


class _Stop(Exception):
    pass


def build_program(stop=None, nb=NB, qts=None):
    nc = bass.Bass("TRN2", target_bir_lowering=False)
    dbg_outs = {}

    def din(name, shape, dt=F32):
        return nc.dram_tensor(name, list(shape), dt, kind="ExternalInput").ap()

    x_d = din("x", [NB, S, D])
    p_d = din("p", [NB, S, 256])
    pos_d = din("pos", [NB, NT, 128], I32)
    gmix_d = din("g_mix", [8, 128])
    gffn_d = din("g_ffn", [8, 128])
    gple_d = din("g_ple", [8, 128])
    win_d = din("w_in", [D, 4888])
    gains_d = {n: din(n, [1, 64]) for n in ("nsa_q_gain", "nsa_kc_gain", "nsa_ks_gain", "nsa_kw_gain",
                                            "moba_q_gain", "moba_k_gain")}
    pek_d = din("nsa_pe_k", [32, 64])
    pev_d = din("nsa_pe_v", [32, 64])
    ckw1_d = din("nsa_ck_w1", [2048, 256])
    ckw2_d = din("nsa_ck_w2", [256, 64])
    cvw1_d = din("nsa_cv_w1", [2048, 256])
    cvw2_d = din("nsa_cv_w2", [256, 64])
    wupn_d = din("w_up_nsa", [512, D])
    wupm_d = din("w_up_moba", [512, D])
    wout_d = din("w_out", [D, D])
    wfi_d = din("w_ffn_in", [D, 2 * DFF])
    wfo_d = din("w_ffn_out", [DFF, D])
    wpg_d = din("w_ple_gate", [D, D])
    wpp_d = din("w_ple_proj", [256, D])
    ident_d = din("c_ident", [128, 128])
    tri_d = din("c_tri", [128, 128])
    anti_d = din("c_anti", [128, 128])
    cmask_d = din("c_cmask", [127, S])
    fbs_d = din("c_fbs", [128, NT * 32])
    fbm_d = din("c_fbm", [128, NT * 8])
    ovl_d = din("c_ovl", [127, 33])
    invf_d = din("c_invf", [1, 8])
    emat_d = din("c_emat", [32, S])
    trib4_d = din("c_trib4", [128, 512])
    antib4_d = din("c_antib4", [128, 512])
    out_d = nc.dram_tensor("out", [NB, S, D], F32, kind="ExternalOutput").ap()

    Sx = Sched()
    ARENA_BYTES = 207 * 1024 + 512
    with ExitStack() as es:
        arena_t = es.enter_context(nc.sbuf_tensor("arena", [128, ARENA_BYTES // 2], BF16))
        psum_t = es.enter_context(nc.psum_tensor("psum", [128, 8, 512], F32))
        AR = Arena(arena_t, ARENA_BYTES)

        def bank(i):
            return psum_t[:, i, :]

        def bank_bf(i):
            return psum_t[:, i, :].bitcast(BF16)

        def dma_sp(stream, out, in_, reads=(), writes=(), slow=False):
            if slow:
                return Sx.dma('sp', stream, lambda e: e.dma_start(out=out, in_=in_, allow_slow_non_contiguous=True), reads, writes)
            return Sx.dma('sp', stream, lambda e: e.dma_start(out=out, in_=in_), reads, writes)

        def dma_cast(stream, out, in_, reads=(), writes=()):
            return Sx.dma('pool', stream, lambda e: e.dma_start(out=out, in_=in_), reads, writes)

        def mm(out, lhsT, rhs, start, stop, reads, writes, skip=False):
            return Sx.op('pe', lambda e: e.matmul(out, lhsT=lhsT, rhs=rhs, start=start, stop=stop,
                                                  skip_group_check=skip), reads, writes)

        def tr(out, in_, ident, reads, writes):
            return Sx.op('pe', lambda e: e.transpose(out=out, in_=in_, identity=ident), reads, writes)

        def act(out, in_, func, reads, writes, scale=None, bias=None):
            kw = {}
            if scale is not None:
                kw['scale'] = scale
            if bias is not None:
                kw['bias'] = bias
            return Sx.op('act', lambda e: e.activation(out=out, in_=in_, func=func, **kw), reads, writes)

        def tt(eng, out, in0, in1, op, reads, writes):
            return Sx.op(eng, lambda e: e.tensor_tensor(out=out, in0=in0, in1=in1, op=op), reads, writes)

        def ts(eng, out, in0, s1, s2, op0, op1, reads, writes):
            if op1 is None:
                return Sx.op(eng, lambda e: e.tensor_scalar(out=out, in0=in0, scalar1=s1, scalar2=None, op0=op0),
                             reads, writes)
            return Sx.op(eng, lambda e: e.tensor_scalar(out=out, in0=in0, scalar1=s1, scalar2=s2, op0=op0, op1=op1),
                         reads, writes)

        def stt(out, in0, scalar, in1, op0, op1, reads, writes):
            return Sx.op('dve', lambda e: e.scalar_tensor_tensor(out=out, in0=in0, scalar=scalar, in1=in1,
                                                                 op0=op0, op1=op1), reads, writes)

        def cp(eng, out, in_, reads, writes):
            if eng == 'act':
                return Sx.op(eng, lambda e: e.copy(out=out, in_=in_), reads, writes)
            return Sx.op(eng, lambda e: e.tensor_copy(out=out, in_=in_), reads, writes)

        def recip(out, in_, reads, writes):
            return Sx.op('dve', lambda e: e.reciprocal(out=out, in_=in_), reads, writes)

        def memset(eng, ap, val, writes):
            return Sx.op(eng, lambda e: e.memset(ap, val), (), writes)

        def checkpoint(name, dumps):
            if stop != name:
                return
            Sx.barrier()
            for label, ap in dumps:
                shp = list(ap.shape)
                dt_ = ap.dtype
                d = nc.dram_tensor("dbg_" + label, shp, dt_, kind="ExternalOutput").ap()
                dbg_outs[label] = d
                Sx.dma('sp', 'dbg_' + label, lambda e, d=d, ap=ap: e.dma_start(out=d, in_=ap), (), ())
            raise _Stop()

        identf = AR.alloc(128, [128], F32)
        identb = AR.alloc(128, [128], BF16)
        trib = AR.alloc(128, [128], BF16)
        antib = AR.alloc(128, [128], BF16)
        trib4 = AR.alloc(128, [512], BF16)
        antib4 = AR.alloc(128, [512], BF16)
        cmask = AR.alloc(127, [NT, 128], BF16)
        fbs = AR.alloc(128, [NT, 32], F32)
        fbm = AR.alloc(128, [NT, 8], F32)
        invf = AR.alloc(128, [8], F32)
        gtile = {n: AR.alloc(128, [64], F32) for n in gains_d}
        gq8 = AR.alloc(128, [64], F32)
        gm8 = AR.alloc(128, [64], F32)
        gvec = {n: AR.alloc(128, [8], F32) for n in ("mix", "ffn", "ple")}
        gstage = AR.alloc(8, [128], F32)
        pestage = AR.alloc(32, [2, 64], F32)
        pestb = AR.alloc(32, [2, 64], BF16)
        peT = AR.alloc(64, [2, 32], BF16)
        cos_t = AR.alloc(128, [NT, 8], F32)
        sin_t = AR.alloc(128, [NT, 8], F32)
        cosc = AR.alloc(127, [8], F32)
        sinc = AR.alloc(127, [8], F32)
        n_ss = AR.alloc(128, [8], F32)
        n_rs = AR.alloc(128, [8], F32)
        x_ss2 = AR.alloc(128, [2], F32)
        x_rs2 = AR.alloc(128, [2], F32)

        dma_sp('c0', identf, ident_d, writes=['identf'])
        cp('dve', identb, identf, ['identf'], ['identb'])
        dma_cast('c1', trib, tri_d, writes=['trib'])
        dma_cast('c2', antib, anti_d, writes=['antib'])
        dma_cast('c2b', trib4, trib4_d, writes=['trib4'])
        dma_cast('c2c', antib4, antib4_d, writes=['antib4'])
        dma_cast('c3', cmask, cmask_d.rearrange("p (a b) -> p a b", a=NT), writes=['cmask'])
        dma_sp('c4', fbs, fbs_d.rearrange("p (a b) -> p a b", a=NT), writes=['fbs'])
        dma_sp('c5', fbm, fbm_d.rearrange("p (a b) -> p a b", a=NT), writes=['fbm'])
        dma_sp('c6', invf, invf_d.partition_broadcast(128), writes=['invf'])
        for i, n in enumerate(gains_d):
            dma_sp('c7_%d' % i, gtile[n], gains_d[n].partition_broadcast(128), writes=['g_' + n])
        ts('dve', gq8, gtile["nsa_q_gain"], 0.125, None, ALU.mult, None, ['g_nsa_q_gain'], ['gq8'])
        ts('dve', gm8, gtile["moba_q_gain"], 0.125, None, ALU.mult, None, ['g_moba_q_gain'], ['gm8'])
        for n, dd in (("mix", gmix_d), ("ffn", gffn_d), ("ple", gple_d)):
            dma_sp('c8', gstage, dd, writes=['gstage'])
            tr(bank(7)[:, 0:8], gstage, identf[0:8, 0:8], ['gstage', 'identf'], ['b7'])
            cp('dve', gvec[n], bank(7)[:, 0:8], ['b7'], ['gvec' + n])
        dma_sp('c9', pestage[:, 0, :], pek_d, writes=['pestage'])
        dma_sp('c9', pestage[:, 1, :], pev_d, writes=['pestage'])
        cp('dve', pestb, pestage, ['pestage'], ['pestb'])
        for kv in range(2):
            tr(bank_bf(7)[0:64, kv * 32:(kv + 1) * 32], pestb[:, kv, :], identb[0:32, 0:32], ['pestb', 'identb'], ['b7'])
        cp('dve', peT, bank_bf(7)[0:64, 0:64].rearrange("p (a b) -> p a b", a=2), ['b7'], ['peT'])

        PERSIST = AR.mark()

        ntc = [0]

        def norm_transpose(src, gain, dst, rk, wk, xn, bk):
            ntc[0] ^= 1
            i = ntc[0]
            sq = xn_sqs[i]
            xnb = xn_bs[i]
            ss = x_ss2[:, i:i + 1]
            rs = x_rs2[:, i:i + 1]
            bk = 2 if i == 0 else 7
            sk, xk, ssk, rsk = 'xn_sq%d' % i, 'xn%d' % i, 'x_ss%d' % i, 'x_rs%d' % i
            tt('pool', sq, src, src, ALU.mult, [rk], [sk])
            Sx.op('dve', lambda e: e.tensor_reduce(out=ss, in_=sq, axis=AX.X, op=ALU.add), [sk], [ssk])
            act(rs, ss, AF.Ln, [ssk], [rsk], scale=1.0 / D, bias=EPS)
            act(rs, rs, AF.Exp, [rsk], [rsk], scale=-0.5)
            act(xnb, src, AF.Copy, [rk, rsk], [xk], scale=rs)
            pb = bank_bf(bk).rearrange("p (a b) -> p a b", a=8)
            for c in range(8):
                tr(pb[:, c, :], xnb[:, c * 128:(c + 1) * 128], identb, [xk, 'identb'], ['b%d' % bk])
            tt('dve', dst, pb, bc(gain.unsqueeze(2), [128, 8, 128]), ALU.mult, ['b%d' % bk], [wk])

        def head_norm_rope(src, H, rows, gain3, cosb, sinb, out_bf, rk, wk, extra_reads=()):
            R = rows
            qraw = hn_raw[0:R, 0:H * 64]
            sq = hn_sq[0:R, 0:H * 64]
            q3 = qraw.rearrange("p (h d) -> p h d", h=H)
            act(qraw, src, AF.Copy, [rk], ['hn_raw'])
            tt('dve', sq, qraw, qraw, ALU.mult, ['hn_raw'], ['hn_sq'])
            Sx.op('dve', lambda e: e.tensor_reduce(out=n_ss[0:R, 0:H], in_=sq.rearrange("p (h d) -> p h d", h=H),
                                                   axis=AX.X, op=ALU.add), ['hn_sq'], ['n_ss'])
            act(n_rs[0:R, 0:H], n_ss[0:R, 0:H], AF.Ln, ['n_ss'], ['n_rs'], scale=1.0 / 64, bias=EPS)
            act(n_rs[0:R, 0:H], n_rs[0:R, 0:H], AF.Exp, ['n_rs'], ['n_rs'], scale=-0.5)
            tt('dve', q3, q3, bc(n_rs[0:R, 0:H].unsqueeze(2), [R, H, 64]), ALU.mult, ['hn_raw', 'n_rs'], ['hn_raw'])
            tt('dve', q3, q3, gain3, ALU.mult, ['hn_raw'] + list(extra_reads), ['hn_raw'])
            cp('pool', out_bf[:, :, 16:64], q3[:, :, 16:64], ['hn_raw'], [wk])
            x1 = q3[:, :, 0:8]
            x2 = q3[:, :, 8:16]
            c3 = bc(cosb.unsqueeze(1), [R, H, 8])
            s3 = bc(sinb.unsqueeze(1), [R, H, 8])
            ra = hn_r[0:R, 0, 0:H, :]
            rb = hn_r[0:R, 1, 0:H, :]
            rc = hn_r[0:R, 2, 0:H, :]
            rd = hn_r[0:R, 3, 0:H, :]
            tt('dve', ra, x1, c3, ALU.mult, ['hn_raw', 'rope'], ['hn_ra'])
            tt('dve', rb, x2, s3, ALU.mult, ['hn_raw', 'rope'], ['hn_rb'])
            tt('dve', rc, x2, c3, ALU.mult, ['hn_raw', 'rope'], ['hn_rc'])
            tt('dve', rd, x1, s3, ALU.mult, ['hn_raw', 'rope'], ['hn_rd'])
            tt('dve', out_bf[:, :, 0:8], ra, rb, ALU.subtract, ['hn_ra', 'hn_rb'], [wk])
            tt('dve', out_bf[:, :, 8:16], rc, rd, ALU.add, ['hn_rc', 'hn_rd'], [wk])

        sq_eng = ['dve']
        pjc = [0]

        def next_pj():
            pjc[0] ^= 1
            return pjc[0]

        try:
          main_body = True
          for b in range(nb):
            AR.release(PERSIST)
            Sx.barrier()
            hT = AR.alloc(128, [8, S], BF16)
            yTn = AR.alloc(128, [4, S], BF16)
            yTm = AR.alloc(128, [4, S], BF16)
            ATT = AR.mark()
            ksA = AR.alloc(96, [2, S], BF16)
            kwT = AR.alloc(64, [2, S], BF16)
            kmT = AR.alloc(128, [4, S], BF16)
            VS = AR.alloc(128, [NT, 2, 65], BF16)
            VW = AR.alloc(128, [NT, 2, 65], BF16)
            VM = AR.alloc(128, [NT, 8, 65], BF16)
            kcT = AR.alloc(64, [2, 127], BF16)
            VC = AR.alloc(127, [2, 97], BF16)
            kmean = AR.alloc(128, [4, 8], BF16)
            kmean_f = AR.alloc(128, [4, 8], F32)
            kmean_z = AR.alloc(128, [8, 8], BF16)
            hn_raw = AR.alloc(128, [512], F32)
            hn_sq = AR.alloc(128, [512], F32)
            hn_r = AR.alloc(128, [4, 8, 8], F32)
            xn_sqs = [AR.alloc(128, [1024], F32) for _ in range(2)]
            xn_bs = [AR.alloc(128, [1024], BF16) for _ in range(2)]
            xn_b = None
            BC = AR.mark()
            xs = [AR.alloc(128, [1024], F32) for _ in range(2)]
            wb = [AR.alloc(128, [8, 512], BF16)]
            ks_b = AR.alloc(128, [2, 64], BF16)
            kvT = AR.alloc(64, [4, S], BF16)
            w1sb = AR.alloc(64, [32, 256], BF16)
            _sv = AR.mark()
            AR.release(_sv - 16 * 1024)
            wb.append(AR.alloc(128, [8, 512], BF16))
            AR.release(_sv)
            w2sb = AR.alloc(128, [2, 2, 64], BF16)
            hsb = AR.alloc(128, [2, 127], BF16)
            hb = AR.alloc(128, [2], F32)
            tm_b = AR.alloc(128, [512], BF16)
            posi = AR.alloc(16, [128], I32)
            posf16 = AR.alloc(16, [128], F32)
            posf = AR.alloc(128, [NT], F32)
            ang = AR.alloc(128, [NT, 8], F32)
            ang2 = AR.alloc(128, [NT, 8], F32)
            angi = AR.alloc(128, [NT, 8], I32)
            angn = AR.alloc(128, [NT, 8], F32)
            posci = AR.alloc(127, [1], I32)
            poscf = AR.alloc(127, [1], F32)

            def sincos(angv, outv, R, shape, shift):
                a2 = ang2[0:R] if len(shape) == 2 else ang2[0:R, 0, :]
                ai = angi[0:R] if len(shape) == 2 else angi[0:R, 0, :]
                an = angn[0:R] if len(shape) == 2 else angn[0:R, 0, :]
                ts('dve', a2, angv, shift, 1.0 / (2 * PI), ALU.add, ALU.mult, ['ang'], ['ang2'])
                cp('dve', ai, a2, ['ang2'], ['angi'])
                cp('dve', an, ai, ['angi'], ['angn'])
                ts('dve', an, an, -2 * PI, None, ALU.mult, None, ['angn'], ['angn'])
                tt('dve', a2, an, angv, ALU.add, ['angn', 'ang'], ['ang2'])
                ts('dve', a2, a2, shift, None, ALU.add, None, ['ang2'], ['ang2'])
                ts('dve', an, a2, PI, -2 * PI, ALU.is_gt, ALU.mult, ['ang2'], ['angn'])
                tt('dve', a2, a2, an, ALU.add, ['ang2', 'angn'], ['ang2'])
                ts('dve', an, a2, -PI, 2 * PI, ALU.is_lt, ALU.mult, ['ang2'], ['angn'])
                tt('dve', a2, a2, an, ALU.add, ['ang2', 'angn'], ['ang2'])
                ts('dve', a2, a2, PI, -PI, ALU.min, ALU.max, ['ang2'], ['ang2'])
                act(outv, a2, AF.Sin, ['ang2'], ['rope'])

            dma_sp('pos', posi, pos_d[b], writes=['posi'])
            cp('dve', posf16, posi, ['posi'], ['posf16'])
            tr(bank(7)[:, 0:16], posf16, identf[0:16, 0:16], ['posf16', 'identf'], ['b7'])
            cp('dve', posf, bank(7)[:, 0:16], ['b7'], ['posf'])
            tt('dve', ang, bc(posf.unsqueeze(2), [128, NT, 8]), bc(invf.unsqueeze(1), [128, NT, 8]), ALU.mult,
               ['posf', 'invf'], ['ang'])
            sincos(ang, sin_t, 128, [NT, 8], 0.0)
            sincos(ang, cos_t, 128, [NT, 8], PI / 2)
            posflat = pos_d[b].rearrange("a b -> (a b)")
            dma_sp('posc', posci, posflat[16:S].rearrange("(c s) -> c s", s=16)[:, 15:16], writes=['posci'], slow=True)
            cp('dve', poscf, posci, ['posci'], ['poscf'])
            angc = ang[0:127, 0, :]
            ts('dve', angc, invf[0:127, :], poscf[:, 0:1], None, ALU.mult, None, ['poscf', 'invf', 'rope'], ['ang'])
            sincos(angc, sinc, 127, [8], 0.0)
            sincos(angc, cosc, 127, [8], PI / 2)

            sq_eng[0] = 'pool'
            for t in range(NT):
                xb = xs[t % 2]
                dma_sp('xs%d' % (t % 2), xb, x_d[b, t * 128:(t + 1) * 128, :], writes=['xs%d' % (t % 2)])
                norm_transpose(xb, gvec["mix"], hT[:, :, t * 128:(t + 1) * 128], 'xs%d' % (t % 2), 'hT%d' % t, xn_b, 2)

            sq_eng[0] = 'dve'
            checkpoint('A', [('hT', hT), ('cos', cos_t), ('sin', sin_t), ('cosc', cosc), ('sinc', sinc)])
            memset('pool', VS[:, :, :, 64:65], 1.0, ['VS'])
            memset('pool', VW[:, :, :, 64:65], 1.0, ['VW'])
            memset('pool', VM[:, :, :, 64:65], 1.0, ['VM'])
            for g in range(2):
                dma_cast('emat', ksA[64:96, g, :], emat_d, writes=['ksA_e'])
                dma_cast('ovl', VC[:, g, 64:97], ovl_d, writes=['VC_o'])

            gks3 = bc(gtile["nsa_ks_gain"].unsqueeze(1), [128, 2, 64])
            gkw3 = bc(gtile["nsa_kw_gain"].unsqueeze(1), [128, 2, 64])
            gkm3 = bc(gtile["moba_k_gain"].unsqueeze(1), [128, 8, 64])
            chunks = [(512, 512, 'kcvcksvs'), (1024, 256, 'kwvw'), (1816, 512, 'km'), (2328, 512, 'vm')]
            hT_all = ['hT%d' % t for t in range(NT)]
            for ci, (c0, cw, kind) in enumerate(chunks):
                wbuf = wb[ci % 2]
                wk = 'wb0' if ci % 2 == 0 else 'w1sb'
                dma_cast(wk, wbuf[:, :, 0:cw], win_d[:, c0:c0 + cw].rearrange("(c p) n -> p c n", p=128), writes=[wk])

                def mmB(t, wbuf=wbuf, wk=wk, cw=cw):
                    pj = t % 2
                    tsl = slice(t * 128, (t + 1) * 128)
                    for kc in range(8):
                        mm(bank(pj)[:, 0:cw], hT[:, kc, tsl], wbuf[:, kc, 0:cw], kc == 0, kc == 7, ['hT%d' % t, wk], ['b%d' % pj])

                def postB(t, kind=kind):
                    pj = t % 2
                    pk = 'b%d' % pj
                    tsl = slice(t * 128, (t + 1) * 128)
                    ptb = bank_bf(2)
                    if kind == 'kcvcksvs':
                        cp('act', tm_b[:, 0:256], bank(pj)[:, 0:256], [pk], ['tm_b'])
                        cp('act', VS[:, t, :, 0:64], bank(pj)[:, 384:512].rearrange("p (g d) -> p g d", g=2), [pk], ['VS'])
                        head_norm_rope(bank(pj)[:, 256:384], 2, 128, gks3, cos_t[:, t, :], sin_t[:, t, :], ks_b, pk, 'ks_b',
                                       ['g_nsa_ks_gain'])
                        for i in range(4):
                            tr(ptb[0:64, i * 128:(i + 1) * 128], tm_b[:, i * 64:(i + 1) * 64], identb, ['tm_b', 'identb'], ['b2'])
                        cp('act', kvT[:, :, tsl], ptb[0:64, 0:512].rearrange("p (a b) -> p a b", a=4), ['b2'], ['kvT'])
                        for g in range(2):
                            tr(ptb[0:64, g * 128:(g + 1) * 128], ks_b[:, g, :], identb,
                               ['ks_b', 'identb'], ['b2'])
                        cp('act', ksA[0:64, :, tsl], ptb[0:64, 0:256].rearrange("p (a b) -> p a b", a=2), ['b2'], ['ksA'])
                    elif kind == 'kwvw':
                        cp('act', VW[:, t, :, 0:64], bank(pj)[:, 128:256].rearrange("p (g d) -> p g d", g=2), [pk], ['VW'])
                        kwb = tm_b[:, 0:128].rearrange("p (h d) -> p h d", h=2)
                        head_norm_rope(bank(pj)[:, 0:128], 2, 128, gkw3, cos_t[:, t, :], sin_t[:, t, :], kwb, pk, 'tm_b',
                                       ['g_nsa_kw_gain'])
                        for g in range(2):
                            tr(ptb[0:64, g * 128:(g + 1) * 128], tm_b[:, g * 64:(g + 1) * 64], identb, ['tm_b', 'identb'], ['b2'])
                        cp('act', kwT[0:64, :, tsl], ptb[0:64, 0:256].rearrange("p (a b) -> p a b", a=2), ['b2'], ['kwT'])
                    elif kind == 'km':
                        kmb = tm_b.rearrange("p (h d) -> p h d", h=8)
                        head_norm_rope(bank(pj), 8, 128, gkm3, cos_t[:, t, :], sin_t[:, t, :], kmb, pk, 'tm_b',
                                       ['g_moba_k_gain'])
                        for hp in range(4):
                            tr(ptb[:, hp * 128:(hp + 1) * 128], tm_b[:, hp * 128:(hp + 1) * 128], identb, ['tm_b', 'identb'], ['b2'])
                        cp('act', kmT[:, :, tsl], ptb[:, 0:512].rearrange("p (a b) -> p a b", a=4), ['b2'], ['kmT'])
                    else:
                        cp('act', VM[:, t, :, 0:64], bank(pj).rearrange("p (h d) -> p h d", h=8), [pk], ['VM'])

                mmB(0)
                for t in range(NT):
                    if t + 1 < NT:
                        mmB(t + 1)
                    postB(t)
            Sx.op('dve', lambda e: e.tensor_reduce(out=kmean_f, in_=kmT.rearrange("p h (n k) -> p h n k", k=256),
                                                   axis=AX.X, op=ALU.add), ['kmT'], ['kmean_f'])
            ts('dve', kmean, kmean_f, 1.0 / 256, None, ALU.mult, None, ['kmean_f'], ['kmean'])
            memset('dve', kmean_z, 0.0, ['kmean_z'])
            kz4 = kmean_z.rearrange("p (hp two) n -> p hp two n", two=2)
            cp('dve', kz4[0:64, :, 0, :], kmean[0:64, :, :], ['kmean', 'kmean_z'], ['kmean_z'])
            cp('dve', kz4[64:128, :, 1, :], kmean[64:128, :, :], ['kmean', 'kmean_z'], ['kmean_z'])

            checkpoint('B', [('ksA', ksA), ('kwT', kwT), ('kmT', kmT), ('VS', VS), ('VW', VW), ('VM', VM), ('kvT', kvT), ('kmean', kmean_f)])
            for kv, (w1_d, w2_d) in enumerate(((ckw1_d, ckw2_d), (cvw1_d, cvw2_d))):
                dma_cast('w1', w1sb, w1_d.rearrange("(i d) h -> d i h", d=64), writes=['w1sb'])
                dma_cast('w2', w2sb[:, kv, :, :], w2_d.rearrange("(hh p) d -> p hh d", p=128), writes=['w2sb'])
                for hh in range(2):
                    for i in range(32):
                        mm(bank(7)[:, hh:hh + 1], w1sb[:, i, hh * 128:(hh + 1) * 128], peT[:, kv, i:i + 1], i == 0, i == 31,
                           ['w1sb', 'peT'], ['b7'])
                cp('dve', hb, bank(7)[:, 0:2], ['b7'], ['hb'])
                for g in range(2):
                    src = kvT[:, kv * 2 + g, :].rearrange("p (c s) -> p c s", s=16)
                    for hh in range(2):
                        pj = next_pj()
                        pk = 'b%d' % pj
                        for i in range(32):
                            rhs = src[:, 0:127, i] if i < 16 else src[:, 1:128, i - 16]
                            mm(bank(pj)[:, 0:127], w1sb[:, i, hh * 128:(hh + 1) * 128], rhs, i == 0, i == 31, ['w1sb', 'kvT'], [pk])
                        act(hsb[:, hh, :], bank(pj)[:, 0:127], AF.Silu, [pk, 'hb'], ['hsb'], bias=hb[:, hh:hh + 1])
                    pj = next_pj()
                    pk = 'b%d' % pj
                    for hh in range(2):
                        mm(bank(pj)[0:127, 0:64], hsb[:, hh, :], w2sb[:, kv, hh, :], hh == 0, hh == 1, ['hsb', 'w2sb'], [pk])
                    if kv == 0:
                        kcb = tm_b[0:127, 0:64].rearrange("p (h d) -> p h d", h=1)
                        head_norm_rope(bank(pj)[0:127, 0:64], 1, 127, bc(gtile["nsa_kc_gain"][0:127].unsqueeze(1), [127, 1, 64]),
                                       cosc, sinc, kcb, pk, 'tm_b', ['g_nsa_kc_gain'])
                        tr(bank_bf(2)[0:64, 0:127], tm_b[0:127, 0:64], identb[0:127, 0:127], ['tm_b', 'identb'], ['b2'])
                        cp('act', kcT[:, g, :], bank_bf(2)[0:64, 0:127], ['b2'], ['kcT'])
                    else:
                        cp('act', VC[:, g, 0:64], bank(pj)[0:127, 0:64], [pk], ['VC'])

            checkpoint('C', [('kcT', kcT), ('VC', VC)])
            AR.release(BC)
            Sx.barrier()
            wq = AR.alloc(128, [8, 1048], BF16)
            QAs = [AR.alloc(96, [2, 512], BF16) for _ in range(2)]
            QMs = [AR.alloc(128, [4, 128], BF16) for _ in range(2)]
            qn_b = AR.alloc(128, [8, 64], BF16)
            qm_b = AR.alloc(128, [8, 64], BF16)
            sigs = [AR.alloc(128, [8, 3], F32) for _ in range(2)]
            Ec = AR.alloc(127, [512], BF16)
            Em = AR.alloc(127, [512], BF16)
            NPT = 6
            PT = [AR.alloc(128, [512], BF16) for _ in range(NPT)]
            rs4 = AR.alloc(128, [4], F32)
            coef = AR.alloc(128, [4], F32)
            impn = AR.alloc(128, [32], F32)
            top8 = AR.alloc(128, [8], F32)
            selb = AR.alloc(128, [32], F32)
            Bpads = [AR.alloc(128, [96], BF16) for _ in range(2)]
            rs4c = AR.alloc(128, [4], F32)
            coefc = AR.alloc(128, [4], F32)
            yaccs = [AR.alloc(128, [8, 64], F32) for _ in range(2)]
            ytmp = AR.alloc(128, [4, 64], F32)
            oacc = AR.alloc(128, [8, 65], F32)
            scm = AR.alloc(128, [8, 8], F32)
            topm = AR.alloc(128, [8, 8], F32)
            selm = AR.alloc(128, [8, 8], F32)
            rsm = AR.alloc(128, [8], F32)
            y_b = AR.alloc(128, [512], BF16)
            y_b2 = AR.alloc(128, [512], BF16)

            dma_cast('wq', wq[:, :, 0:512], win_d[:, 0:512].rearrange("(c p) n -> p c n", p=128), writes=['wq'])
            dma_cast('wq', wq[:, :, 512:536], win_d[:, 1280:1304].rearrange("(c p) n -> p c n", p=128), writes=['wq'])
            dma_cast('wq', wq[:, :, 536:1048], win_d[:, 1304:1816].rearrange("(c p) n -> p c n", p=128), writes=['wq'])
            memset('dve', Bpads[0], 0.0, ['Bpad0'])
            memset('dve', Bpads[1], 0.0, ['Bpad1'])
            gq3 = bc(gq8.unsqueeze(1), [128, 8, 64])
            gm3 = bc(gm8.unsqueeze(1), [128, 8, 64])
            ptc = [0]

            def next_pt():
                ptc[0] = (ptc[0] + 1) % NPT
                return PT[ptc[0]], 'PT%d' % ptc[0]

            psc = [0]

            PSB = [3, 4, 0, 1]

            def next_ps():
                psc[0] = (psc[0] + 1) % len(PSB)
                return PSB[psc[0]]

            ptb = bank_bf(2)

            qlist0 = list(range(NT) if qts is None else qts)
            PAR = {q: i % 2 for i, q in enumerate(qlist0)}

            def front_a(qt):
                tsl = slice(qt * 128, (qt + 1) * 128)
                hk = 'hT%d' % qt
                for kc in range(8):
                    mm(bank(0), hT[:, kc, tsl], wq[:, kc, 0:512], kc == 0, kc == 7, [hk, 'wq'], ['b0'])
                for kc in range(8):
                    mm(bank(1), hT[:, kc, tsl], wq[:, kc, 536:1048], kc == 0, kc == 7, [hk, 'wq'], ['b1'])
                for kc in range(8):
                    mm(bank(7)[:, 0:24], hT[:, kc, tsl], wq[:, kc, 512:536], kc == 0, kc == 7, [hk, 'wq'], ['b7'])

            def front_b1(qt):
                p = PAR[qt]
                sig2 = sigs[p].rearrange("p h c -> p (h c)")
                sk = 'sig%d' % p
                act(sig2, bank(7)[:, 0:24], AF.Exp, ['b7'], [sk], scale=-1.0)
                ts('dve', sig2, sig2, 1.0, None, ALU.add, None, [sk], [sk])
                recip(sig2, sig2, [sk], [sk])
                head_norm_rope(bank(0), 8, 128, gq3, cos_t[:, qt, :], sin_t[:, qt, :], qn_b, 'b0', 'qn_b', ['gq8'])
                head_norm_rope(bank(1), 8, 128, gm3, cos_t[:, qt, :], sin_t[:, qt, :], qm_b, 'b1', 'qm_b', ['gm8'])

            def front_b2(qt):
                p = PAR[qt]
                QA, QM = QAs[p], QMs[p]
                for h in range(8):
                    tr(ptb[0:64, h * 128:(h + 1) * 128], qn_b[:, h, :], identb, ['qn_b', 'identb'], ['b2'])
                cp('act', QA[0:64, :, :], ptb[0:64, :].rearrange("p (g n) -> p g n", g=2), ['b2'], ['QA%d' % p])
                qm2 = qm_b.rearrange("p h d -> p (h d)")
                for hp in range(4):
                    tr(ptb[:, hp * 128:(hp + 1) * 128], qm2[:, hp * 128:(hp + 1) * 128], identb, ['qm_b', 'identb'], ['b2'])
                cp('act', QM, ptb[:, 0:512].rearrange("p (a b) -> p a b", a=4), ['b2'], ['QM%d' % p])

            def cs1(qt, g):
                p = PAR[qt]
                QA = QAs[p]
                ps_i = next_ps()
                pk = 'b%d' % ps_i
                mm(bank(ps_i)[0:127, :], kcT[:, g, :], QA[0:64, g, :], True, True, ['kcT', 'QA%d' % p], [pk])
                act(Ec, bank(ps_i)[0:127, :], AF.Exp, [pk], ['Ec'])
                tt('pool', Em.rearrange("p (r t) -> p r t", r=4), Ec.rearrange("p (r t) -> p r t", r=4),
                   bc(cmask[:, qt, :].unsqueeze(1), [127, 4, 128]), ALU.mult, ['Ec', 'cmask'], ['Em'])

            def cs2(qt, g):
                p = PAR[qt]
                sig, yacc = sigs[p], yaccs[p]
                sk = 'sig%d' % p
                po = bank(7)[:, 0:388].rearrange("p (r c) -> p r c", r=4)
                for r in range(4):
                    mm(po[:, r, :], Em[:, r * 128:(r + 1) * 128], VC[:, g, :], True, True, ['Em', 'VC', 'VC_o'], ['b7'])
                ts('dve', rs4c, po[:, :, 96], 1e-30, None, ALU.max, None, ['b7'], ['rs4c'])
                recip(rs4c, rs4c, ['rs4c'], ['rs4c'])
                ts('dve', impn, po[:, 0, 64:96], rs4c[:, 0:1], None, ALU.mult, None, ['b7', 'rs4c'], ['impn'])
                for r in range(1, 4):
                    stt(impn, po[:, r, 64:96], rs4c[:, r:r + 1], impn, ALU.mult, ALU.add, ['b7', 'rs4c', 'impn'], ['impn'])
                tt('dve', coefc, rs4c, sig[:, 4 * g:4 * g + 4, 0], ALU.mult, ['rs4c', sk], ['coefc'])
                tt('dve', yacc[:, 4 * g:4 * g + 4, :], po[:, :, 0:64], bc(coefc.unsqueeze(2), [128, 4, 64]), ALU.mult,
                   ['b7', 'coefc'], ['yacc%d_%d' % (p, g)])
                tt('dve', impn, impn, fbs[:, qt, :], ALU.add, ['impn', 'fbs'], ['impn'])
                Sx.op('dve', lambda e: e.max(out=top8, in_=impn), ['impn'], ['top8'])
                ts('dve', selb, impn, top8[:, 7:8], -NEG, ALU.is_ge, ALU.mult, ['impn', 'top8'], ['selb'])
                ts('dve', Bpads[g][:, 64:96], selb, NEG, None, ALU.add, None, ['selb'], ['Bpad%d' % g])

            def cs3(qt, g):
                p = PAR[qt]
                QA = QAs[p]
                tr(ptb[0:96, 0:128], Bpads[g], identb, ['Bpad%d' % g, 'identb'], ['b2'])
                cp('act', QA[64:96, g, :].rearrange("p (r t) -> p r t", r=4),
                   bc(ptb[64:96, 0:128].unsqueeze(1), [32, 4, 128]), ['b2'], ['QAb%d' % p])

            def cmp_sel(qt):
                for g in range(2):
                    cs1(qt, g)
                    cs2(qt, g)
                    cs3(qt, g)

            def nsa_units(qt, g, kind):
                p = PAR[qt]
                QA, sig, yacc = QAs[p], sigs[p], yaccs[p]
                if kind == 'sel':
                    kts = [(kt, trib4 if kt == qt else None) for kt in range(qt + 1)]
                    K, kT, kkey, Vt, vkey, gate_idx, bk = 96, ksA, 'ksA', VS, 'VS', 1, 5 + g
                    qkeys = ['QA%d' % p, 'QAb%d' % p, 'ksA_e']
                else:
                    kts = []
                    for kt in range(max(0, qt - 4), qt + 1):
                        m = trib4 if kt == qt else (antib4 if kt == qt - 4 else None)
                        kts.append((kt, m))
                    K, kT, kkey, Vt, vkey, gate_idx, bk = 64, kwT, 'kwT', VW, 'VW', 2, 5
                    qkeys = ['QA%d' % p]
                po2 = bank(bk)[:, 0:260].rearrange("p (r c) -> p r c", r=4)
                pok = 'b%d' % bk
                units = []
                nk = len(kts)
                for j, (kt, mask) in enumerate(kts):
                    st = {}

                    def qk(j=j, kt=kt, mask=mask, st=st):
                        ps_i = next_ps()
                        pk = 'b%d' % ps_i
                        mm(bank(ps_i), kT[0:K, g, kt * 128:(kt + 1) * 128], QA[0:K, g, :], True, mask is None, [kkey] + qkeys, [pk])
                        if mask is not None:
                            mm(bank(ps_i), identb, mask, False, True, ['identb', 'trib4', 'antib4'], [pk])
                        pt, ptk = next_pt()
                        act(pt, bank(ps_i), AF.Exp, [pk], [ptk])
                        st['pt'] = (pt, ptk)

                    def pv(j=j, kt=kt, st=st):
                        pt, ptk = st['pt']
                        for r in range(4):
                            mm(po2[:, r, :], pt[:, r * 128:(r + 1) * 128], Vt[:, kt, g, :], (j == 0 and r == 0), j == nk - 1,
                               [ptk, vkey], [pok], skip=True)
                        if j == nk - 1:
                            ts('dve', rs4, po2[:, :, 64], 1e-30, None, ALU.max, None, [pok], ['rs4'])
                            recip(rs4, rs4, ['rs4'], ['rs4'])
                            tt('dve', coef, rs4, sig[:, 4 * g:4 * g + 4, gate_idx], ALU.mult, ['rs4', 'sig%d' % p], ['coef'])
                            tt('dve', ytmp, po2[:, :, 0:64], bc(coef.unsqueeze(2), [128, 4, 64]), ALU.mult, [pok, 'coef'], ['ytmp'])
                            yk = 'yacc%d_%d' % (p, g)
                            tt('dve', yacc[:, 4 * g:4 * g + 4, :], yacc[:, 4 * g:4 * g + 4, :], ytmp, ALU.add, ['ytmp', yk], [yk])
                    units.append((qk, pv))
                return units

            def moba_sel(qt):
                p = PAR[qt]
                QM = QMs[p]
                psel = bank(7)[:, 0:64].rearrange("p (h n) -> p h n", h=8)
                for h in range(8):
                    mm(psel[:, h, :], QM[:, h // 2, :], kmean_z[:, h, :], True, True, ['QM%d' % p, 'kmean_z'], ['b7'])
                tt('dve', scm, psel, bc(fbm[:, qt, :].unsqueeze(1), [128, 8, 8]), ALU.add, ['b7', 'fbm'], ['scm'])
                for h in range(8):
                    Sx.op('dve', lambda e, h=h: e.max(out=topm[:, h, :], in_=scm[:, h, :]), ['scm'], ['topm'])
                tt('dve', selm, scm, bc(topm[:, :, 2:3], [128, 8, 8]), ALU.is_ge, ['scm', 'topm'], ['selm'])

            mbc = [0]

            def moba_units(qt):
                p = PAR[qt]
                QM = QMs[p]
                own = qt // 2
                use_sel = own >= 4
                units = []
                for h in range(8):
                    hp, base = h // 2, 64 * (h % 2)
                    blocks = list(range(own + 1))
                    groups = [blocks[i:i + 2] for i in range(0, len(blocks), 2)]
                    hst = {}
                    for gi, grp in enumerate(groups):
                        tiles = []
                        for n in grp:
                            for kt in (2 * n, 2 * n + 1):
                                if kt <= qt:
                                    tiles.append((n, kt))
                        st = {}

                        def qk(hp=hp, base=base, tiles=tiles, st=st):
                            ps_i = next_ps()
                            pk = 'b%d' % ps_i
                            for j, (n, kt) in enumerate(tiles):
                                diag = (kt == qt)
                                mm(bank(ps_i)[:, j * 128:(j + 1) * 128], kmT[base:base + 64, hp, kt * 128:(kt + 1) * 128],
                                   QM[base:base + 64, hp, :], True, not diag, ['kmT', 'QM%d' % p], [pk], skip=True)
                                if diag:
                                    mm(bank(ps_i)[:, j * 128:(j + 1) * 128], identb, trib4[:, 0:128], False, True,
                                       ['identb', 'trib4'], [pk], skip=True)
                            pt, ptk = next_pt()
                            W = 128 * len(tiles)
                            act(pt[:, 0:W], bank(ps_i)[:, 0:W], AF.Exp, [pk], [ptk])
                            st['pt'] = (pt, ptk)

                        def pv(h=h, gi=gi, grp=grp, tiles=tiles, st=st, hst=hst, ngroups=len(groups)):
                            pt, ptk = st['pt']
                            ok = 'oacc%d' % h
                            if not use_sel:
                                if gi == 0:
                                    mbc[0] ^= 1
                                    hst['bk'] = 5 + mbc[0]
                                bk = hst['bk']
                                bkk = 'b%d' % bk
                                pm = bank(bk)[:, 0:65]
                                for j, (n, kt) in enumerate(tiles):
                                    mm(pm, pt[:, j * 128:(j + 1) * 128], VM[:, kt, h, :], (gi == 0 and j == 0),
                                       (gi == ngroups - 1 and j == len(tiles) - 1), [ptk, 'VM'], [bkk])
                                if gi == ngroups - 1:
                                    cp('dve', oacc[:, h, :], pm, [bkk], [ok])
                                return
                            mbc[0] ^= 1
                            bk = 5 + mbc[0]
                            bkk = 'b%d' % bk
                            for bi, n in enumerate(grp):
                                pm = bank(bk)[:, bi * 65:(bi + 1) * 65]
                                tl = [(j, kt) for j, (nn, kt) in enumerate(tiles) if nn == n]
                                for jj, (j, kt) in enumerate(tl):
                                    mm(pm, pt[:, j * 128:(j + 1) * 128], VM[:, kt, h, :], jj == 0, jj == len(tl) - 1, [ptk, 'VM'], [bkk])
                            for bi, n in enumerate(grp):
                                pm = bank(bk)[:, bi * 65:(bi + 1) * 65]
                                if n == 0:
                                    ts('dve', oacc[:, h, :], pm, selm[:, h, n:n + 1], None, ALU.mult, None, [bkk, 'selm'], [ok])
                                elif n != own:
                                    stt(oacc[:, h, :], pm, selm[:, h, n:n + 1], oacc[:, h, :], ALU.mult, ALU.add, [bkk, 'selm', ok], [ok])
                                else:
                                    tt('dve', oacc[:, h, :], pm, oacc[:, h, :], ALU.add, [bkk, ok], [ok])
                        units.append((qk, pv))
                return units

            def finish_nsa(qt):
                p = PAR[qt]
                tsl = slice(qt * 128, (qt + 1) * 128)
                cp('act', y_b, yaccs[p].rearrange("p h d -> p (h d)"), ['yacc%d_0' % p, 'yacc%d_1' % p], ['y_b'])
                for c in range(4):
                    tr(ptb[:, c * 128:(c + 1) * 128], y_b[:, c * 128:(c + 1) * 128], identb, ['y_b', 'identb'], ['b2'])
                cp('act', yTn[:, :, tsl], ptb[:, 0:512].rearrange("p (a b) -> p a b", a=4), ['b2'], ['yTn%d' % qt])

            def finish_moba(qt):
                tsl = slice(qt * 128, (qt + 1) * 128)
                oks = ['oacc%d' % h for h in range(8)]
                ts('dve', rsm, oacc[:, :, 64], 1e-30, None, ALU.max, None, oks, ['rsm'])
                recip(rsm, rsm, ['rsm'], ['rsm'])
                tt('dve', y_b2.rearrange("p (h d) -> p h d", h=8), oacc[:, :, 0:64], bc(rsm.unsqueeze(2), [128, 8, 64]), ALU.mult,
                   oks + ['rsm'], ['y_b2'])
                for c in range(4):
                    tr(ptb[:, c * 128:(c + 1) * 128], y_b2[:, c * 128:(c + 1) * 128], identb, ['y_b2', 'identb'], ['b2'])
                cp('act', yTm[:, :, tsl], ptb[:, 0:512].rearrange("p (a b) -> p a b", a=4), ['b2'], ['yTm%d' % qt])

            DEPTH = 3

            def run_units(units, hooks):
                n = len(units)
                for i in range(n + DEPTH):
                    for fn in hooks.pop(i, []):
                        fn()
                    if i < n:
                        units[i][0]()
                    if i - DEPTH >= 0:
                        units[i - DEPTH][1]()
                for i in sorted(hooks):
                    for fn in hooks[i]:
                        fn()

            qlist = list(range(NT) if qts is None else qts)
            front_a(qlist[0])
            front_b1(qlist[0])
            front_b2(qlist[0])
            cmp_sel(qlist[0])
            for qi, qt in enumerate(qlist):
                nxt = qlist[qi + 1] if qi + 1 < len(qlist) else None
                prev = qlist[qi - 1] if qi > 0 else None
                if qt // 2 >= 4:
                    moba_sel(qt)
                win = nsa_units(qt, 0, 'win') + nsa_units(qt, 1, 'win')
                mob = moba_units(qt)
                sel = nsa_units(qt, 0, 'sel') + nsa_units(qt, 1, 'sel')
                units = win + mob + sel
                nW, nM = len(win), len(mob)
                hooks = {}

                def addh(i, fn):
                    hooks.setdefault(i, []).append(fn)
                if prev is not None:
                    addh(1, lambda prev=prev: finish_nsa(prev))
                addh(nW + nM + DEPTH, lambda qt=qt: finish_moba(qt))
                if nxt is not None:
                    addh(nW + 1, lambda nxt=nxt: (front_a(nxt), front_b1(nxt)))
                    addh(nW + nM, lambda nxt=nxt: front_b2(nxt))
                    base = nW + nM + 1
                    addh(base, lambda nxt=nxt: cs1(nxt, 0))
                    addh(base + 2, lambda nxt=nxt: cs2(nxt, 0))
                    addh(base + 3, lambda nxt=nxt: cs1(nxt, 1))
                    addh(base + 5, lambda nxt=nxt: cs3(nxt, 0))
                    addh(base + 5, lambda nxt=nxt: cs2(nxt, 1))
                    addh(base + 8, lambda nxt=nxt: cs3(nxt, 1))
                run_units(units, hooks)
            finish_nsa(qlist[-1])
            QA, QM, sig, yacc = QAs[0], QMs[0], sigs[0], yaccs[0]

            checkpoint('D', [('yTn', yTn), ('yTm', yTm), ('QA', QA), ('QM', QM), ('sig', sig), ('impn', impn), ('yacc', yacc), ('oacc', oacc), ('selm', selm)])
            AR.release(ATT)
            Sx.barrier()
            xres = AR.alloc(128, [NT, D], F32)
            XE = AR.mark()
            mT = AR.alloc(128, [8, 1024], BF16)
            wE = [(AR.alloc(128, [8, 128], BF16), AR.alloc(128, [8, 128], BF16),
                   AR.alloc(128, [4, 128], BF16), AR.alloc(128, [4, 128], BF16)) for _ in range(2)]
            wo = AR.alloc(128, [8, 512], BF16)
            sga = AR.alloc(128, [512], BF16)
            sgb = AR.alloc(128, [512], BF16)
            m1 = AR.alloc(128, [512], F32)
            m2 = AR.alloc(128, [512], F32)
            xst = [AR.alloc(128, [512], F32) for _ in range(2)]
            yTn_all = ['yTn%d' % t for t in range(NT)]
            yTm_all = ['yTm%d' % t for t in range(NT)]
            for half in range(2):
                for c in range(8):
                    wa, wbb, wun, wum = wE[c % 2]
                    wk = 'wE%d' % (c % 2)
                    dma_cast(wk, wa, win_d[:, 2840 + c * 128:2840 + (c + 1) * 128].rearrange("(c p) n -> p c n", p=128), writes=[wk])
                    dma_cast(wk, wbb, win_d[:, 3864 + c * 128:3864 + (c + 1) * 128].rearrange("(c p) n -> p c n", p=128), writes=[wk])
                    dma_cast(wk, wun, wupn_d[:, c * 128:(c + 1) * 128].rearrange("(c p) n -> p c n", p=128), writes=[wk])
                    dma_cast(wk, wum, wupm_d[:, c * 128:(c + 1) * 128].rearrange("(c p) n -> p c n", p=128), writes=[wk])
                    for tg in range(2):
                        t0 = half * 1024 + tg * 512
                        cs = slice(t0, t0 + 512)
                        hk = ['hT%d' % t for t in range(t0 // 128, t0 // 128 + 4)]
                        ynk = ['yTn%d' % t for t in range(t0 // 128, t0 // 128 + 4)]
                        ymk = ['yTm%d' % t for t in range(t0 // 128, t0 // 128 + 4)]
                        for kc in range(8):
                            mm(bank(0), wa[:, kc, :], hT[:, kc, cs], kc == 0, kc == 7, hk + [wk], ['b0'])
                        for kc in range(8):
                            mm(bank(1), wbb[:, kc, :], hT[:, kc, cs], kc == 0, kc == 7, hk + [wk], ['b1'])
                        for kc in range(4):
                            mm(bank(3), wun[:, kc, :], yTn[:, kc, cs], kc == 0, kc == 3, ynk + [wk], ['b3'])
                        for kc in range(4):
                            mm(bank(4), wum[:, kc, :], yTm[:, kc, cs], kc == 0, kc == 3, ymk + [wk], ['b4'])
                        act(sga, bank(0), AF.Sigmoid, ['b0'], ['sga'])
                        act(sgb, bank(1), AF.Sigmoid, ['b1'], ['sgb'])
                        tt('dve', m1, bank(3), sga, ALU.mult, ['b3', 'sga'], ['m1'])
                        tt('dve', m2, bank(4), sgb, ALU.mult, ['b4', 'sgb'], ['m2'])
                        tt('dve', mT[:, c, tg * 512:(tg + 1) * 512], m1, m2, ALU.add, ['m1', 'm2'], ['mT'])
                for ch in range(2):
                    dma_cast('wo', wo, wout_d[:, ch * 512:(ch + 1) * 512].rearrange("(c p) n -> p c n", p=128), writes=['wo'])
                    for tl in range(8):
                        t = half * 8 + tl
                        xk = 'xst%d' % (tl % 2)
                        dma_sp(xk, xst[tl % 2], x_d[b, t * 128:(t + 1) * 128, ch * 512:(ch + 1) * 512], writes=[xk])
                        pj = next_pj()
                        pk = 'b%d' % pj
                        for kc in range(8):
                            mm(bank(pj), mT[:, kc, tl * 128:(tl + 1) * 128], wo[:, kc, :], kc == 0, kc == 7, ['mT', 'wo'], [pk])
                        tt('dve', xres[:, t, ch * 512:(ch + 1) * 512], bank(pj), xst[tl % 2], ALU.add, [pk, xk], ['xres%d' % t])

            checkpoint('E', [('xres', xres)])
            AR.release(XE)
            Sx.barrier()
            xn_sqs = [AR.alloc(128, [1024], F32) for _ in range(2)]
            xn_bs = [AR.alloc(128, [1024], BF16) for _ in range(2)]
            xn_b = None
            wfo = AR.alloc(128, [NFC, 512], BF16)
            wgu = [(AR.alloc(128, [8, 128], BF16), AR.alloc(128, [8, 128], BF16)) for _ in range(2)]
            sgt = AR.alloc(128, [512], BF16)
            ost = [AR.alloc(128, [512], F32) for _ in range(2)]
            pst = AR.alloc(128, [256], F32)
            pstb = AR.alloc(128, [256], BF16)
            sgp = AR.alloc(128, [512], F32)
            sv = AR.mark()
            AR.release(PERSIST)
            h2T = AR.alloc(128, [8, 1024], BF16)
            actT = AR.alloc(128, [NFC, 1024], BF16)
            pT = AR.alloc(128, [2, 1024], BF16)
            assert AR.off <= ATT, (AR.off, ATT)
            AR.off = sv
            wpg = wfo[:, 0:16, :].rearrange("p (k c) n -> p k (c n)", k=8)
            wpp = AR.alloc(128, [2, 1024], BF16)

            for half in range(2):
                for tl in range(8):
                    t = half * 8 + tl
                    norm_transpose(xres[:, t, :], gvec["ffn"], h2T[:, :, tl * 128:(tl + 1) * 128], 'xres%d' % t, 'h2T', xn_b, 2)
                for fc in range(NFC):
                    wg_, wu_ = wgu[fc % 2]
                    wk = 'wgu%d' % (fc % 2)
                    dma_cast(wk, wg_, wfi_d[:, fc * 128:(fc + 1) * 128].rearrange("(c p) n -> p c n", p=128), writes=[wk])
                    dma_cast(wk, wu_, wfi_d[:, DFF + fc * 128:DFF + (fc + 1) * 128].rearrange("(c p) n -> p c n", p=128), writes=[wk])
                    for tg in range(2):
                        cs = slice(tg * 512, (tg + 1) * 512)
                        bg, bu = (0, 1) if tg == 0 else (3, 4)
                        for kc in range(8):
                            mm(bank(bg), wg_[:, kc, :], h2T[:, kc, cs], kc == 0, kc == 7, ['h2T', wk], ['b%d' % bg])
                        for kc in range(8):
                            mm(bank(bu), wu_[:, kc, :], h2T[:, kc, cs], kc == 0, kc == 7, ['h2T', wk], ['b%d' % bu])
                        act(sgt, bank(bg), AF.Silu, ['b%d' % bg], ['sgt'])
                        tt('dve', actT[:, fc, cs], bank(bu), sgt, ALU.mult, ['b%d' % bu, 'sgt'], ['actT'])
                for ch in range(2):
                    dma_cast('wfo', wfo, wfo_d[:, ch * 512:(ch + 1) * 512].rearrange("(c p) n -> p c n", p=128), writes=['wfo'])
                    for tl in range(8):
                        t = half * 8 + tl
                        pj = 5 + (tl % 2)
                        pk = 'b%d' % pj
                        for fc in range(NFC):
                            mm(bank(pj), actT[:, fc, tl * 128:(tl + 1) * 128], wfo[:, fc, :], fc == 0, fc == NFC - 1, ['actT', 'wfo'], [pk])
                        xr = xres[:, t, ch * 512:(ch + 1) * 512]
                        tt('dve', xr, bank(pj), xr, ALU.add, [pk, 'xres%d' % t], ['xres%d' % t])
                for tl in range(8):
                    t = half * 8 + tl
                    norm_transpose(xres[:, t, :], gvec["ple"], h2T[:, :, tl * 128:(tl + 1) * 128], 'xres%d' % t, 'h2T', xn_b, 2)
                    dma_sp('pst', pst, p_d[b, t * 128:(t + 1) * 128, :], writes=['pst'])
                    cp('dve', pstb, pst, ['pst'], ['pstb'])
                    for c in range(2):
                        tr(bank_bf(7)[:, c * 128:(c + 1) * 128], pstb[:, c * 128:(c + 1) * 128], identb, ['pstb', 'identb'], ['b7'])
                    cp('act', pT[:, :, tl * 128:(tl + 1) * 128], bank_bf(7)[:, 0:256].rearrange("p (a b) -> p a b", a=2), ['b7'], ['pT'])
                dma_cast('wfo', wpg, wpg_d.rearrange("(c p) n -> p c n", p=128), writes=['wfo'])
                dma_cast('wpp', wpp, wpp_d.rearrange("(c p) n -> p c n", p=128), writes=['wpp'])
                for tl in range(8):
                    t = half * 8 + tl
                    tls = slice(tl * 128, (tl + 1) * 128)
                    for ch in range(2):
                        ccs = slice(ch * 512, (ch + 1) * 512)
                        bg, bp = (0, 1) if ch == 0 else (3, 4)
                        for kc in range(8):
                            mm(bank(bg), h2T[:, kc, tls], wpg[:, kc, ccs], kc == 0, kc == 7, ['h2T', 'wfo'], ['b%d' % bg])
                        for kc in range(2):
                            mm(bank(bp), pT[:, kc, tls], wpp[:, kc, ccs], kc == 0, kc == 1, ['pT', 'wpp'], ['b%d' % bp])
                        act(sgp, bank(bg), AF.Sigmoid, ['b%d' % bg], ['sgp'])
                        ob = ost[ch]
                        ok = 'ost%d' % ch
                        tt('dve', ob, bank(bp), sgp, ALU.mult, ['b%d' % bp, 'sgp'], [ok])
                        tt('dve', ob, ob, xres[:, t, ccs], ALU.add, [ok, 'xres%d' % t], [ok])
                        dma_sp('out%d' % ch, out_d[b, t * 128:(t + 1) * 128, ccs], ob, reads=[ok], writes=['outd'])
        except _Stop:
            pass
        Sx.barrier()

        keys = Sx.sem_keys()
        sems = {k: es.enter_context(nc.semaphore(k.replace(':', '_'))) for k in keys}
        with nc.Block() as block:
            run = Sx.runner(sems)
            block.sync(run('sp'))
            block.tensor(run('pe'))
            block.scalar(run('act'))
            block.vector(run('dve'))
            block.gpsimd(run('pool'))
    build_program.last_dbg = dbg_outs
    return nc


def _consts():
    c = {}
    c["c_ident"] = np.eye(128, dtype=np.float32)
    k = np.arange(128)[:, None]
    t = np.arange(128)[None, :]
    c["c_tri"] = (k <= t).astype(np.float32)
    c["c_anti"] = (k > t).astype(np.float32)
    c["c_trib4"] = np.tile(np.where(k <= t, 0.0, NEG).astype(np.float32), (1, 4))
    c["c_antib4"] = np.tile(np.where(k > t, 0.0, NEG).astype(np.float32), (1, 4))
    cc = np.arange(127)[:, None]
    tt_ = np.arange(S)[None, :]
    c["c_cmask"] = ((16 * cc + 31) <= tt_).astype(np.float32)
    tl = np.arange(128)[:, None, None]
    qt = np.arange(NT)[None, :, None]
    j = np.arange(32)[None, None, :]
    cur = (qt * 128 + tl) // 64
    forced = (j == 0) | (j == cur) | (j == cur - 1)
    fb = np.where(forced, 1e9, np.where(j > cur, -1e9, 0.0)).astype(np.float32)
    c["c_fbs"] = np.ascontiguousarray(fb.reshape(128, NT * 32))
    n = np.arange(8)[None, None, :]
    own = (qt // 2)
    fm = np.where(n >= own, -1e9, 0.0).astype(np.float32) + np.zeros((128, 1, 1), np.float32)
    c["c_fbm"] = np.ascontiguousarray(fm.reshape(128, NT * 8))
    cs = np.arange(127) * 16
    ce = cs + 31
    jj = np.arange(32)
    ov = ((cs[:, None] <= jj[None, :] * 64 + 63) & (ce[:, None] >= jj[None, :] * 64)).astype(np.float32)
    c["c_ovl"] = np.concatenate([ov, np.ones((127, 1), np.float32)], axis=1)
    half = 8
    c["c_invf"] = (500000.0 ** (-np.arange(half, dtype=np.float32) / half)).astype(np.float32).reshape(1, 8)
    c["c_emat"] = (np.arange(S)[None, :] // 64 == np.arange(32)[:, None]).astype(np.float32)
    return c


_NC_CACHE = {}


def make_in_maps(inputs, n_cores=8):
    consts = _consts()
    f = lambda a: np.ascontiguousarray(np.asarray(a))
    shared = dict(consts)
    shared["g_mix"] = f(inputs["g_mix"][0]).reshape(8, 128)
    shared["g_ffn"] = f(inputs["g_ffn"][0]).reshape(8, 128)
    shared["g_ple"] = f(inputs["g_ple"][0]).reshape(8, 128)
    for n in ("nsa_q_gain", "nsa_kc_gain", "nsa_ks_gain", "nsa_kw_gain", "moba_q_gain", "moba_k_gain"):
        shared[n] = f(inputs[n][0]).reshape(1, 64)
    for n in ("w_in", "nsa_pe_k", "nsa_pe_v", "nsa_ck_w1", "nsa_ck_w2", "nsa_cv_w1", "nsa_cv_w2", "w_up_nsa", "w_up_moba",
              "w_out", "w_ffn_in", "w_ffn_out", "w_ple_gate", "w_ple_proj"):
        shared[n] = f(inputs[n][0])
    x = np.asarray(inputs["x"])
    p = np.asarray(inputs["p"])[0]
    pos = np.asarray(inputs["positions"]).astype(np.int32)
    in_maps = []
    for c in range(n_cores):
        m = dict(shared)
        m["x"] = f(x[c * NB:(c + 1) * NB])
        m["p"] = f(p[c * NB:(c + 1) * NB])
        m["pos"] = f(pos[c * NB:(c + 1) * NB]).reshape(NB, NT, 128)
        in_maps.append(m)
    return in_maps


def kernel(**inputs):
    n_cores = 8
    if "nc" not in _NC_CACHE:
        _NC_CACHE["nc"] = build_program()
    nc = _NC_CACHE["nc"]
    in_maps = make_in_maps(inputs, n_cores)
    res = run_bass_kernel_spmd(nc, in_maps, core_ids=list(range(n_cores)))
    out = np.concatenate([np.asarray(r["out"]) for r in res.results], axis=0)
    return out.astype(np.float32)
```

```python
import math
from contextlib import ExitStack
import numpy as np
import concourse.bass as bass
import concourse.mybir as mybir
from concourse.bass_utils import run_bass_kernel_spmd

F32 = mybir.dt.float32
BF16 = mybir.dt.bfloat16
I32 = mybir.dt.int32
AF = mybir.ActivationFunctionType
ALU = mybir.AluOpType
AX = mybir.AxisListType

NB = 2
S = 2048
D = 1024
NT = S // 128
DFF = 2816
NFC = DFF // 128
EPS = 1e-6
NEG = -30000.0
PI = math.pi


class Sched:
    ENG = ('pe', 'act', 'dve', 'pool', 'sp')

    def __init__(self):
        self.ops = {e: [] for e in self.ENG}
        self.cnt = {e: 0 for e in self.ENG}
        self.seen = {e: {} for e in self.ENG}
        self.res = {}
        self.dma_cnt = {}

    def _deps(self, eng, reads, writes):
        deps = {}

        def add(tok):
            if tok is None:
                return
            k, v = tok
            if deps.get(k, 0) < v:
                deps[k] = v
        for key in reads:
            r = self.res.get(key)
            if r is not None:
                add(r[0])
        for key in writes:
            r = self.res.get(key)
            if r is not None:
                add(r[0])
                for k, v in r[1].items():
                    add((k, v))
        out = []
        for k, v in deps.items():
            if eng == 'pe' and k == 'pe':
                continue
            if self.seen[eng].get(k, 0) >= v:
                continue
            self.seen[eng][k] = v
            out.append((k, v))
        return out

    def _commit(self, tok, reads, writes):
        k, v = tok
        for key in reads:
            r = self.res.setdefault(key, [None, {}])
            if r[1].get(k, 0) < v:
                r[1][k] = v
        for key in writes:
            self.res[key] = [tok, {}]

    def op(self, eng, fn, reads=(), writes=()):
        waits = self._deps(eng, reads, writes)
        self.cnt[eng] += 1
        tok = (eng, self.cnt[eng])
        self.ops[eng].append((waits, fn, (eng, 1)))
        self._commit(tok, reads, writes)
        return tok

    def dma(self, eng, stream, fn, reads=(), writes=()):
        waits = self._deps(eng, reads, writes)
        self.dma_cnt[stream] = self.dma_cnt.get(stream, 0) + 16
        tok = ('dma:' + stream, self.dma_cnt[stream])
        self.ops[eng].append((waits, fn, ('dma:' + stream, 16)))
        self._commit(tok, reads, writes)
        return tok

    def barrier(self):
        toks = [(e, self.cnt[e]) for e in self.ENG if self.cnt[e] > 0]
        toks += [('dma:' + s, v) for s, v in self.dma_cnt.items()]
        for e in self.ENG:
            waits = []
            for k, v in toks:
                if k == e and e == 'pe':
                    continue
                if self.seen[e].get(k, 0) >= v:
                    continue
                self.seen[e][k] = v
                waits.append((k, v))
            if waits:
                self.ops[e].append((waits, None, None))
        self.res = {}

    def sem_keys(self):
        ks = set(self.ENG)
        ks.update('dma:' + s for s in self.dma_cnt)
        return sorted(ks)

    def runner(self, sems):
        def run(eng_name):
            def body(engine):
                for waits, fn, inc in self.ops[eng_name]:
                    for k, v in waits:
                        engine.wait_ge(sems[k], v)
                    if fn is not None:
                        ins = fn(engine)
                        ins.then_inc(sems[inc[0]], inc[1])
            return body
        return run


class Arena:
    def __init__(self, t, nbytes):
        self.t = t
        self.off = 0
        self.nbytes = nbytes
        self.peak = 0

    def alloc(self, parts, shape, dtype):
        n = 1
        for s in shape:
            n *= s
        esz = 2 if dtype == BF16 else 4
        nb = (n * esz + 3) // 4 * 4
        assert self.off + nb <= self.nbytes, ("arena overflow", self.off, nb, self.nbytes)
        ap = self.t[0:parts, self.off // 2:(self.off + n * esz) // 2]
        self.off += nb
        self.peak = max(self.peak, self.off)
        if dtype != BF16:
            ap = ap.bitcast(dtype)
        if len(shape) == 2:
            ap = ap.rearrange("p (a b) -> p a b", a=shape[0])
        elif len(shape) == 3:
            ap = ap.rearrange("p (a b c) -> p a b c", a=shape[0], b=shape[1])
        return ap

    def mark(self):
        return self.off

    def release(self, m):
        self.off = m


def bc(ap, shape):
    return ap.broadcast_to(list(shape))


class _Stop(Exception):
    pass


def build_program(stop=None, nb=NB, qts=None):
    nc = bass.Bass("TRN2", target_bir_lowering=False)
    dbg_outs = {}

    def din(name, shape, dt=F32):
        return nc.dram_tensor(name, list(shape), dt, kind="ExternalInput").ap()

    x_d = din("x", [NB, S, D])
    p_d = din("p", [NB, S, 256])
    pos_d = din("pos", [NB, NT, 128], I32)
    gmix_d = din("g_mix", [8, 128])
    gffn_d = din("g_ffn", [8, 128])
    gple_d = din("g_ple", [8, 128])
    win_d = din("w_in", [D, 4888])
    gains_d = {n: din(n, [1, 64]) for n in ("nsa_q_gain", "nsa_kc_gain", "nsa_ks_gain", "nsa_kw_gain",
                                            "moba_q_gain", "moba_k_gain")}
    pek_d = din("nsa_pe_k", [32, 64])
    pev_d = din("nsa_pe_v", [32, 64])
    ckw1_d = din("nsa_ck_w1", [2048, 256])
    ckw2_d = din("nsa_ck_w2", [256, 64])
    cvw1_d = din("nsa_cv_w1", [2048, 256])
    cvw2_d = din("nsa_cv_w2", [256, 64])
    wupn_d = din("w_up_nsa", [512, D])
    wupm_d = din("w_up_moba", [512, D])
    wout_d = din("w_out", [D, D])
    wfi_d = din("w_ffn_in", [D, 2 * DFF])
    wfo_d = din("w_ffn_out", [DFF, D])
    wpg_d = din("w_ple_gate", [D, D])
    wpp_d = din("w_ple_proj", [256, D])
    ident_d = din("c_ident", [128, 128])
    tri_d = din("c_tri", [128, 128])
    anti_d = din("c_anti", [128, 128])
    cmask_d = din("c_cmask", [127, S])
    fbs_d = din("c_fbs", [128, NT * 32])
    fbm_d = din("c_fbm", [128, NT * 8])
    ovl_d = din("c_ovl", [127, 33])
    invf_d = din("c_invf", [1, 8])
    emat_d = din("c_emat", [32, S])
    trib4_d = din("c_trib4", [128, 512])
    antib4_d = din("c_antib4", [128, 512])
    out_d = nc.dram_tensor("out", [NB, S, D], F32, kind="ExternalOutput").ap()

    Sx = Sched()
    ARENA_BYTES = 207 * 1024 + 512
    with ExitStack() as es:
        arena_t = es.enter_context(nc.sbuf_tensor("arena", [128, ARENA_BYTES // 2], BF16))
        psum_t = es.enter_context(nc.psum_tensor("psum", [128, 8, 512], F32))
        AR = Arena(arena_t, ARENA_BYTES)

        def bank(i):
            return psum_t[:, i, :]

        def bank_bf(i):
            return psum_t[:, i, :].bitcast(BF16)

        def dma_sp(stream, out, in_, reads=(), writes=(), slow=False):
            if slow:
                return Sx.dma('sp', stream, lambda e: e.dma_start(out=out, in_=in_, allow_slow_non_contiguous=True), reads, writes)
            return Sx.dma('sp', stream, lambda e: e.dma_start(out=out, in_=in_), reads, writes)

        def dma_cast(stream, out, in_, reads=(), writes=()):
            return Sx.dma('pool', stream, lambda e: e.dma_start(out=out, in_=in_), reads, writes)

        def mm(out, lhsT, rhs, start, stop, reads, writes, skip=False):
            return Sx.op('pe', lambda e: e.matmul(out, lhsT=lhsT, rhs=rhs, start=start, stop=stop,
                                                  skip_group_check=skip), reads, writes)

        def tr(out, in_, ident, reads, writes):
            return Sx.op('pe', lambda e: e.transpose(out=out, in_=in_, identity=ident), reads, writes)

        def act(out, in_, func, reads, writes, scale=None, bias=None):
            kw = {}
            if scale is not None:
                kw['scale'] = scale
            if bias is not None:
                kw['bias'] = bias
            return Sx.op('act', lambda e: e.activation(out=out, in_=in_, func=func, **kw), reads, writes)

        def tt(eng, out, in0, in1, op, reads, writes):
            return Sx.op(eng, lambda e: e.tensor_tensor(out=out, in0=in0, in1=in1, op=op), reads, writes)

        def ts(eng, out, in0, s1, s2, op0, op1, reads, writes):
            if op1 is None:
                return Sx.op(eng, lambda e: e.tensor_scalar(out=out, in0=in0, scalar1=s1, scalar2=None, op0=op0),
                             reads, writes)
            return Sx.op(eng, lambda e: e.tensor_scalar(out=out, in0=in0, scalar1=s1, scalar2=s2, op0=op0, op1=op1),
                         reads, writes)

        def stt(out, in0, scalar, in1, op0, op1, reads, writes):
            return Sx.op('dve', lambda e: e.scalar_tensor_tensor(out=out, in0=in0, scalar=scalar, in1=in1,
                                                                 op0=op0, op1=op1), reads, writes)

        def cp(eng, out, in_, reads, writes):
            if eng == 'act':
                return Sx.op(eng, lambda e: e.copy(out=out, in_=in_), reads, writes)
            return Sx.op(eng, lambda e: e.tensor_copy(out=out, in_=in_), reads, writes)

        def recip(out, in_, reads, writes):
            return Sx.op('dve', lambda e: e.reciprocal(out=out, in_=in_), reads, writes)

        def memset(eng, ap, val, writes):
            return Sx.op(eng, lambda e: e.memset(ap, val), (), writes)

        def checkpoint(name, dumps):
            if stop != name:
                return
            Sx.barrier()
            for label, ap in dumps:
                shp = list(ap.shape)
                dt_ = ap.dtype
                d = nc.dram_tensor("dbg_" + label, shp, dt_, kind="ExternalOutput").ap()
                dbg_outs[label] = d
                Sx.dma('sp', 'dbg_' + label, lambda e, d=d, ap=ap: e.dma_start(out=d, in_=ap), (), ())
            raise _Stop()

        identf = AR.alloc(128, [128], F32)
        identb = AR.alloc(128, [128], BF16)
        trib = AR.alloc(128, [128], BF16)
        antib = AR.alloc(128, [128], BF16)
        trib4 = AR.alloc(128, [512], BF16)
        antib4 = AR.alloc(128, [512], BF16)
        cmask = AR.alloc(127, [NT, 128], BF16)
        fbs = AR.alloc(128, [NT, 32], F32)
        fbm = AR.alloc(128, [NT, 8], F32)
        invf = AR.alloc(128, [8], F32)
        gtile = {n: AR.alloc(128, [64], F32) for n in gains_d}
        gq8 = AR.alloc(128, [64], F32)
        gm8 = AR.alloc(128, [64], F32)
        gvec = {n: AR.alloc(128, [8], F32) for n in ("mix", "ffn", "ple")}
        gstage = AR.alloc(8, [128], F32)
        pestage = AR.alloc(32, [2, 64], F32)
        pestb = AR.alloc(32, [2, 64], BF16)
        peT = AR.alloc(64, [2, 32], BF16)
        cos_t = AR.alloc(128, [NT, 8], F32)
        sin_t = AR.alloc(128, [NT, 8], F32)
        cosc = AR.alloc(127, [8], F32)
        sinc = AR.alloc(127, [8], F32)
        n_ss2 = AR.alloc(128, [16], F32)
        n_rs2 = AR.alloc(128, [16], F32)
        x_ss2 = AR.alloc(128, [2], F32)
        x_rs2 = AR.alloc(128, [2], F32)

        dma_sp('c0', identf, ident_d, writes=['identf'])
        cp('dve', identb, identf, ['identf'], ['identb'])
        dma_cast('c1', trib, tri_d, writes=['trib'])
        dma_cast('c2', antib, anti_d, writes=['antib'])
        dma_cast('c2b', trib4, trib4_d, writes=['trib4'])
        dma_cast('c2c', antib4, antib4_d, writes=['antib4'])
        dma_cast('c3', cmask, cmask_d.rearrange("p (a b) -> p a b", a=NT), writes=['cmask'])
        dma_sp('c4', fbs, fbs_d.rearrange("p (a b) -> p a b", a=NT), writes=['fbs'])
        dma_sp('c5', fbm, fbm_d.rearrange("p (a b) -> p a b", a=NT), writes=['fbm'])
        dma_sp('c6', invf, invf_d.partition_broadcast(128), writes=['invf'])
        for i, n in enumerate(gains_d):
            dma_sp('c7_%d' % i, gtile[n], gains_d[n].partition_broadcast(128), writes=['g_' + n])
        ts('dve', gq8, gtile["nsa_q_gain"], 0.125, None, ALU.mult, None, ['g_nsa_q_gain'], ['gq8'])
        ts('dve', gm8, gtile["moba_q_gain"], 0.125, None, ALU.mult, None, ['g_moba_q_gain'], ['gm8'])
        for n, dd in (("mix", gmix_d), ("ffn", gffn_d), ("ple", gple_d)):
            dma_sp('c8', gstage, dd, writes=['gstage'])
            tr(bank(7)[:, 0:8], gstage, identf[0:8, 0:8], ['gstage', 'identf'], ['b7'])
            cp('dve', gvec[n], bank(7)[:, 0:8], ['b7'], ['gvec' + n])
        dma_sp('c9', pestage[:, 0, :], pek_d, writes=['pestage'])
        dma_sp('c9', pestage[:, 1, :], pev_d, writes=['pestage'])
        cp('dve', pestb, pestage, ['pestage'], ['pestb'])
        for kv in range(2):
            tr(bank_bf(7)[0:64, kv * 32:(kv + 1) * 32], pestb[:, kv, :], identb[0:32, 0:32], ['pestb', 'identb'], ['b7'])
        cp('dve', peT, bank_bf(7)[0:64, 0:64].rearrange("p (a b) -> p a b", a=2), ['b7'], ['peT'])

        PERSIST = AR.mark()

        ntc = [0]

        def norm_transpose(src, gain, dst, rk, wk, xn, bk):
            ntc[0] ^= 1
            i = ntc[0]
            sq = xn_sqs[i]
            xnb = xn_bs[i]
            ss = x_ss2[:, i:i + 1]
            rs = x_rs2[:, i:i + 1]
            bk = 2 if i == 0 else 7
            sk, xk, ssk, rsk = 'xn_sq%d' % i, 'xn%d' % i, 'x_ss%d' % i, 'x_rs%d' % i
            tt('pool', sq, src, src, ALU.mult, [rk], [sk])
            Sx.op('dve', lambda e: e.tensor_reduce(out=ss, in_=sq, axis=AX.X, op=ALU.add), [sk], [ssk])
            act(rs, ss, AF.Ln, [ssk], [rsk], scale=1.0 / D, bias=EPS)
            act(rs, rs, AF.Exp, [rsk], [rsk], scale=-0.5)
            act(xnb, src, AF.Copy, [rk, rsk], [xk], scale=rs)
            pb = bank_bf(bk).rearrange("p (a b) -> p a b", a=8)
            for c in range(8):
                tr(pb[:, c, :], xnb[:, c * 128:(c + 1) * 128], identb, [xk, 'identb'], ['b%d' % bk])
            tt('dve', dst, pb, bc(gain.unsqueeze(2), [128, 8, 128]), ALU.mult, ['b%d' % bk], [wk])

        def head_norm_rope_gen(src, H, rows, gain3, cosb, sinb, out_bf, rk, wk, extra_reads=(), sfx=0):
            R = rows
            kr, kq, kss, krs = 'hn_raw%d' % sfx, 'hn_sq%d' % sfx, 'n_ss%d' % sfx, 'n_rs%d' % sfx
            qraw = hn_raws[sfx][0:R, 0:H * 64]
            sq = hn_sqs[sfx][0:R, 0:H * 64]
            hr = hn_rs[sfx]
            nss = n_ss2[0:R, sfx * 8:sfx * 8 + H]
            nrs = n_rs2[0:R, sfx * 8:sfx * 8 + H]
            q3 = qraw.rearrange("p (h d) -> p h d", h=H)
            act(qraw, src, AF.Copy, [rk], [kr]); yield
            tt('pool', sq, qraw, qraw, ALU.mult, [kr], [kq]); yield
            Sx.op('dve', lambda e: e.tensor_reduce(out=nss, in_=sq.rearrange("p (h d) -> p h d", h=H),
                                                   axis=AX.X, op=ALU.add), [kq], [kss]); yield
            act(nrs, nss, AF.Ln, [kss], [krs], scale=1.0 / 64, bias=EPS); yield
            act(nrs, nrs, AF.Exp, [krs], [krs], scale=-0.5); yield
            tt('dve', q3, q3, bc(nrs.unsqueeze(2), [R, H, 64]), ALU.mult, [kr, krs], [kr]); yield
            tt('dve', q3, q3, gain3, ALU.mult, [kr] + list(extra_reads), [kr]); yield
            cp('pool', out_bf[:, :, 16:64], q3[:, :, 16:64], [kr], [wk]); yield
            x1 = q3[:, :, 0:8]
            x2 = q3[:, :, 8:16]
            c3 = bc(cosb.unsqueeze(1), [R, H, 8])
            s3 = bc(sinb.unsqueeze(1), [R, H, 8])
            ra = hr[0:R, 0, 0:H, :]
            rb = hr[0:R, 1, 0:H, :]
            rc = hr[0:R, 2, 0:H, :]
            rd = hr[0:R, 3, 0:H, :]
            ks_ = ['hn_r%s%d' % (c, sfx) for c in 'abcd']
            tt('dve', ra, x1, c3, ALU.mult, [kr, 'rope'], [ks_[0]]); yield
            tt('dve', rb, x2, s3, ALU.mult, [kr, 'rope'], [ks_[1]]); yield
            tt('dve', rc, x2, c3, ALU.mult, [kr, 'rope'], [ks_[2]]); yield
            tt('dve', rd, x1, s3, ALU.mult, [kr, 'rope'], [ks_[3]]); yield
            tt('dve', out_bf[:, :, 0:8], ra, rb, ALU.subtract, [ks_[0], ks_[1]], [wk]); yield
            tt('dve', out_bf[:, :, 8:16], rc, rd, ALU.add, [ks_[2], ks_[3]], [wk]); yield

        def head_norm_rope(src, H, rows, gain3, cosb, sinb, out_bf, rk, wk, extra_reads=(), sfx=0):
            for _ in head_norm_rope_gen(src, H, rows, gain3, cosb, sinb, out_bf, rk, wk, extra_reads, sfx):
                pass

        def run_interleaved(gens):
            gens = list(gens)
            while gens:
                for g_ in list(gens):
                    try:
                        next(g_)
                    except StopIteration:
                        gens.remove(g_)

        sq_eng = ['dve']
        pjc = [0]

        def next_pj():
            pjc[0] ^= 1
            return pjc[0]

        try:
          main_body = True
          for b in range(nb):
            AR.release(PERSIST)
            Sx.barrier()
            hT = AR.alloc(128, [8, S], BF16)
            yTn = AR.alloc(128, [4, S], BF16)
            yTm = AR.alloc(128, [4, S], BF16)
            ATT = AR.mark()
            ksA = AR.alloc(96, [2, S], BF16)
            kwT = AR.alloc(64, [2, S], BF16)
            kmT = AR.alloc(128, [4, S], BF16)
            VS = AR.alloc(128, [NT, 2, 65], BF16)
            VW = AR.alloc(128, [NT, 2, 65], BF16)
            VM = AR.alloc(128, [NT, 8, 65], BF16)
            kcT = AR.alloc(64, [2, 127], BF16)
            VC = AR.alloc(127, [2, 97], BF16)
            kmean = AR.alloc(128, [4, 8], BF16)
            kmean_f = AR.alloc(128, [4, 8], F32)
            kmean_z = AR.alloc(128, [8, 8], BF16)
            hn_raws = [AR.alloc(128, [512], F32)]
            hn_sqs = [AR.alloc(128, [512], F32)]
            hn_rs = [AR.alloc(128, [4, 8, 8], F32)]
            xn_sqs = [AR.alloc(128, [1024], F32)]
            xn_bs = [AR.alloc(128, [1024], BF16)]
            _al = AR.mark()
            xn_sqs.append(AR.alloc(128, [1024], F32))
            xn_bs.append(AR.alloc(128, [1024], BF16))
            _al_end = AR.mark()
            AR.release(_al)
            hn_raws.append(AR.alloc(128, [512], F32))
            hn_sqs.append(AR.alloc(128, [512], F32))
            hn_rs.append(AR.alloc(128, [4, 8, 8], F32))
            tm_b2 = AR.alloc(128, [512], BF16)
            assert AR.off <= _al_end
            AR.release(_al_end)
            xn_b = None
            BC = AR.mark()
            xs = [AR.alloc(128, [1024], F32) for _ in range(2)]
            wb = [AR.alloc(128, [8, 512], BF16)]
            ks_bs = [AR.alloc(128, [2, 64], BF16) for _ in range(2)]
            kvT = AR.alloc(64, [4, S], BF16)
            w1sb = AR.alloc(64, [32, 256], BF16)
            _sv = AR.mark()
            AR.release(_sv - 16 * 1024)
            wb.append(AR.alloc(128, [8, 512], BF16))
            AR.release(_sv)
            w2sb = AR.alloc(128, [2, 2, 64], BF16)
            hsb = AR.alloc(128, [2, 127], BF16)
            hb = AR.alloc(128, [2], F32)
            tm_b = AR.alloc(128, [512], BF16)
            tm_bs = [tm_b, tm_b2]
            posi = AR.alloc(16, [128], I32)
            posf16 = AR.alloc(16, [128], F32)
            posf = AR.alloc(128, [NT], F32)
            ang = AR.alloc(128, [NT, 8], F32)
            ang2 = AR.alloc(128, [NT, 8], F32)
            angi = AR.alloc(128, [NT, 8], I32)
            angn = AR.alloc(128, [NT, 8], F32)
            posci = AR.alloc(127, [1], I32)
            poscf = AR.alloc(127, [1], F32)

            def sincos(angv, outv, R, shape, shift):
                a2 = ang2[0:R] if len(shape) == 2 else ang2[0:R, 0, :]
                ai = angi[0:R] if len(shape) == 2 else angi[0:R, 0, :]
                an = angn[0:R] if len(shape) == 2 else angn[0:R, 0, :]
                ts('dve', a2, angv, shift, 1.0 / (2 * PI), ALU.add, ALU.mult, ['ang'], ['ang2'])
                cp('dve', ai, a2, ['ang2'], ['angi'])
                cp('dve', an, ai, ['angi'], ['angn'])
                ts('dve', an, an, -2 * PI, None, ALU.mult, None, ['angn'], ['angn'])
                tt('dve', a2, an, angv, ALU.add, ['angn', 'ang'], ['ang2'])
                ts('dve', a2, a2, shift, None, ALU.add, None, ['ang2'], ['ang2'])
                ts('dve', an, a2, PI, -2 * PI, ALU.is_gt, ALU.mult, ['ang2'], ['angn'])
                tt('dve', a2, a2, an, ALU.add, ['ang2', 'angn'], ['ang2'])
                ts('dve', an, a2, -PI, 2 * PI, ALU.is_lt, ALU.mult, ['ang2'], ['angn'])
                tt('dve', a2, a2, an, ALU.add, ['ang2', 'angn'], ['ang2'])
                ts('dve', a2, a2, PI, -PI, ALU.min, ALU.max, ['ang2'], ['ang2'])
                act(outv, a2, AF.Sin, ['ang2'], ['rope'])

            dma_sp('pos', posi, pos_d[b], writes=['posi'])
            cp('dve', posf16, posi, ['posi'], ['posf16'])
            tr(bank(7)[:, 0:16], posf16, identf[0:16, 0:16], ['posf16', 'identf'], ['b7'])
            cp('dve', posf, bank(7)[:, 0:16], ['b7'], ['posf'])
            tt('dve', ang, bc(posf.unsqueeze(2), [128, NT, 8]), bc(invf.unsqueeze(1), [128, NT, 8]), ALU.mult,
               ['posf', 'invf'], ['ang'])
            sincos(ang, sin_t, 128, [NT, 8], 0.0)
            sincos(ang, cos_t, 128, [NT, 8], PI / 2)
            posflat = pos_d[b].rearrange("a b -> (a b)")
            dma_sp('posc', posci, posflat[16:S].rearrange("(c s) -> c s", s=16)[:, 15:16], writes=['posci'], slow=True)
            cp('dve', poscf, posci, ['posci'], ['poscf'])
            angc = ang[0:127, 0, :]
            ts('dve', angc, invf[0:127, :], poscf[:, 0:1], None, ALU.mult, None, ['poscf', 'invf', 'rope'], ['ang'])
            sincos(angc, sinc, 127, [8], 0.0)
            sincos(angc, cosc, 127, [8], PI / 2)

            sq_eng[0] = 'pool'
            for t in range(NT):
                xb = xs[t % 2]
                dma_sp('xs%d' % (t % 2), xb, x_d[b, t * 128:(t + 1) * 128, :], writes=['xs%d' % (t % 2)])
                norm_transpose(xb, gvec["mix"], hT[:, :, t * 128:(t + 1) * 128], 'xs%d' % (t % 2), 'hT%d' % t, xn_b, 2)

            sq_eng[0] = 'dve'
            Sx.barrier()
            checkpoint('A', [('hT', hT), ('cos', cos_t), ('sin', sin_t), ('cosc', cosc), ('sinc', sinc)])
            memset('pool', VS[:, :, :, 64:65], 1.0, ['VS'])
            memset('pool', VW[:, :, :, 64:65], 1.0, ['VW'])
            memset('pool', VM[:, :, :, 64:65], 1.0, ['VM'])
            for g in range(2):
                dma_cast('emat', ksA[64:96, g, :], emat_d, writes=['ksA_e'])
                dma_cast('ovl', VC[:, g, 64:97], ovl_d, writes=['VC_o'])

            gks3 = bc(gtile["nsa_ks_gain"].unsqueeze(1), [128, 2, 64])
            gkw3 = bc(gtile["nsa_kw_gain"].unsqueeze(1), [128, 2, 64])
            gkm3 = bc(gtile["moba_k_gain"].unsqueeze(1), [128, 8, 64])
            chunks = [(512, 512, 'kcvcksvs'), (1024, 256, 'kwvw'), (1816, 512, 'km'), (2328, 512, 'vm')]
            hT_all = ['hT%d' % t for t in range(NT)]
            for ci, (c0, cw, kind) in enumerate(chunks):
                wbuf = wb[ci % 2]
                wk = 'wb0' if ci % 2 == 0 else 'w1sb'
                dma_cast(wk, wbuf[:, :, 0:cw], win_d[:, c0:c0 + cw].rearrange("(c p) n -> p c n", p=128), writes=[wk])

                PJB = [0, 1, 3, 4]

                def mmB(t, wbuf=wbuf, wk=wk, cw=cw):
                    pj = PJB[t % 4]
                    tsl = slice(t * 128, (t + 1) * 128)
                    for kc in range(8):
                        mm(bank(pj)[:, 0:cw], hT[:, kc, tsl], wbuf[:, kc, 0:cw], kc == 0, kc == 7, ['hT%d' % t, wk], ['b%d' % pj])

                def postB(t, kind=kind):
                    pj = PJB[t % 4]
                    par = t % 2
                    pk = 'b%d' % pj
                    tsl = slice(t * 128, (t + 1) * 128)
                    ptb = bank_bf(2)
                    tmb = tm_bs[par]
                    tk = 'tm_b%d' % par
                    ksb = ks_bs[par]
                    kk = 'ks_b%d' % par
                    if kind == 'kcvcksvs':
                        cp('act', tmb[:, 0:256], bank(pj)[:, 0:256], [pk], [tk]); yield
                        cp('act', VS[:, t, :, 0:64], bank(pj)[:, 384:512].rearrange("p (g d) -> p g d", g=2), [pk], ['VS']); yield
                        yield from head_norm_rope_gen(bank(pj)[:, 256:384], 2, 128, gks3, cos_t[:, t, :], sin_t[:, t, :], ksb, pk, kk,
                                                      ['g_nsa_ks_gain'], par)
                        for i in range(4):
                            tr(ptb[0:64, i * 128:(i + 1) * 128], tmb[:, i * 64:(i + 1) * 64], identb, [tk, 'identb'], ['b2'])
                        cp('act', kvT[:, :, tsl], ptb[0:64, 0:512].rearrange("p (a b) -> p a b", a=4), ['b2'], ['kvT']); yield
                        for g in range(2):
                            tr(ptb[0:64, g * 128:(g + 1) * 128], ksb[:, g, :], identb, [kk, 'identb'], ['b2'])
                        cp('act', ksA[0:64, :, tsl], ptb[0:64, 0:256].rearrange("p (a b) -> p a b", a=2), ['b2'], ['ksA']); yield
                    elif kind == 'kwvw':
                        cp('act', VW[:, t, :, 0:64], bank(pj)[:, 128:256].rearrange("p (g d) -> p g d", g=2), [pk], ['VW']); yield
                        kwb = tmb[:, 0:128].rearrange("p (h d) -> p h d", h=2)
                        yield from head_norm_rope_gen(bank(pj)[:, 0:128], 2, 128, gkw3, cos_t[:, t, :], sin_t[:, t, :], kwb, pk, tk,
                                                      ['g_nsa_kw_gain'], par)
                        for g in range(2):
                            tr(ptb[0:64, g * 128:(g + 1) * 128], tmb[:, g * 64:(g + 1) * 64], identb, [tk, 'identb'], ['b2'])
                        cp('act', kwT[0:64, :, tsl], ptb[0:64, 0:256].rearrange("p (a b) -> p a b", a=2), ['b2'], ['kwT']); yield
                    elif kind == 'km':
                        kmb = tmb.rearrange("p (h d) -> p h d", h=8)
                        yield from head_norm_rope_gen(bank(pj), 8, 128, gkm3, cos_t[:, t, :], sin_t[:, t, :], kmb, pk, tk,
                                                      ['g_moba_k_gain'], par)
                        for hp in range(4):
                            tr(ptb[:, hp * 128:(hp + 1) * 128], tmb[:, hp * 128:(hp + 1) * 128], identb, [tk, 'identb'], ['b2'])
                        cp('act', kmT[:, :, tsl], ptb[:, 0:512].rearrange("p (a b) -> p a b", a=4), ['b2'], ['kmT']); yield
                    else:
                        cp('act', VM[:, t, :, 0:64], bank(pj).rearrange("p (h d) -> p h d", h=8), [pk], ['VM']); yield

                mmB(0)
                mmB(1)
                for pr in range(NT // 2):
                    if pr + 1 < NT // 2:
                        mmB(2 * pr + 2)
                        mmB(2 * pr + 3)
                    run_interleaved([postB(2 * pr), postB(2 * pr + 1)])
            Sx.op('dve', lambda e: e.tensor_reduce(out=kmean_f, in_=kmT.rearrange("p h (n k) -> p h n k", k=256),
                                                   axis=AX.X, op=ALU.add), ['kmT'], ['kmean_f'])
            ts('dve', kmean, kmean_f, 1.0 / 256, None, ALU.mult, None, ['kmean_f'], ['kmean'])
            memset('dve', kmean_z, 0.0, ['kmean_z'])
            kz4 = kmean_z.rearrange("p (hp two) n -> p hp two n", two=2)
            cp('dve', kz4[0:64, :, 0, :], kmean[0:64, :, :], ['kmean', 'kmean_z'], ['kmean_z'])
            cp('dve', kz4[64:128, :, 1, :], kmean[64:128, :, :], ['kmean', 'kmean_z'], ['kmean_z'])

            checkpoint('B', [('ksA', ksA), ('kwT', kwT), ('kmT', kmT), ('VS', VS), ('VW', VW), ('VM', VM), ('kvT', kvT), ('kmean', kmean_f)])
            for kv, (w1_d, w2_d) in enumerate(((ckw1_d, ckw2_d), (cvw1_d, cvw2_d))):
                dma_cast('w1', w1sb, w1_d.rearrange("(i d) h -> d i h", d=64), writes=['w1sb'])
                dma_cast('w2', w2sb[:, kv, :, :], w2_d.rearrange("(hh p) d -> p hh d", p=128), writes=['w2sb'])
                for hh in range(2):
                    for i in range(32):
                        mm(bank(7)[:, hh:hh + 1], w1sb[:, i, hh * 128:(hh + 1) * 128], peT[:, kv, i:i + 1], i == 0, i == 31,
                           ['w1sb', 'peT'], ['b7'])
                cp('dve', hb, bank(7)[:, 0:2], ['b7'], ['hb'])
                for g in range(2):
                    src = kvT[:, kv * 2 + g, :].rearrange("p (c s) -> p c s", s=16)
                    for hh in range(2):
                        pj = next_pj()
                        pk = 'b%d' % pj
                        for i in range(32):
                            rhs = src[:, 0:127, i] if i < 16 else src[:, 1:128, i - 16]
                            mm(bank(pj)[:, 0:127], w1sb[:, i, hh * 128:(hh + 1) * 128], rhs, i == 0, i == 31, ['w1sb', 'kvT'], [pk])
                        act(hsb[:, hh, :], bank(pj)[:, 0:127], AF.Silu, [pk, 'hb'], ['hsb'], bias=hb[:, hh:hh + 1])
                    pj = next_pj()
                    pk = 'b%d' % pj
                    for hh in range(2):
                        mm(bank(pj)[0:127, 0:64], hsb[:, hh, :], w2sb[:, kv, hh, :], hh == 0, hh == 1, ['hsb', 'w2sb'], [pk])
                    if kv == 0:
                        kcb = tm_b[0:127, 0:64].rearrange("p (h d) -> p h d", h=1)
                        head_norm_rope(bank(pj)[0:127, 0:64], 1, 127, bc(gtile["nsa_kc_gain"][0:127].unsqueeze(1), [127, 1, 64]),
                                       cosc, sinc, kcb, pk, 'tm_b0', ['g_nsa_kc_gain'])
                        tr(bank_bf(2)[0:64, 0:127], tm_b[0:127, 0:64], identb[0:127, 0:127], ['tm_b0', 'identb'], ['b2'])
                        cp('act', kcT[:, g, :], bank_bf(2)[0:64, 0:127], ['b2'], ['kcT'])
                    else:
                        cp('act', VC[:, g, 0:64], bank(pj)[0:127, 0:64], [pk], ['VC'])

            checkpoint('C', [('kcT', kcT), ('VC', VC)])
            AR.release(BC)
            Sx.barrier()
            wq = AR.alloc(128, [8, 1048], BF16)
            QAs = [AR.alloc(96, [2, 512], BF16) for _ in range(2)]
            QMs = [AR.alloc(128, [4, 128], BF16) for _ in range(2)]
            qn_b = AR.alloc(128, [8, 64], BF16)
            qm_b = AR.alloc(128, [8, 64], BF16)
            sigs = [AR.alloc(128, [8, 3], F32) for _ in range(2)]
            Ec = AR.alloc(127, [512], BF16)
            Em = AR.alloc(127, [512], BF16)
            NPT = 6
            PT = [AR.alloc(128, [512], BF16) for _ in range(NPT)]
            rs4 = AR.alloc(128, [4], F32)
            coef = AR.alloc(128, [4], F32)
            impn = AR.alloc(128, [32], F32)
            top8 = AR.alloc(128, [8], F32)
            selb = AR.alloc(128, [32], F32)
            Bpads = [AR.alloc(128, [96], BF16) for _ in range(2)]
            rs4c = AR.alloc(128, [4], F32)
            coefc = AR.alloc(128, [4], F32)
            yaccs = [AR.alloc(128, [8, 64], F32) for _ in range(2)]
            ytmp = AR.alloc(128, [4, 64], F32)
            oacc = AR.alloc(128, [8, 65], F32)
            scm = AR.alloc(128, [8, 8], F32)
            topm = AR.alloc(128, [8, 8], F32)
            selm = AR.alloc(128, [8, 8], F32)
            rsm = AR.alloc(128, [8], F32)
            y_b = AR.alloc(128, [512], BF16)
            y_b2 = AR.alloc(128, [512], BF16)

            dma_cast('wq', wq[:, :, 0:512], win_d[:, 0:512].rearrange("(c p) n -> p c n", p=128), writes=['wq'])
            dma_cast('wq', wq[:, :, 512:536], win_d[:, 1280:1304].rearrange("(c p) n -> p c n", p=128), writes=['wq'])
            dma_cast('wq', wq[:, :, 536:1048], win_d[:, 1304:1816].rearrange("(c p) n -> p c n", p=128), writes=['wq'])
            memset('dve', Bpads[0], 0.0, ['Bpad0'])
            memset('dve', Bpads[1], 0.0, ['Bpad1'])
            gq3 = bc(gq8.unsqueeze(1), [128, 8, 64])
            gm3 = bc(gm8.unsqueeze(1), [128, 8, 64])
            ptc = [0]

            def next_pt():
                ptc[0] = (ptc[0] + 1) % NPT
                return PT[ptc[0]], 'PT%d' % ptc[0]

            psc = [0]

            PSB = [3, 4, 0, 1]

            def next_ps():
                psc[0] = (psc[0] + 1) % len(PSB)
                return PSB[psc[0]]

            ptb = bank_bf(2)

            qlist0 = list(range(NT) if qts is None else qts)
            PAR = {q: i % 2 for i, q in enumerate(qlist0)}

            def front_a(qt):
                tsl = slice(qt * 128, (qt + 1) * 128)
                hk = 'hT%d' % qt
                for kc in range(8):
                    mm(bank(0), hT[:, kc, tsl], wq[:, kc, 0:512], kc == 0, kc == 7, [hk, 'wq'], ['b0'])
                for kc in range(8):
                    mm(bank(1), hT[:, kc, tsl], wq[:, kc, 536:1048], kc == 0, kc == 7, [hk, 'wq'], ['b1'])
                for kc in range(8):
                    mm(bank(7)[:, 0:24], hT[:, kc, tsl], wq[:, kc, 512:536], kc == 0, kc == 7, [hk, 'wq'], ['b7'])

            def front_b1(qt):
                p = PAR[qt]
                sig2 = sigs[p].rearrange("p h c -> p (h c)")
                sk = 'sig%d' % p
                act(sig2, bank(7)[:, 0:24], AF.Exp, ['b7'], [sk], scale=-1.0)
                ts('dve', sig2, sig2, 1.0, None, ALU.add, None, [sk], [sk])
                recip(sig2, sig2, [sk], [sk])
                run_interleaved([
                    head_norm_rope_gen(bank(0), 8, 128, gq3, cos_t[:, qt, :], sin_t[:, qt, :], qn_b, 'b0', 'qn_b', ['gq8'], 0),
                    head_norm_rope_gen(bank(1), 8, 128, gm3, cos_t[:, qt, :], sin_t[:, qt, :], qm_b, 'b1', 'qm_b', ['gm8'], 1)])

            def front_b2(qt):
                p = PAR[qt]
                QA, QM = QAs[p], QMs[p]
                for h in range(8):
                    tr(ptb[0:64, h * 128:(h + 1) * 128], qn_b[:, h, :], identb, ['qn_b', 'identb'], ['b2'])
                cp('act', QA[0:64, :, :], ptb[0:64, :].rearrange("p (g n) -> p g n", g=2), ['b2'], ['QA%d' % p])
                qm2 = qm_b.rearrange("p h d -> p (h d)")
                for hp in range(4):
                    tr(ptb[:, hp * 128:(hp + 1) * 128], qm2[:, hp * 128:(hp + 1) * 128], identb, ['qm_b', 'identb'], ['b2'])
                cp('act', QM, ptb[:, 0:512].rearrange("p (a b) -> p a b", a=4), ['b2'], ['QM%d' % p])

            def cs1(qt, g):
                p = PAR[qt]
                QA = QAs[p]
                ps_i = next_ps()
                pk = 'b%d' % ps_i
                mm(bank(ps_i)[0:127, :], kcT[:, g, :], QA[0:64, g, :], True, True, ['kcT', 'QA%d' % p], [pk])
                act(Ec, bank(ps_i)[0:127, :], AF.Exp, [pk], ['Ec'])
                tt('pool', Em.rearrange("p (r t) -> p r t", r=4), Ec.rearrange("p (r t) -> p r t", r=4),
                   bc(cmask[:, qt, :].unsqueeze(1), [127, 4, 128]), ALU.mult, ['Ec', 'cmask'], ['Em'])

            def cs2(qt, g):
                p = PAR[qt]
                sig, yacc = sigs[p], yaccs[p]
                sk = 'sig%d' % p
                po = bank(7)[:, 0:388].rearrange("p (r c) -> p r c", r=4)
                for r in range(4):
                    mm(po[:, r, :], Em[:, r * 128:(r + 1) * 128], VC[:, g, :], True, True, ['Em', 'VC', 'VC_o'], ['b7'])
                ts('dve', rs4c, po[:, :, 96], 1e-30, None, ALU.max, None, ['b7'], ['rs4c'])
                recip(rs4c, rs4c, ['rs4c'], ['rs4c'])
                ts('dve', impn, po[:, 0, 64:96], rs4c[:, 0:1], None, ALU.mult, None, ['b7', 'rs4c'], ['impn'])
                for r in range(1, 4):
                    stt(impn, po[:, r, 64:96], rs4c[:, r:r + 1], impn, ALU.mult, ALU.add, ['b7', 'rs4c', 'impn'], ['impn'])
                tt('dve', coefc, rs4c, sig[:, 4 * g:4 * g + 4, 0], ALU.mult, ['rs4c', sk], ['coefc'])
                tt('dve', yacc[:, 4 * g:4 * g + 4, :], po[:, :, 0:64], bc(coefc.unsqueeze(2), [128, 4, 64]), ALU.mult,
                   ['b7', 'coefc'], ['yacc%d_%d' % (p, g)])
                tt('dve', impn, impn, fbs[:, qt, :], ALU.add, ['impn', 'fbs'], ['impn'])
                Sx.op('dve', lambda e: e.max(out=top8, in_=impn), ['impn'], ['top8'])
                ts('dve', selb, impn, top8[:, 7:8], -NEG, ALU.is_ge, ALU.mult, ['impn', 'top8'], ['selb'])
                ts('dve', Bpads[g][:, 64:96], selb, NEG, None, ALU.add, None, ['selb'], ['Bpad%d' % g])

            def cs3(qt, g):
                p = PAR[qt]
                QA = QAs[p]
                tr(ptb[0:96, 0:128], Bpads[g], identb, ['Bpad%d' % g, 'identb'], ['b2'])
                cp('act', QA[64:96, g, :].rearrange("p (r t) -> p r t", r=4),
                   bc(ptb[64:96, 0:128].unsqueeze(1), [32, 4, 128]), ['b2'], ['QAb%d' % p])

            def cmp_sel(qt):
                for g in range(2):
                    cs1(qt, g)
                    cs2(qt, g)
                    cs3(qt, g)

            def nsa_units(qt, g, kind):
                p = PAR[qt]
                QA, sig, yacc = QAs[p], sigs[p], yaccs[p]
                if kind == 'sel':
                    kts = [(kt, trib4 if kt == qt else None) for kt in range(qt + 1)]
                    K, kT, kkey, Vt, vkey, gate_idx, bk = 96, ksA, 'ksA', VS, 'VS', 1, 5 + g
                    qkeys = ['QA%d' % p, 'QAb%d' % p, 'ksA_e']
                else:
                    kts = []
                    for kt in range(max(0, qt - 4), qt + 1):
                        m = trib4 if kt == qt else (antib4 if kt == qt - 4 else None)
                        kts.append((kt, m))
                    K, kT, kkey, Vt, vkey, gate_idx, bk = 64, kwT, 'kwT', VW, 'VW', 2, 5
                    qkeys = ['QA%d' % p]
                po2 = bank(bk)[:, 0:260].rearrange("p (r c) -> p r c", r=4)
                pok = 'b%d' % bk
                units = []
                nk = len(kts)
                for j, (kt, mask) in enumerate(kts):
                    st = {}

                    def qk(j=j, kt=kt, mask=mask, st=st):
                        ps_i = next_ps()
                        pk = 'b%d' % ps_i
                        mm(bank(ps_i), kT[0:K, g, kt * 128:(kt + 1) * 128], QA[0:K, g, :], True, mask is None, [kkey] + qkeys, [pk])
                        if mask is not None:
                            mm(bank(ps_i), identb, mask, False, True, ['identb', 'trib4', 'antib4'], [pk])
                        pt, ptk = next_pt()
                        act(pt, bank(ps_i), AF.Exp, [pk], [ptk])
                        st['pt'] = (pt, ptk)

                    def pv(j=j, kt=kt, st=st):
                        pt, ptk = st['pt']
                        for r in range(4):
                            mm(po2[:, r, :], pt[:, r * 128:(r + 1) * 128], Vt[:, kt, g, :], (j == 0 and r == 0), j == nk - 1,
                               [ptk, vkey], [pok], skip=True)
                        if j == nk - 1:
                            ts('dve', rs4, po2[:, :, 64], 1e-30, None, ALU.max, None, [pok], ['rs4'])
                            recip(rs4, rs4, ['rs4'], ['rs4'])
                            tt('dve', coef, rs4, sig[:, 4 * g:4 * g + 4, gate_idx], ALU.mult, ['rs4', 'sig%d' % p], ['coef'])
                            tt('dve', ytmp, po2[:, :, 0:64], bc(coef.unsqueeze(2), [128, 4, 64]), ALU.mult, [pok, 'coef'], ['ytmp'])
                            yk = 'yacc%d_%d' % (p, g)
                            tt('dve', yacc[:, 4 * g:4 * g + 4, :], yacc[:, 4 * g:4 * g + 4, :], ytmp, ALU.add, ['ytmp', yk], [yk])
                    units.append((qk, pv))
                return units

            def moba_sel(qt):
                p = PAR[qt]
                QM = QMs[p]
                psel = bank(7)[:, 0:64].rearrange("p (h n) -> p h n", h=8)
                for h in range(8):
                    mm(psel[:, h, :], QM[:, h // 2, :], kmean_z[:, h, :], True, True, ['QM%d' % p, 'kmean_z'], ['b7'])
                tt('dve', scm, psel, bc(fbm[:, qt, :].unsqueeze(1), [128, 8, 8]), ALU.add, ['b7', 'fbm'], ['scm'])
                for h in range(8):
                    Sx.op('dve', lambda e, h=h: e.max(out=topm[:, h, :], in_=scm[:, h, :]), ['scm'], ['topm'])
                tt('dve', selm, scm, bc(topm[:, :, 2:3], [128, 8, 8]), ALU.is_ge, ['scm', 'topm'], ['selm'])

            mbc = [0]

            def moba_units(qt):
                p = PAR[qt]
                QM = QMs[p]
                own = qt // 2
                use_sel = own >= 4
                units = []
                for h in range(8):
                    hp, base = h // 2, 64 * (h % 2)
                    blocks = list(range(own + 1))
                    groups = [blocks[i:i + 2] for i in range(0, len(blocks), 2)]
                    hst = {}
                    for gi, grp in enumerate(groups):
                        tiles = []
                        for n in grp:
                            for kt in (2 * n, 2 * n + 1):
                                if kt <= qt:
                                    tiles.append((n, kt))
                        st = {}

                        def qk(hp=hp, base=base, tiles=tiles, st=st):
                            ps_i = next_ps()
                            pk = 'b%d' % ps_i
                            for j, (n, kt) in enumerate(tiles):
                                diag = (kt == qt)
                                mm(bank(ps_i)[:, j * 128:(j + 1) * 128], kmT[base:base + 64, hp, kt * 128:(kt + 1) * 128],
                                   QM[base:base + 64, hp, :], True, not diag, ['kmT', 'QM%d' % p], [pk], skip=True)
                                if diag:
                                    mm(bank(ps_i)[:, j * 128:(j + 1) * 128], identb, trib4[:, 0:128], False, True,
                                       ['identb', 'trib4'], [pk], skip=True)
                            pt, ptk = next_pt()
                            W = 128 * len(tiles)
                            act(pt[:, 0:W], bank(ps_i)[:, 0:W], AF.Exp, [pk], [ptk])
                            st['pt'] = (pt, ptk)

                        def pv(h=h, gi=gi, grp=grp, tiles=tiles, st=st, hst=hst, ngroups=len(groups)):
                            pt, ptk = st['pt']
                            ok = 'oacc%d' % h
                            if not use_sel:
                                if gi == 0:
                                    mbc[0] ^= 1
                                    hst['bk'] = 5 + mbc[0]
                                bk = hst['bk']
                                bkk = 'b%d' % bk
                                pm = bank(bk)[:, 0:65]
                                for j, (n, kt) in enumerate(tiles):
                                    mm(pm, pt[:, j * 128:(j + 1) * 128], VM[:, kt, h, :], (gi == 0 and j == 0),
                                       (gi == ngroups - 1 and j == len(tiles) - 1), [ptk, 'VM'], [bkk])
                                if gi == ngroups - 1:
                                    cp('dve', oacc[:, h, :], pm, [bkk], [ok])
                                return
                            mbc[0] ^= 1
                            bk = 5 + mbc[0]
                            bkk = 'b%d' % bk
                            for bi, n in enumerate(grp):
                                pm = bank(bk)[:, bi * 65:(bi + 1) * 65]
                                tl = [(j, kt) for j, (nn, kt) in enumerate(tiles) if nn == n]
                                for jj, (j, kt) in enumerate(tl):
                                    mm(pm, pt[:, j * 128:(j + 1) * 128], VM[:, kt, h, :], jj == 0, jj == len(tl) - 1, [ptk, 'VM'], [bkk])
                            for bi, n in enumerate(grp):
                                pm = bank(bk)[:, bi * 65:(bi + 1) * 65]
                                if n == 0:
                                    ts('dve', oacc[:, h, :], pm, selm[:, h, n:n + 1], None, ALU.mult, None, [bkk, 'selm'], [ok])
                                elif n != own:
                                    stt(oacc[:, h, :], pm, selm[:, h, n:n + 1], oacc[:, h, :], ALU.mult, ALU.add, [bkk, 'selm', ok], [ok])
                                else:
                                    tt('dve', oacc[:, h, :], pm, oacc[:, h, :], ALU.add, [bkk, ok], [ok])
                        units.append((qk, pv))
                return units

            def finish_nsa(qt):
                p = PAR[qt]
                tsl = slice(qt * 128, (qt + 1) * 128)
                cp('act', y_b, yaccs[p].rearrange("p h d -> p (h d)"), ['yacc%d_0' % p, 'yacc%d_1' % p], ['y_b'])
                for c in range(4):
                    tr(ptb[:, c * 128:(c + 1) * 128], y_b[:, c * 128:(c + 1) * 128], identb, ['y_b', 'identb'], ['b2'])
                cp('act', yTn[:, :, tsl], ptb[:, 0:512].rearrange("p (a b) -> p a b", a=4), ['b2'], ['yTn%d' % qt])

            def finish_moba(qt):
                tsl = slice(qt * 128, (qt + 1) * 128)
                oks = ['oacc%d' % h for h in range(8)]
                ts('dve', rsm, oacc[:, :, 64], 1e-30, None, ALU.max, None, oks, ['rsm'])
                recip(rsm, rsm, ['rsm'], ['rsm'])
                tt('dve', y_b2.rearrange("p (h d) -> p h d", h=8), oacc[:, :, 0:64], bc(rsm.unsqueeze(2), [128, 8, 64]), ALU.mult,
                   oks + ['rsm'], ['y_b2'])
                for c in range(4):
                    tr(ptb[:, c * 128:(c + 1) * 128], y_b2[:, c * 128:(c + 1) * 128], identb, ['y_b2', 'identb'], ['b2'])
                cp('act', yTm[:, :, tsl], ptb[:, 0:512].rearrange("p (a b) -> p a b", a=4), ['b2'], ['yTm%d' % qt])

            DEPTH = 3

            def run_units(units, hooks):
                n = len(units)
                for i in range(n + DEPTH):
                    for fn in hooks.pop(i, []):
                        fn()
                    if i < n:
                        units[i][0]()
                    if i - DEPTH >= 0:
                        units[i - DEPTH][1]()
                for i in sorted(hooks):
                    for fn in hooks[i]:
                        fn()

            qlist = list(range(NT) if qts is None else qts)
            front_a(qlist[0])
            front_b1(qlist[0])
            front_b2(qlist[0])
            cmp_sel(qlist[0])
            for qi, qt in enumerate(qlist):
                nxt = qlist[qi + 1] if qi + 1 < len(qlist) else None
                prev = qlist[qi - 1] if qi > 0 else None
                if qt // 2 >= 4:
                    moba_sel(qt)
                win = nsa_units(qt, 0, 'win') + nsa_units(qt, 1, 'win')
                mob = moba_units(qt)
                sel = nsa_units(qt, 0, 'sel') + nsa_units(qt, 1, 'sel')
                units = win + mob + sel
                nW, nM = len(win), len(mob)
                hooks = {}

                def addh(i, fn):
                    hooks.setdefault(i, []).append(fn)
                if prev is not None:
                    addh(1, lambda prev=prev: finish_nsa(prev))
                addh(nW + nM + DEPTH, lambda qt=qt: finish_moba(qt))
                if nxt is not None:
                    addh(nW + 1, lambda nxt=nxt: (front_a(nxt), front_b1(nxt)))
                    addh(nW + nM, lambda nxt=nxt: front_b2(nxt))
                    base = nW + nM + 1
                    addh(base, lambda nxt=nxt: cs1(nxt, 0))
                    addh(base + 2, lambda nxt=nxt: cs2(nxt, 0))
                    addh(base + 3, lambda nxt=nxt: cs1(nxt, 1))
                    addh(base + 5, lambda nxt=nxt: cs3(nxt, 0))
                    addh(base + 5, lambda nxt=nxt: cs2(nxt, 1))
                    addh(base + 8, lambda nxt=nxt: cs3(nxt, 1))
                run_units(units, hooks)
            finish_nsa(qlist[-1])
            QA, QM, sig, yacc = QAs[0], QMs[0], sigs[0], yaccs[0]

            checkpoint('D', [('yTn', yTn), ('yTm', yTm), ('QA', QA), ('QM', QM), ('sig', sig), ('impn', impn), ('yacc', yacc), ('oacc', oacc), ('selm', selm)])
            AR.release(ATT)
            Sx.barrier()
            xres = AR.alloc(128, [NT, D], F32)
            XE = AR.mark()
            mT = AR.alloc(128, [8, 1024], BF16)
            wE = [(AR.alloc(128, [8, 128], BF16), AR.alloc(128, [8, 128], BF16),
                   AR.alloc(128, [4, 128], BF16), AR.alloc(128, [4, 128], BF16)) for _ in range(2)]
            wo = AR.alloc(128, [8, 512], BF16)
            sga = AR.alloc(128, [512], BF16)
            sgb = AR.alloc(128, [512], BF16)
            m1 = AR.alloc(128, [512], F32)
            m2 = AR.alloc(128, [512], F32)
            xst = [AR.alloc(128, [512], F32) for _ in range(2)]
            yTn_all = ['yTn%d' % t for t in range(NT)]
            yTm_all = ['yTm%d' % t for t in range(NT)]
            for half in range(2):
                for c in range(8):
                    wa, wbb, wun, wum = wE[c % 2]
                    wk = 'wE%d' % (c % 2)
                    dma_cast(wk, wa, win_d[:, 2840 + c * 128:2840 + (c + 1) * 128].rearrange("(c p) n -> p c n", p=128), writes=[wk])
                    dma_cast(wk, wbb, win_d[:, 3864 + c * 128:3864 + (c + 1) * 128].rearrange("(c p) n -> p c n", p=128), writes=[wk])
                    dma_cast(wk, wun, wupn_d[:, c * 128:(c + 1) * 128].rearrange("(c p) n -> p c n", p=128), writes=[wk])
                    dma_cast(wk, wum, wupm_d[:, c * 128:(c + 1) * 128].rearrange("(c p) n -> p c n", p=128), writes=[wk])
                    for tg in range(2):
                        t0 = half * 1024 + tg * 512
                        cs = slice(t0, t0 + 512)
                        hk = ['hT%d' % t for t in range(t0 // 128, t0 // 128 + 4)]
                        ynk = ['yTn%d' % t for t in range(t0 // 128, t0 // 128 + 4)]
                        ymk = ['yTm%d' % t for t in range(t0 // 128, t0 // 128 + 4)]
                        for kc in range(8):
                            mm(bank(0), wa[:, kc, :], hT[:, kc, cs], kc == 0, kc == 7, hk + [wk], ['b0'])
                        for kc in range(8):
                            mm(bank(1), wbb[:, kc, :], hT[:, kc, cs], kc == 0, kc == 7, hk + [wk], ['b1'])
                        for kc in range(4):
                            mm(bank(3), wun[:, kc, :], yTn[:, kc, cs], kc == 0, kc == 3, ynk + [wk], ['b3'])
                        for kc in range(4):
                            mm(bank(4), wum[:, kc, :], yTm[:, kc, cs], kc == 0, kc == 3, ymk + [wk], ['b4'])
                        act(sga, bank(0), AF.Sigmoid, ['b0'], ['sga'])
                        act(sgb, bank(1), AF.Sigmoid, ['b1'], ['sgb'])
                        tt('dve', m1, bank(3), sga, ALU.mult, ['b3', 'sga'], ['m1'])
                        tt('dve', m2, bank(4), sgb, ALU.mult, ['b4', 'sgb'], ['m2'])
                        tt('dve', mT[:, c, tg * 512:(tg + 1) * 512], m1, m2, ALU.add, ['m1', 'm2'], ['mT'])
                for ch in range(2):
                    dma_cast('wo', wo, wout_d[:, ch * 512:(ch + 1) * 512].rearrange("(c p) n -> p c n", p=128), writes=['wo'])
                    for tl in range(8):
                        t = half * 8 + tl
                        xk = 'xst%d' % (tl % 2)
                        dma_sp(xk, xst[tl % 2], x_d[b, t * 128:(t + 1) * 128, ch * 512:(ch + 1) * 512], writes=[xk])
                        pj = next_pj()
                        pk = 'b%d' % pj
                        for kc in range(8):
                            mm(bank(pj), mT[:, kc, tl * 128:(tl + 1) * 128], wo[:, kc, :], kc == 0, kc == 7, ['mT', 'wo'], [pk])
                        tt('dve', xres[:, t, ch * 512:(ch + 1) * 512], bank(pj), xst[tl % 2], ALU.add, [pk, xk], ['xres%d' % t])

            checkpoint('E', [('xres', xres)])
            AR.release(XE)
            Sx.barrier()
            xn_sqs = [AR.alloc(128, [1024], F32) for _ in range(2)]
            xn_bs = [AR.alloc(128, [1024], BF16) for _ in range(2)]
            xn_b = None
            wfo = AR.alloc(128, [NFC, 512], BF16)
            wgu = [(AR.alloc(128, [8, 128], BF16), AR.alloc(128, [8, 128], BF16)) for _ in range(2)]
            sgt = AR.alloc(128, [512], BF16)
            ost = [AR.alloc(128, [512], F32) for _ in range(2)]
            pst = AR.alloc(128, [256], F32)
            pstb = AR.alloc(128, [256], BF16)
            sgp = AR.alloc(128, [512], F32)
            sv = AR.mark()
            AR.release(PERSIST)
            h2T = AR.alloc(128, [8, 1024], BF16)
            actT = AR.alloc(128, [NFC, 1024], BF16)
            pT = AR.alloc(128, [2, 1024], BF16)
            assert AR.off <= ATT, (AR.off, ATT)
            AR.off = sv
            wpg = wfo[:, 0:16, :].rearrange("p (k c) n -> p k (c n)", k=8)
            wpp = AR.alloc(128, [2, 1024], BF16)

            for half in range(2):
                for tl in range(8):
                    t = half * 8 + tl
                    norm_transpose(xres[:, t, :], gvec["ffn"], h2T[:, :, tl * 128:(tl + 1) * 128], 'xres%d' % t, 'h2T', xn_b, 2)
                for fc in range(NFC):
                    wg_, wu_ = wgu[fc % 2]
                    wk = 'wgu%d' % (fc % 2)
                    dma_cast(wk, wg_, wfi_d[:, fc * 128:(fc + 1) * 128].rearrange("(c p) n -> p c n", p=128), writes=[wk])
                    dma_cast(wk, wu_, wfi_d[:, DFF + fc * 128:DFF + (fc + 1) * 128].rearrange("(c p) n -> p c n", p=128), writes=[wk])
                    for tg in range(2):
                        cs = slice(tg * 512, (tg + 1) * 512)
                        bg, bu = (0, 1) if tg == 0 else (3, 4)
                        for kc in range(8):
                            mm(bank(bg), wg_[:, kc, :], h2T[:, kc, cs], kc == 0, kc == 7, ['h2T', wk], ['b%d' % bg])
                        for kc in range(8):
                            mm(bank(bu), wu_[:, kc, :], h2T[:, kc, cs], kc == 0, kc == 7, ['h2T', wk], ['b%d' % bu])
                        act(sgt, bank(bg), AF.Silu, ['b%d' % bg], ['sgt'])
                        tt('dve', actT[:, fc, cs], bank(bu), sgt, ALU.mult, ['b%d' % bu, 'sgt'], ['actT'])
                for ch in range(2):
                    dma_cast('wfo', wfo, wfo_d[:, ch * 512:(ch + 1) * 512].rearrange("(c p) n -> p c n", p=128), writes=['wfo'])
                    for tl in range(8):
                        t = half * 8 + tl
                        pj = 5 + (tl % 2)
                        pk = 'b%d' % pj
                        for fc in range(NFC):
                            mm(bank(pj), actT[:, fc, tl * 128:(tl + 1) * 128], wfo[:, fc, :], fc == 0, fc == NFC - 1, ['actT', 'wfo'], [pk])
                        xr = xres[:, t, ch * 512:(ch + 1) * 512]
                        tt('dve', xr, bank(pj), xr, ALU.add, [pk, 'xres%d' % t], ['xres%d' % t])
                for tl in range(8):
                    t = half * 8 + tl
                    norm_transpose(xres[:, t, :], gvec["ple"], h2T[:, :, tl * 128:(tl + 1) * 128], 'xres%d' % t, 'h2T', xn_b, 2)
                    dma_sp('pst', pst, p_d[b, t * 128:(t + 1) * 128, :], writes=['pst'])
                    cp('dve', pstb, pst, ['pst'], ['pstb'])
                    for c in range(2):
                        tr(bank_bf(7)[:, c * 128:(c + 1) * 128], pstb[:, c * 128:(c + 1) * 128], identb, ['pstb', 'identb'], ['b7'])
                    cp('act', pT[:, :, tl * 128:(tl + 1) * 128], bank_bf(7)[:, 0:256].rearrange("p (a b) -> p a b", a=2), ['b7'], ['pT'])
                dma_cast('wfo', wpg, wpg_d.rearrange("(c p) n -> p c n", p=128), writes=['wfo'])
                dma_cast('wpp', wpp, wpp_d.rearrange("(c p) n -> p c n", p=128), writes=['wpp'])
                for tl in range(8):
                    t = half * 8 + tl
                    tls = slice(tl * 128, (tl + 1) * 128)
                    for ch in range(2):
                        ccs = slice(ch * 512, (ch + 1) * 512)
                        bg, bp = (0, 1) if ch == 0 else (3, 4)
                        for kc in range(8):
                            mm(bank(bg), h2T[:, kc, tls], wpg[:, kc, ccs], kc == 0, kc == 7, ['h2T', 'wfo'], ['b%d' % bg])
                        for kc in range(2):
                            mm(bank(bp), pT[:, kc, tls], wpp[:, kc, ccs], kc == 0, kc == 1, ['pT', 'wpp'], ['b%d' % bp])
                        act(sgp, bank(bg), AF.Sigmoid, ['b%d' % bg], ['sgp'])
                        ob = ost[ch]
                        ok = 'ost%d' % ch
                        tt('dve', ob, bank(bp), sgp, ALU.mult, ['b%d' % bp, 'sgp'], [ok])
                        tt('dve', ob, ob, xres[:, t, ccs], ALU.add, [ok, 'xres%d' % t], [ok])
                        dma_sp('out%d' % ch, out_d[b, t * 128:(t + 1) * 128, ccs], ob, reads=[ok], writes=['outd'])
        except _Stop:
            pass
        Sx.barrier()

        keys = Sx.sem_keys()
        sems = {k: es.enter_context(nc.semaphore(k.replace(':', '_'))) for k in keys}
        with nc.Block() as block:
            run = Sx.runner(sems)
            block.sync(run('sp'))
            block.tensor(run('pe'))
            block.scalar(run('act'))
            block.vector(run('dve'))
            block.gpsimd(run('pool'))
    build_program.last_dbg = dbg_outs
    return nc


def _consts():
    c = {}
    c["c_ident"] = np.eye(128, dtype=np.float32)
    k = np.arange(128)[:, None]
    t = np.arange(128)[None, :]
    c["c_tri"] = (k <= t).astype(np.float32)
    c["c_anti"] = (k > t).astype(np.float32)
    c["c_trib4"] = np.tile(np.where(k <= t, 0.0, NEG).astype(np.float32), (1, 4))
    c["c_antib4"] = np.tile(np.where(k > t, 0.0, NEG).astype(np.float32), (1, 4))
    cc = np.arange(127)[:, None]
    tt_ = np.arange(S)[None, :]
    c["c_cmask"] = ((16 * cc + 31) <= tt_).astype(np.float32)
    tl = np.arange(128)[:, None, None]
    qt = np.arange(NT)[None, :, None]
    j = np.arange(32)[None, None, :]
    cur = (qt * 128 + tl) // 64
    forced = (j == 0) | (j == cur) | (j == cur - 1)
    fb = np.where(forced, 1e9, np.where(j > cur, -1e9, 0.0)).astype(np.float32)
    c["c_fbs"] = np.ascontiguousarray(fb.reshape(128, NT * 32))
    n = np.arange(8)[None, None, :]
    own = (qt // 2)
    fm = np.where(n >= own, -1e9, 0.0).astype(np.float32) + np.zeros((128, 1, 1), np.float32)
    c["c_fbm"] = np.ascontiguousarray(fm.reshape(128, NT * 8))
    cs = np.arange(127) * 16
    ce = cs + 31
    jj = np.arange(32)
    ov = ((cs[:, None] <= jj[None, :] * 64 + 63) & (ce[:, None] >= jj[None, :] * 64)).astype(np.float32)
    c["c_ovl"] = np.concatenate([ov, np.ones((127, 1), np.float32)], axis=1)
    half = 8
    c["c_invf"] = (500000.0 ** (-np.arange(half, dtype=np.float32) / half)).astype(np.float32).reshape(1, 8)
    c["c_emat"] = (np.arange(S)[None, :] // 64 == np.arange(32)[:, None]).astype(np.float32)
    return c


_NC_CACHE = {}


def make_in_maps(inputs, n_cores=8):
    consts = _consts()
    f = lambda a: np.ascontiguousarray(np.asarray(a))
    shared = dict(consts)
    shared["g_mix"] = f(inputs["g_mix"][0]).reshape(8, 128)
    shared["g_ffn"] = f(inputs["g_ffn"][0]).reshape(8, 128)
    shared["g_ple"] = f(inputs["g_ple"][0]).reshape(8, 128)
    for n in ("nsa_q_gain", "nsa_kc_gain", "nsa_ks_gain", "nsa_kw_gain", "moba_q_gain", "moba_k_gain"):
        shared[n] = f(inputs[n][0]).reshape(1, 64)
    for n in ("w_in", "nsa_pe_k", "nsa_pe_v", "nsa_ck_w1", "nsa_ck_w2", "nsa_cv_w1", "nsa_cv_w2", "w_up_nsa", "w_up_moba",
              "w_out", "w_ffn_in", "w_ffn_out", "w_ple_gate", "w_ple_proj"):
        shared[n] = f(inputs[n][0])
    x = np.asarray(inputs["x"])
    p = np.asarray(inputs["p"])[0]
    pos = np.asarray(inputs["positions"]).astype(np.int32)
    in_maps = []
    for c in range(n_cores):
        m = dict(shared)
        m["x"] = f(x[c * NB:(c + 1) * NB])
        m["p"] = f(p[c * NB:(c + 1) * NB])
        m["pos"] = f(pos[c * NB:(c + 1) * NB]).reshape(NB, NT, 128)
        in_maps.append(m)
    return in_maps


def kernel(**inputs):
    n_cores = 8
    if "nc" not in _NC_CACHE:
        _NC_CACHE["nc"] = build_program()
    nc = _NC_CACHE["nc"]
    in_maps = make_in_maps(inputs, n_cores)
    res = run_bass_kernel_spmd(nc, in_maps, core_ids=list(range(n_cores)))
    out = np.concatenate([np.asarray(r["out"]) for r in res.results], axis=0)
    return out.astype(np.float32)
```

```python
import math
from contextlib import ExitStack
import numpy as np
import concourse.bass as bass
import concourse.mybir as mybir
from concourse.bass_utils import run_bass_kernel_spmd

F32 = mybir.dt.float32
BF16 = mybir.dt.bfloat16
I32 = mybir.dt.int32
AF = mybir.ActivationFunctionType
ALU = mybir.AluOpType
AX = mybir.AxisListType

NB = 2
S = 2048
D = 1024
NT = S // 128
DFF = 2816
NFC = DFF // 128
EPS = 1e-6
NEG = -30000.0
PI = math.pi


class Sched:
    ENG = ('pe', 'act', 'dve', 'pool', 'sp')

    def __init__(self):
        self.ops = {e: [] for e in self.ENG}
        self.cnt = {e: 0 for e in self.ENG}
        self.seen = {e: {} for e in self.ENG}
        self.res = {}
        self.dma_cnt = {}

    def _deps(self, eng, reads, writes):
        deps = {}

        def add(tok):
            if tok is None:
                return
            k, v = tok
            if deps.get(k, 0) < v:
                deps[k] = v
        for key in reads:
            r = self.res.get(key)
            if r is not None:
                add(r[0])
        for key in writes:
            r = self.res.get(key)
            if r is not None:
                add(r[0])
                for k, v in r[1].items():
                    add((k, v))
        out = []
        for k, v in deps.items():
            if eng == 'pe' and k == 'pe':
                continue
            if self.seen[eng].get(k, 0) >= v:
                continue
            self.seen[eng][k] = v
            out.append((k, v))
        return out

    def _commit(self, tok, reads, writes):
        k, v = tok
        for key in reads:
            r = self.res.setdefault(key, [None, {}])
            if r[1].get(k, 0) < v:
                r[1][k] = v
        for key in writes:
            self.res[key] = [tok, {}]

    def op(self, eng, fn, reads=(), writes=()):
        waits = self._deps(eng, reads, writes)
        self.cnt[eng] += 1
        tok = (eng, self.cnt[eng])
        self.ops[eng].append((waits, fn, (eng, 1)))
        self._commit(tok, reads, writes)
        return tok

    def dma(self, eng, stream, fn, reads=(), writes=()):
        waits = self._deps(eng, reads, writes)
        self.dma_cnt[stream] = self.dma_cnt.get(stream, 0) + 16
        tok = ('dma:' + stream, self.dma_cnt[stream])
        self.ops[eng].append((waits, fn, ('dma:' + stream, 16)))
        self._commit(tok, reads, writes)
        return tok

    def barrier(self):
        toks = [(e, self.cnt[e]) for e in self.ENG if self.cnt[e] > 0]
        toks += [('dma:' + s, v) for s, v in self.dma_cnt.items()]
        for e in self.ENG:
            waits = []
            for k, v in toks:
                if k == e and e == 'pe':
                    continue
                if self.seen[e].get(k, 0) >= v:
                    continue
                self.seen[e][k] = v
                waits.append((k, v))
            if waits:
                self.ops[e].append((waits, None, None))
        self.res = {}

    def sem_keys(self):
        ks = set(self.ENG)
        ks.update('dma:' + s for s in self.dma_cnt)
        return sorted(ks)

    def runner(self, sems):
        def run(eng_name):
            def body(engine):
                for waits, fn, inc in self.ops[eng_name]:
                    for k, v in waits:
                        engine.wait_ge(sems[k], v)
                    if fn is not None:
                        ins = fn(engine)
                        ins.then_inc(sems[inc[0]], inc[1])
            return body
        return run


class Arena:
    def __init__(self, t, nbytes):
        self.t = t
        self.off = 0
        self.nbytes = nbytes
        self.peak = 0

    def alloc(self, parts, shape, dtype):
        n = 1
        for s in shape:
            n *= s
        esz = 2 if dtype == BF16 else 4
        nb = (n * esz + 3) // 4 * 4
        assert self.off + nb <= self.nbytes, ("arena overflow", self.off, nb, self.nbytes)
        ap = self.t[0:parts, self.off // 2:(self.off + n * esz) // 2]
        self.off += nb
        self.peak = max(self.peak, self.off)
        if dtype != BF16:
            ap = ap.bitcast(dtype)
        if len(shape) == 2:
            ap = ap.rearrange("p (a b) -> p a b", a=shape[0])
        elif len(shape) == 3:
            ap = ap.rearrange("p (a b c) -> p a b c", a=shape[0], b=shape[1])
        return ap

    def mark(self):
        return self.off

    def release(self, m):
        self.off = m


def bc(ap, shape):
    return ap.broadcast_to(list(shape))


class _Stop(Exception):
    pass


def build_program(stop=None, nb=NB, qts=None):
    nc = bass.Bass("TRN2", target_bir_lowering=False)
    dbg_outs = {}

    def din(name, shape, dt=F32):
        return nc.dram_tensor(name, list(shape), dt, kind="ExternalInput").ap()

    x_d = din("x", [NB, S, D])
    p_d = din("p", [NB, S, 256])
    pos_d = din("pos", [NB, NT, 128], I32)
    gmix_d = din("g_mix", [8, 128])
    gffn_d = din("g_ffn", [8, 128])
    gple_d = din("g_ple", [8, 128])
    win_d = din("w_in", [D, 4888])
    gains_d = {n: din(n, [1, 64]) for n in ("nsa_q_gain", "nsa_kc_gain", "nsa_ks_gain", "nsa_kw_gain",
                                            "moba_q_gain", "moba_k_gain")}
    pek_d = din("nsa_pe_k", [32, 64])
    pev_d = din("nsa_pe_v", [32, 64])
    ckw1_d = din("nsa_ck_w1", [2048, 256])
    ckw2_d = din("nsa_ck_w2", [256, 64])
    cvw1_d = din("nsa_cv_w1", [2048, 256])
    cvw2_d = din("nsa_cv_w2", [256, 64])
    wupn_d = din("w_up_nsa", [512, D])
    wupm_d = din("w_up_moba", [512, D])
    wout_d = din("w_out", [D, D])
    wfi_d = din("w_ffn_in", [D, 2 * DFF])
    wfo_d = din("w_ffn_out", [DFF, D])
    wpg_d = din("w_ple_gate", [D, D])
    wpp_d = din("w_ple_proj", [256, D])
    ident_d = din("c_ident", [128, 128])
    tri_d = din("c_tri", [128, 128])
    anti_d = din("c_anti", [128, 128])
    cmask_d = din("c_cmask", [127, S])
    fbs_d = din("c_fbs", [128, NT * 32])
    fbm_d = din("c_fbm", [128, NT * 8])
    ovl_d = din("c_ovl", [127, 33])
    invf_d = din("c_invf", [1, 8])
    emat_d = din("c_emat", [32, S])
    trib4_d = din("c_trib4", [128, 512])
    antib4_d = din("c_antib4", [128, 512])
    out_d = nc.dram_tensor("out", [NB, S, D], F32, kind="ExternalOutput").ap()

    Sx = Sched()
    ARENA_BYTES = 207 * 1024 + 512
    with ExitStack() as es:
        arena_t = es.enter_context(nc.sbuf_tensor("arena", [128, ARENA_BYTES // 2], BF16))
        psum_t = es.enter_context(nc.psum_tensor("psum", [128, 8, 512], F32))
        AR = Arena(arena_t, ARENA_BYTES)

        def bank(i):
            return psum_t[:, i, :]

        def bank_bf(i):
            return psum_t[:, i, :].bitcast(BF16)

        def dma_sp(stream, out, in_, reads=(), writes=(), slow=False):
            if slow:
                return Sx.dma('sp', stream, lambda e: e.dma_start(out=out, in_=in_, allow_slow_non_contiguous=True), reads, writes)
            return Sx.dma('sp', stream, lambda e: e.dma_start(out=out, in_=in_), reads, writes)

        def dma_cast(stream, out, in_, reads=(), writes=()):
            return Sx.dma('pool', stream, lambda e: e.dma_start(out=out, in_=in_), reads, writes)

        def mm(out, lhsT, rhs, start, stop, reads, writes, skip=False):
            return Sx.op('pe', lambda e: e.matmul(out, lhsT=lhsT, rhs=rhs, start=start, stop=stop,
                                                  skip_group_check=skip), reads, writes)

        def tr(out, in_, ident, reads, writes):
            return Sx.op('pe', lambda e: e.transpose(out=out, in_=in_, identity=ident), reads, writes)

        def act(out, in_, func, reads, writes, scale=None, bias=None):
            kw = {}
            if scale is not None:
                kw['scale'] = scale
            if bias is not None:
                kw['bias'] = bias
            return Sx.op('act', lambda e: e.activation(out=out, in_=in_, func=func, **kw), reads, writes)

        def tt(eng, out, in0, in1, op, reads, writes):
            return Sx.op(eng, lambda e: e.tensor_tensor(out=out, in0=in0, in1=in1, op=op), reads, writes)

        def ts(eng, out, in0, s1, s2, op0, op1, reads, writes):
            if op1 is None:
                return Sx.op(eng, lambda e: e.tensor_scalar(out=out, in0=in0, scalar1=s1, scalar2=None, op0=op0),
                             reads, writes)
            return Sx.op(eng, lambda e: e.tensor_scalar(out=out, in0=in0, scalar1=s1, scalar2=s2, op0=op0, op1=op1),
                         reads, writes)

        def stt(out, in0, scalar, in1, op0, op1, reads, writes):
            return Sx.op('dve', lambda e: e.scalar_tensor_tensor(out=out, in0=in0, scalar=scalar, in1=in1,
                                                                 op0=op0, op1=op1), reads, writes)

        def cp(eng, out, in_, reads, writes):
            if eng == 'act':
                return Sx.op(eng, lambda e: e.copy(out=out, in_=in_), reads, writes)
            return Sx.op(eng, lambda e: e.tensor_copy(out=out, in_=in_), reads, writes)

        def recip(out, in_, reads, writes):
            return Sx.op('dve', lambda e: e.reciprocal(out=out, in_=in_), reads, writes)

        def memset(eng, ap, val, writes):
            return Sx.op(eng, lambda e: e.memset(ap, val), (), writes)

        def checkpoint(name, dumps):
            if stop != name:
                return
            Sx.barrier()
            for label, ap in dumps:
                shp = list(ap.shape)
                dt_ = ap.dtype
                d = nc.dram_tensor("dbg_" + label, shp, dt_, kind="ExternalOutput").ap()
                dbg_outs[label] = d
                Sx.dma('sp', 'dbg_' + label, lambda e, d=d, ap=ap: e.dma_start(out=d, in_=ap), (), ())
            raise _Stop()

        identf = AR.alloc(128, [128], F32)
        identb = AR.alloc(128, [128], BF16)
        trib = AR.alloc(128, [128], BF16)
        antib = AR.alloc(128, [128], BF16)
        trib4 = AR.alloc(128, [512], BF16)
        antib4 = AR.alloc(128, [512], BF16)
        cmask = AR.alloc(127, [NT, 128], BF16)
        fbs = AR.alloc(128, [NT, 32], F32)
        fbm = AR.alloc(128, [NT, 8], F32)
        invf = AR.alloc(128, [8], F32)
        gtile = {n: AR.alloc(128, [64], F32) for n in gains_d}
        gq8 = AR.alloc(128, [64], F32)
        gm8 = AR.alloc(128, [64], F32)
        gvec = {n: AR.alloc(128, [8], F32) for n in ("mix", "ffn", "ple")}
        gstage = AR.alloc(8, [128], F32)
        pestage = AR.alloc(32, [2, 64], F32)
        pestb = AR.alloc(32, [2, 64], BF16)
        peT = AR.alloc(64, [2, 32], BF16)
        cos_t = AR.alloc(128, [NT, 8], F32)
        sin_t = AR.alloc(128, [NT, 8], F32)
        cosc = AR.alloc(127, [8], F32)
        sinc = AR.alloc(127, [8], F32)
        n_ss2 = AR.alloc(128, [16], F32)
        n_rs2 = AR.alloc(128, [16], F32)
        x_ss2 = AR.alloc(128, [2], F32)
        x_rs2 = AR.alloc(128, [2], F32)

        dma_sp('c0', identf, ident_d, writes=['identf'])
        cp('dve', identb, identf, ['identf'], ['identb'])
        dma_cast('c1', trib, tri_d, writes=['trib'])
        dma_cast('c2', antib, anti_d, writes=['antib'])
        dma_cast('c2b', trib4, trib4_d, writes=['trib4'])
        dma_cast('c2c', antib4, antib4_d, writes=['antib4'])
        dma_cast('c3', cmask, cmask_d.rearrange("p (a b) -> p a b", a=NT), writes=['cmask'])
        dma_sp('c4', fbs, fbs_d.rearrange("p (a b) -> p a b", a=NT), writes=['fbs'])
        dma_sp('c5', fbm, fbm_d.rearrange("p (a b) -> p a b", a=NT), writes=['fbm'])
        dma_sp('c6', invf, invf_d.partition_broadcast(128), writes=['invf'])
        for i, n in enumerate(gains_d):
            dma_sp('c7_%d' % i, gtile[n], gains_d[n].partition_broadcast(128), writes=['g_' + n])
        ts('dve', gq8, gtile["nsa_q_gain"], 0.125, None, ALU.mult, None, ['g_nsa_q_gain'], ['gq8'])
        ts('dve', gm8, gtile["moba_q_gain"], 0.125, None, ALU.mult, None, ['g_moba_q_gain'], ['gm8'])
        for n, dd in (("mix", gmix_d), ("ffn", gffn_d), ("ple", gple_d)):
            dma_sp('c8', gstage, dd, writes=['gstage'])
            tr(bank(7)[:, 0:8], gstage, identf[0:8, 0:8], ['gstage', 'identf'], ['b7'])
            cp('dve', gvec[n], bank(7)[:, 0:8], ['b7'], ['gvec' + n])
        dma_sp('c9', pestage[:, 0, :], pek_d, writes=['pestage'])
        dma_sp('c9', pestage[:, 1, :], pev_d, writes=['pestage'])
        cp('dve', pestb, pestage, ['pestage'], ['pestb'])
        for kv in range(2):
            tr(bank_bf(7)[0:64, kv * 32:(kv + 1) * 32], pestb[:, kv, :], identb[0:32, 0:32], ['pestb', 'identb'], ['b7'])
        cp('dve', peT, bank_bf(7)[0:64, 0:64].rearrange("p (a b) -> p a b", a=2), ['b7'], ['peT'])

        PERSIST = AR.mark()

        ntc = [0]

        def norm_transpose(src, gain, dst, rk, wk, xn, bk):
            ntc[0] ^= 1
            i = ntc[0]
            sq = xn_sqs[i]
            xnb = xn_bs[i]
            ss = x_ss2[:, i:i + 1]
            rs = x_rs2[:, i:i + 1]
            bk = 2 if i == 0 else 7
            sk, xk, ssk, rsk = 'xn_sq%d' % i, 'xn%d' % i, 'x_ss%d' % i, 'x_rs%d' % i
            tt('pool', sq, src, src, ALU.mult, [rk], [sk])
            Sx.op('dve', lambda e: e.tensor_reduce(out=ss, in_=sq, axis=AX.X, op=ALU.add), [sk], [ssk])
            act(rs, ss, AF.Ln, [ssk], [rsk], scale=1.0 / D, bias=EPS)
            act(rs, rs, AF.Exp, [rsk], [rsk], scale=-0.5)
            act(xnb, src, AF.Copy, [rk, rsk], [xk], scale=rs)
            pb = bank_bf(bk).rearrange("p (a b) -> p a b", a=8)
            for c in range(8):
                tr(pb[:, c, :], xnb[:, c * 128:(c + 1) * 128], identb, [xk, 'identb'], ['b%d' % bk])
            tt('dve', dst, pb, bc(gain.unsqueeze(2), [128, 8, 128]), ALU.mult, ['b%d' % bk], [wk])

        def head_norm_rope_gen(src, H, rows, gain3, cosb, sinb, out_bf, rk, wk, extra_reads=(), sfx=0):
            R = rows
            kr, kq, kss, krs = 'hn_raw%d' % sfx, 'hn_sq%d' % sfx, 'n_ss%d' % sfx, 'n_rs%d' % sfx
            qraw = hn_raws[sfx][0:R, 0:H * 64]
            sq = hn_sqs[sfx][0:R, 0:H * 64]
            hr = hn_rs[sfx]
            nss = n_ss2[0:R, sfx * 8:sfx * 8 + H]
            nrs = n_rs2[0:R, sfx * 8:sfx * 8 + H]
            q3 = qraw.rearrange("p (h d) -> p h d", h=H)
            act(qraw, src, AF.Copy, [rk], [kr]); yield
            tt('pool', sq, qraw, qraw, ALU.mult, [kr], [kq]); yield
            Sx.op('dve', lambda e: e.tensor_reduce(out=nss, in_=sq.rearrange("p (h d) -> p h d", h=H),
                                                   axis=AX.X, op=ALU.add), [kq], [kss]); yield
            act(nrs, nss, AF.Ln, [kss], [krs], scale=1.0 / 64, bias=EPS); yield
            act(nrs, nrs, AF.Exp, [krs], [krs], scale=-0.5); yield
            tt('dve', q3, q3, bc(nrs.unsqueeze(2), [R, H, 64]), ALU.mult, [kr, krs], [kr]); yield
            tt('dve', q3, q3, gain3, ALU.mult, [kr] + list(extra_reads), [kr]); yield
            cp('pool', out_bf[:, :, 16:64], q3[:, :, 16:64], [kr], [wk]); yield
            x1 = q3[:, :, 0:8]
            x2 = q3[:, :, 8:16]
            c3 = bc(cosb.unsqueeze(1), [R, H, 8])
            s3 = bc(sinb.unsqueeze(1), [R, H, 8])
            ra = hr[0:R, 0, 0:H, :]
            rb = hr[0:R, 1, 0:H, :]
            rc = hr[0:R, 2, 0:H, :]
            rd = hr[0:R, 3, 0:H, :]
            ks_ = ['hn_r%s%d' % (c, sfx) for c in 'abcd']
            tt('dve', ra, x1, c3, ALU.mult, [kr, 'rope'], [ks_[0]]); yield
            tt('dve', rb, x2, s3, ALU.mult, [kr, 'rope'], [ks_[1]]); yield
            tt('dve', rc, x2, c3, ALU.mult, [kr, 'rope'], [ks_[2]]); yield
            tt('dve', rd, x1, s3, ALU.mult, [kr, 'rope'], [ks_[3]]); yield
            tt('dve', out_bf[:, :, 0:8], ra, rb, ALU.subtract, [ks_[0], ks_[1]], [wk]); yield
            tt('dve', out_bf[:, :, 8:16], rc, rd, ALU.add, [ks_[2], ks_[3]], [wk]); yield

        def head_norm_rope(src, H, rows, gain3, cosb, sinb, out_bf, rk, wk, extra_reads=(), sfx=0):
            for _ in head_norm_rope_gen(src, H, rows, gain3, cosb, sinb, out_bf, rk, wk, extra_reads, sfx):
                pass

        def run_interleaved(gens):
            gens = list(gens)
            while gens:
                for g_ in list(gens):
                    try:
                        next(g_)
                    except StopIteration:
                        gens.remove(g_)

        sq_eng = ['dve']
        pjc = [0]

        def next_pj():
            pjc[0] ^= 1
            return pjc[0]

        try:
          main_body = True
          for b in range(nb):
            AR.release(PERSIST)
            Sx.barrier()
            hT = AR.alloc(128, [8, S], BF16)
            yTn = AR.alloc(128, [4, S], BF16)
            yTm = AR.alloc(128, [4, S], BF16)
            ATT = AR.mark()
            ksA = AR.alloc(96, [2, S], BF16)
            kwT = AR.alloc(64, [2, S], BF16)
            kmT = AR.alloc(128, [4, S], BF16)
            VS = AR.alloc(128, [NT, 2, 65], BF16)
            VW = AR.alloc(128, [NT, 2, 65], BF16)
            VM = AR.alloc(128, [NT, 8, 65], BF16)
            kcT = AR.alloc(64, [2, 127], BF16)
            VC = AR.alloc(127, [2, 97], BF16)
            kmean = AR.alloc(128, [4, 8], BF16)
            kmean_f = AR.alloc(128, [4, 8], F32)
            kmean_z = AR.alloc(128, [8, 8], BF16)
            hn_raws = [AR.alloc(128, [512], F32)]
            hn_sqs = [AR.alloc(128, [512], F32)]
            hn_rs = [AR.alloc(128, [4, 8, 8], F32)]
            xn_sqs = [AR.alloc(128, [1024], F32)]
            xn_bs = [AR.alloc(128, [1024], BF16)]
            _al = AR.mark()
            xn_sqs.append(AR.alloc(128, [1024], F32))
            xn_bs.append(AR.alloc(128, [1024], BF16))
            _al_end = AR.mark()
            AR.release(_al)
            hn_raws.append(AR.alloc(128, [512], F32))
            hn_sqs.append(AR.alloc(128, [512], F32))
            hn_rs.append(AR.alloc(128, [4, 8, 8], F32))
            tm_b2 = AR.alloc(128, [512], BF16)
            assert AR.off <= _al_end
            AR.release(_al_end)
            xn_b = None
            BC = AR.mark()
            xs = [AR.alloc(128, [1024], F32) for _ in range(2)]
            wb = [AR.alloc(128, [8, 512], BF16)]
            ks_bs = [AR.alloc(128, [2, 64], BF16) for _ in range(2)]
            kvT = AR.alloc(64, [4, S], BF16)
            w1sb = AR.alloc(64, [32, 256], BF16)
            _sv = AR.mark()
            AR.release(_sv - 16 * 1024)
            wb.append(AR.alloc(128, [8, 512], BF16))
            AR.release(_sv)
            w2sb = AR.alloc(128, [2, 2, 64], BF16)
            hsb = AR.alloc(128, [2, 127], BF16)
            hb = AR.alloc(128, [2], F32)
            tm_b = AR.alloc(128, [512], BF16)
            tm_bs = [tm_b, tm_b2]
            posi = AR.alloc(16, [128], I32)
            posf16 = AR.alloc(16, [128], F32)
            posf = AR.alloc(128, [NT], F32)
            ang = AR.alloc(128, [NT, 8], F32)
            ang2 = AR.alloc(128, [NT, 8], F32)
            angi = AR.alloc(128, [NT, 8], I32)
            angn = AR.alloc(128, [NT, 8], F32)
            posci = AR.alloc(127, [1], I32)
            poscf = AR.alloc(127, [1], F32)

            def sincos(angv, outv, R, shape, shift):
                a2 = ang2[0:R] if len(shape) == 2 else ang2[0:R, 0, :]
                ai = angi[0:R] if len(shape) == 2 else angi[0:R, 0, :]
                an = angn[0:R] if len(shape) == 2 else angn[0:R, 0, :]
                ts('dve', a2, angv, shift, 1.0 / (2 * PI), ALU.add, ALU.mult, ['ang'], ['ang2'])
                cp('dve', ai, a2, ['ang2'], ['angi'])
                cp('dve', an, ai, ['angi'], ['angn'])
                ts('dve', an, an, -2 * PI, None, ALU.mult, None, ['angn'], ['angn'])
                tt('dve', a2, an, angv, ALU.add, ['angn', 'ang'], ['ang2'])
                ts('dve', a2, a2, shift, None, ALU.add, None, ['ang2'], ['ang2'])
                ts('dve', an, a2, PI, -2 * PI, ALU.is_gt, ALU.mult, ['ang2'], ['angn'])
                tt('dve', a2, a2, an, ALU.add, ['ang2', 'angn'], ['ang2'])
                ts('dve', an, a2, -PI, 2 * PI, ALU.is_lt, ALU.mult, ['ang2'], ['angn'])
                tt('dve', a2, a2, an, ALU.add, ['ang2', 'angn'], ['ang2'])
                ts('dve', a2, a2, PI, -PI, ALU.min, ALU.max, ['ang2'], ['ang2'])
                act(outv, a2, AF.Sin, ['ang2'], ['rope'])

            dma_sp('pos', posi, pos_d[b], writes=['posi'])
            cp('dve', posf16, posi, ['posi'], ['posf16'])
            tr(bank(7)[:, 0:16], posf16, identf[0:16, 0:16], ['posf16', 'identf'], ['b7'])
            cp('dve', posf, bank(7)[:, 0:16], ['b7'], ['posf'])
            tt('dve', ang, bc(posf.unsqueeze(2), [128, NT, 8]), bc(invf.unsqueeze(1), [128, NT, 8]), ALU.mult,
               ['posf', 'invf'], ['ang'])
            sincos(ang, sin_t, 128, [NT, 8], 0.0)
            sincos(ang, cos_t, 128, [NT, 8], PI / 2)
            posflat = pos_d[b].rearrange("a b -> (a b)")
            dma_sp('posc', posci, posflat[16:S].rearrange("(c s) -> c s", s=16)[:, 15:16], writes=['posci'], slow=True)
            cp('dve', poscf, posci, ['posci'], ['poscf'])
            angc = ang[0:127, 0, :]
            ts('dve', angc, invf[0:127, :], poscf[:, 0:1], None, ALU.mult, None, ['poscf', 'invf', 'rope'], ['ang'])
            sincos(angc, sinc, 127, [8], 0.0)
            sincos(angc, cosc, 127, [8], PI / 2)

            sq_eng[0] = 'pool'
            for t in range(NT):
                xb = xs[t % 2]
                dma_sp('xs%d' % (t % 2), xb, x_d[b, t * 128:(t + 1) * 128, :], writes=['xs%d' % (t % 2)])
                norm_transpose(xb, gvec["mix"], hT[:, :, t * 128:(t + 1) * 128], 'xs%d' % (t % 2), 'hT%d' % t, xn_b, 2)

            sq_eng[0] = 'dve'
            Sx.barrier()
            checkpoint('A', [('hT', hT), ('cos', cos_t), ('sin', sin_t), ('cosc', cosc), ('sinc', sinc)])
            memset('pool', VS[:, :, :, 64:65], 1.0, ['VS'])
            memset('pool', VW[:, :, :, 64:65], 1.0, ['VW'])
            memset('pool', VM[:, :, :, 64:65], 1.0, ['VM'])
            for g in range(2):
                dma_cast('emat', ksA[64:96, g, :], emat_d, writes=['ksA_e'])
                dma_cast('ovl', VC[:, g, 64:97], ovl_d, writes=['VC_o'])

            gks3 = bc(gtile["nsa_ks_gain"].unsqueeze(1), [128, 2, 64])
            gkw3 = bc(gtile["nsa_kw_gain"].unsqueeze(1), [128, 2, 64])
            gkm3 = bc(gtile["moba_k_gain"].unsqueeze(1), [128, 8, 64])
            chunks = [(512, 512, 'kcvcksvs'), (1024, 256, 'kwvw'), (1816, 512, 'km'), (2328, 512, 'vm')]
            hT_all = ['hT%d' % t for t in range(NT)]
            for ci, (c0, cw, kind) in enumerate(chunks):
                wbuf = wb[ci % 2]
                wk = 'wb0' if ci % 2 == 0 else 'w1sb'
                dma_cast(wk, wbuf[:, :, 0:cw], win_d[:, c0:c0 + cw].rearrange("(c p) n -> p c n", p=128), writes=[wk])

                PJB = [0, 1, 3, 4]

                def mmB(t, wbuf=wbuf, wk=wk, cw=cw):
                    pj = PJB[t % 4]
                    tsl = slice(t * 128, (t + 1) * 128)
                    for kc in range(8):
                        mm(bank(pj)[:, 0:cw], hT[:, kc, tsl], wbuf[:, kc, 0:cw], kc == 0, kc == 7, ['hT%d' % t, wk], ['b%d' % pj])

                def postB(t, kind=kind):
                    pj = PJB[t % 4]
                    par = t % 2
                    pk = 'b%d' % pj
                    tsl = slice(t * 128, (t + 1) * 128)
                    ptb = bank_bf(2)
                    tmb = tm_bs[par]
                    tk = 'tm_b%d' % par
                    ksb = ks_bs[par]
                    kk = 'ks_b%d' % par
                    if kind == 'kcvcksvs':
                        cp('act', tmb[:, 0:256], bank(pj)[:, 0:256], [pk], [tk]); yield
                        cp('act', VS[:, t, :, 0:64], bank(pj)[:, 384:512].rearrange("p (g d) -> p g d", g=2), [pk], ['VS']); yield
                        yield from head_norm_rope_gen(bank(pj)[:, 256:384], 2, 128, gks3, cos_t[:, t, :], sin_t[:, t, :], ksb, pk, kk,
                                                      ['g_nsa_ks_gain'], par)
                        for i in range(4):
                            tr(ptb[0:64, i * 128:(i + 1) * 128], tmb[:, i * 64:(i + 1) * 64], identb, [tk, 'identb'], ['b2'])
                        cp('act', kvT[:, :, tsl], ptb[0:64, 0:512].rearrange("p (a b) -> p a b", a=4), ['b2'], ['kvT']); yield
                        for g in range(2):
                            tr(ptb[0:64, g * 128:(g + 1) * 128], ksb[:, g, :], identb, [kk, 'identb'], ['b2'])
                        cp('act', ksA[0:64, :, tsl], ptb[0:64, 0:256].rearrange("p (a b) -> p a b", a=2), ['b2'], ['ksA']); yield
                    elif kind == 'kwvw':
                        cp('act', VW[:, t, :, 0:64], bank(pj)[:, 128:256].rearrange("p (g d) -> p g d", g=2), [pk], ['VW']); yield
                        kwb = tmb[:, 0:128].rearrange("p (h d) -> p h d", h=2)
                        yield from head_norm_rope_gen(bank(pj)[:, 0:128], 2, 128, gkw3, cos_t[:, t, :], sin_t[:, t, :], kwb, pk, tk,
                                                      ['g_nsa_kw_gain'], par)
                        for g in range(2):
                            tr(ptb[0:64, g * 128:(g + 1) * 128], tmb[:, g * 64:(g + 1) * 64], identb, [tk, 'identb'], ['b2'])
                        cp('act', kwT[0:64, :, tsl], ptb[0:64, 0:256].rearrange("p (a b) -> p a b", a=2), ['b2'], ['kwT']); yield
                    elif kind == 'km':
                        kmb = tmb.rearrange("p (h d) -> p h d", h=8)
                        yield from head_norm_rope_gen(bank(pj), 8, 128, gkm3, cos_t[:, t, :], sin_t[:, t, :], kmb, pk, tk,
                                                      ['g_moba_k_gain'], par)
                        for hp in range(4):
                            tr(ptb[:, hp * 128:(hp + 1) * 128], tmb[:, hp * 128:(hp + 1) * 128], identb, [tk, 'identb'], ['b2'])
                        cp('act', kmT[:, :, tsl], ptb[:, 0:512].rearrange("p (a b) -> p a b", a=4), ['b2'], ['kmT']); yield
                    else:
                        cp('act', VM[:, t, :, 0:64], bank(pj).rearrange("p (h d) -> p h d", h=8), [pk], ['VM']); yield

                mmB(0)
                mmB(1)
                for pr in range(NT // 2):
                    if pr + 1 < NT // 2:
                        mmB(2 * pr + 2)
                        mmB(2 * pr + 3)
                    run_interleaved([postB(2 * pr), postB(2 * pr + 1)])
            Sx.op('dve', lambda e: e.tensor_reduce(out=kmean_f, in_=kmT.rearrange("p h (n k) -> p h n k", k=256),
                                                   axis=AX.X, op=ALU.add), ['kmT'], ['kmean_f'])
            ts('dve', kmean, kmean_f, 1.0 / 256, None, ALU.mult, None, ['kmean_f'], ['kmean'])
            memset('dve', kmean_z, 0.0, ['kmean_z'])
            kz4 = kmean_z.rearrange("p (hp two) n -> p hp two n", two=2)
            cp('dve', kz4[0:64, :, 0, :], kmean[0:64, :, :], ['kmean', 'kmean_z'], ['kmean_z'])
            cp('dve', kz4[64:128, :, 1, :], kmean[64:128, :, :], ['kmean', 'kmean_z'], ['kmean_z'])

            checkpoint('B', [('ksA', ksA), ('kwT', kwT), ('kmT', kmT), ('VS', VS), ('VW', VW), ('VM', VM), ('kvT', kvT), ('kmean', kmean_f)])
            for kv, (w1_d, w2_d) in enumerate(((ckw1_d, ckw2_d), (cvw1_d, cvw2_d))):
                dma_cast('w1', w1sb, w1_d.rearrange("(i d) h -> d i h", d=64), writes=['w1sb'])
                dma_cast('w2', w2sb[:, kv, :, :], w2_d.rearrange("(hh p) d -> p hh d", p=128), writes=['w2sb'])
                for hh in range(2):
                    for i in range(32):
                        mm(bank(7)[:, hh:hh + 1], w1sb[:, i, hh * 128:(hh + 1) * 128], peT[:, kv, i:i + 1], i == 0, i == 31,
                           ['w1sb', 'peT'], ['b7'])
                cp('dve', hb, bank(7)[:, 0:2], ['b7'], ['hb'])
                for g in range(2):
                    src = kvT[:, kv * 2 + g, :].rearrange("p (c s) -> p c s", s=16)
                    for hh in range(2):
                        pj = next_pj()
                        pk = 'b%d' % pj
                        for i in range(32):
                            rhs = src[:, 0:127, i] if i < 16 else src[:, 1:128, i - 16]
                            mm(bank(pj)[:, 0:127], w1sb[:, i, hh * 128:(hh + 1) * 128], rhs, i == 0, i == 31, ['w1sb', 'kvT'], [pk])
                        act(hsb[:, hh, :], bank(pj)[:, 0:127], AF.Silu, [pk, 'hb'], ['hsb'], bias=hb[:, hh:hh + 1])
                    pj = next_pj()
                    pk = 'b%d' % pj
                    for hh in range(2):
                        mm(bank(pj)[0:127, 0:64], hsb[:, hh, :], w2sb[:, kv, hh, :], hh == 0, hh == 1, ['hsb', 'w2sb'], [pk])
                    if kv == 0:
                        kcb = tm_b[0:127, 0:64].rearrange("p (h d) -> p h d", h=1)
                        head_norm_rope(bank(pj)[0:127, 0:64], 1, 127, bc(gtile["nsa_kc_gain"][0:127].unsqueeze(1), [127, 1, 64]),
                                       cosc, sinc, kcb, pk, 'tm_b0', ['g_nsa_kc_gain'])
                        tr(bank_bf(2)[0:64, 0:127], tm_b[0:127, 0:64], identb[0:127, 0:127], ['tm_b0', 'identb'], ['b2'])
                        cp('act', kcT[:, g, :], bank_bf(2)[0:64, 0:127], ['b2'], ['kcT'])
                    else:
                        cp('act', VC[:, g, 0:64], bank(pj)[0:127, 0:64], [pk], ['VC'])

            checkpoint('C', [('kcT', kcT), ('VC', VC)])
            AR.release(BC)
            Sx.barrier()
            wq = AR.alloc(128, [8, 1048], BF16)
            QAs = [AR.alloc(96, [2, 512], BF16) for _ in range(2)]
            QMs = [AR.alloc(128, [4, 128], BF16) for _ in range(2)]
            qn_b = AR.alloc(128, [8, 64], BF16)
            qm_b = AR.alloc(128, [8, 64], BF16)
            sigs = [AR.alloc(128, [8, 3], F32) for _ in range(2)]
            Ec = AR.alloc(127, [512], BF16)
            Em = AR.alloc(127, [512], BF16)
            NPT = 6
            PT = [AR.alloc(128, [512], BF16) for _ in range(NPT)]
            rs4 = AR.alloc(128, [4], F32)
            coef = AR.alloc(128, [4], F32)
            impn = AR.alloc(128, [32], F32)
            top8 = AR.alloc(128, [8], F32)
            selb = AR.alloc(128, [32], F32)
            Bpads = [AR.alloc(128, [96], BF16) for _ in range(2)]
            rs4c = AR.alloc(128, [4], F32)
            coefc = AR.alloc(128, [4], F32)
            yaccs = [AR.alloc(128, [8, 64], F32) for _ in range(2)]
            ytmp = AR.alloc(128, [4, 64], F32)
            oacc = AR.alloc(128, [8, 65], F32)
            scm = AR.alloc(128, [8, 8], F32)
            topm = AR.alloc(128, [8, 8], F32)
            selm = AR.alloc(128, [8, 8], F32)
            rsm = AR.alloc(128, [8], F32)
            y_b = AR.alloc(128, [512], BF16)
            y_b2 = AR.alloc(128, [512], BF16)

            dma_cast('wq', wq[:, :, 0:512], win_d[:, 0:512].rearrange("(c p) n -> p c n", p=128), writes=['wq'])
            dma_cast('wq', wq[:, :, 512:536], win_d[:, 1280:1304].rearrange("(c p) n -> p c n", p=128), writes=['wq'])
            dma_cast('wq', wq[:, :, 536:1048], win_d[:, 1304:1816].rearrange("(c p) n -> p c n", p=128), writes=['wq'])
            memset('dve', Bpads[0], 0.0, ['Bpad0'])
            memset('dve', Bpads[1], 0.0, ['Bpad1'])
            gq3 = bc(gq8.unsqueeze(1), [128, 8, 64])
            gm3 = bc(gm8.unsqueeze(1), [128, 8, 64])
            ptc = [0]

            def next_pt():
                ptc[0] = (ptc[0] + 1) % NPT
                return PT[ptc[0]], 'PT%d' % ptc[0]

            psc = [0]

            PSB = [3, 4, 0, 1]

            def next_ps():
                psc[0] = (psc[0] + 1) % len(PSB)
                return PSB[psc[0]]

            ptb = bank_bf(2)

            qlist0 = list(range(NT) if qts is None else qts)
            PAR = {q: i % 2 for i, q in enumerate(qlist0)}

            def front_a(qt):
                tsl = slice(qt * 128, (qt + 1) * 128)
                hk = 'hT%d' % qt
                for kc in range(8):
                    mm(bank(0), hT[:, kc, tsl], wq[:, kc, 0:512], kc == 0, kc == 7, [hk, 'wq'], ['b0'])
                for kc in range(8):
                    mm(bank(1), hT[:, kc, tsl], wq[:, kc, 536:1048], kc == 0, kc == 7, [hk, 'wq'], ['b1'])
                for kc in range(8):
                    mm(bank(7)[:, 0:24], hT[:, kc, tsl], wq[:, kc, 512:536], kc == 0, kc == 7, [hk, 'wq'], ['b7'])

            def front_b1(qt):
                p = PAR[qt]
                sig2 = sigs[p].rearrange("p h c -> p (h c)")
                sk = 'sig%d' % p
                act(sig2, bank(7)[:, 0:24], AF.Exp, ['b7'], [sk], scale=-1.0)
                ts('dve', sig2, sig2, 1.0, None, ALU.add, None, [sk], [sk])
                recip(sig2, sig2, [sk], [sk])
                run_interleaved([
                    head_norm_rope_gen(bank(0), 8, 128, gq3, cos_t[:, qt, :], sin_t[:, qt, :], qn_b, 'b0', 'qn_b', ['gq8'], 0),
                    head_norm_rope_gen(bank(1), 8, 128, gm3, cos_t[:, qt, :], sin_t[:, qt, :], qm_b, 'b1', 'qm_b', ['gm8'], 1)])

            def front_b2(qt):
                p = PAR[qt]
                QA, QM = QAs[p], QMs[p]
                for h in range(8):
                    tr(ptb[0:64, h * 128:(h + 1) * 128], qn_b[:, h, :], identb, ['qn_b', 'identb'], ['b2'])
                cp('act', QA[0:64, :, :], ptb[0:64, :].rearrange("p (g n) -> p g n", g=2), ['b2'], ['QA%d' % p])
                qm2 = qm_b.rearrange("p h d -> p (h d)")
                for hp in range(4):
                    tr(ptb[:, hp * 128:(hp + 1) * 128], qm2[:, hp * 128:(hp + 1) * 128], identb, ['qm_b', 'identb'], ['b2'])
                cp('act', QM, ptb[:, 0:512].rearrange("p (a b) -> p a b", a=4), ['b2'], ['QM%d' % p])

            def cs1(qt, g):
                p = PAR[qt]
                QA = QAs[p]
                ps_i = next_ps()
                pk = 'b%d' % ps_i
                mm(bank(ps_i)[0:127, :], kcT[:, g, :], QA[0:64, g, :], True, True, ['kcT', 'QA%d' % p], [pk])
                act(Ec, bank(ps_i)[0:127, :], AF.Exp, [pk], ['Ec'])
                tt('pool', Em.rearrange("p (r t) -> p r t", r=4), Ec.rearrange("p (r t) -> p r t", r=4),
                   bc(cmask[:, qt, :].unsqueeze(1), [127, 4, 128]), ALU.mult, ['Ec', 'cmask'], ['Em'])

            def cs2(qt, g):
                p = PAR[qt]
                sig, yacc = sigs[p], yaccs[p]
                sk = 'sig%d' % p
                po = bank(7)[:, 0:388].rearrange("p (r c) -> p r c", r=4)
                for r in range(4):
                    mm(po[:, r, :], Em[:, r * 128:(r + 1) * 128], VC[:, g, :], True, True, ['Em', 'VC', 'VC_o'], ['b7'])
                ts('dve', rs4c, po[:, :, 96], 1e-30, None, ALU.max, None, ['b7'], ['rs4c'])
                recip(rs4c, rs4c, ['rs4c'], ['rs4c'])
                ts('dve', impn, po[:, 0, 64:96], rs4c[:, 0:1], None, ALU.mult, None, ['b7', 'rs4c'], ['impn'])
                for r in range(1, 4):
                    stt(impn, po[:, r, 64:96], rs4c[:, r:r + 1], impn, ALU.mult, ALU.add, ['b7', 'rs4c', 'impn'], ['impn'])
                tt('dve', coefc, rs4c, sig[:, 4 * g:4 * g + 4, 0], ALU.mult, ['rs4c', sk], ['coefc'])
                tt('dve', yacc[:, 4 * g:4 * g + 4, :], po[:, :, 0:64], bc(coefc.unsqueeze(2), [128, 4, 64]), ALU.mult,
                   ['b7', 'coefc'], ['yacc%d_%d' % (p, g)])
                tt('dve', impn, impn, fbs[:, qt, :], ALU.add, ['impn', 'fbs'], ['impn'])
                Sx.op('dve', lambda e: e.max(out=top8, in_=impn), ['impn'], ['top8'])
                ts('dve', selb, impn, top8[:, 7:8], -NEG, ALU.is_ge, ALU.mult, ['impn', 'top8'], ['selb'])
                ts('dve', Bpads[g][:, 64:96], selb, NEG, None, ALU.add, None, ['selb'], ['Bpad%d' % g])

            def cs3(qt, g):
                p = PAR[qt]
                QA = QAs[p]
                tr(ptb[0:96, 0:128], Bpads[g], identb, ['Bpad%d' % g, 'identb'], ['b2'])
                cp('act', QA[64:96, g, :].rearrange("p (r t) -> p r t", r=4),
                   bc(ptb[64:96, 0:128].unsqueeze(1), [32, 4, 128]), ['b2'], ['QAb%d' % p])

            def cmp_sel(qt):
                for g in range(2):
                    cs1(qt, g)
                    cs2(qt, g)
                    cs3(qt, g)

            def nsa_units(qt, g, kind):
                p = PAR[qt]
                QA, sig, yacc = QAs[p], sigs[p], yaccs[p]
                if kind == 'sel':
                    kts = [(kt, trib4 if kt == qt else None) for kt in range(qt + 1)]
                    K, kT, kkey, Vt, vkey, gate_idx, bk = 96, ksA, 'ksA', VS, 'VS', 1, 5 + g
                    qkeys = ['QA%d' % p, 'QAb%d' % p, 'ksA_e']
                else:
                    kts = []
                    for kt in range(max(0, qt - 4), qt + 1):
                        m = trib4 if kt == qt else (antib4 if kt == qt - 4 else None)
                        kts.append((kt, m))
                    K, kT, kkey, Vt, vkey, gate_idx, bk = 64, kwT, 'kwT', VW, 'VW', 2, 5
                    qkeys = ['QA%d' % p]
                po2 = bank(bk)[:, 0:260].rearrange("p (r c) -> p r c", r=4)
                pok = 'b%d' % bk
                units = []
                nk = len(kts)
                for j, (kt, mask) in enumerate(kts):
                    st = {}

                    def qk(j=j, kt=kt, mask=mask, st=st):
                        ps_i = next_ps()
                        pk = 'b%d' % ps_i
                        mm(bank(ps_i), kT[0:K, g, kt * 128:(kt + 1) * 128], QA[0:K, g, :], True, mask is None, [kkey] + qkeys, [pk])
                        if mask is not None:
                            mm(bank(ps_i), identb, mask, False, True, ['identb', 'trib4', 'antib4'], [pk])
                        pt, ptk = next_pt()
                        act(pt, bank(ps_i), AF.Exp, [pk], [ptk])
                        st['pt'] = (pt, ptk)

                    def pv(j=j, kt=kt, st=st):
                        pt, ptk = st['pt']
                        for r in range(4):
                            mm(po2[:, r, :], pt[:, r * 128:(r + 1) * 128], Vt[:, kt, g, :], (j == 0 and r == 0), j == nk - 1,
                               [ptk, vkey], [pok], skip=True)
                        if j == nk - 1:
                            ts('dve', rs4, po2[:, :, 64], 1e-30, None, ALU.max, None, [pok], ['rs4'])
                            recip(rs4, rs4, ['rs4'], ['rs4'])
                            tt('dve', coef, rs4, sig[:, 4 * g:4 * g + 4, gate_idx], ALU.mult, ['rs4', 'sig%d' % p], ['coef'])
                            tt('dve', ytmp, po2[:, :, 0:64], bc(coef.unsqueeze(2), [128, 4, 64]), ALU.mult, [pok, 'coef'], ['ytmp'])
                            yk = 'yacc%d_%d' % (p, g)
                            tt('dve', yacc[:, 4 * g:4 * g + 4, :], yacc[:, 4 * g:4 * g + 4, :], ytmp, ALU.add, ['ytmp', yk], [yk])
                    units.append((qk, pv))
                return units

            def moba_sel(qt):
                p = PAR[qt]
                QM = QMs[p]
                psel = bank(7)[:, 0:64].rearrange("p (h n) -> p h n", h=8)
                for h in range(8):
                    mm(psel[:, h, :], QM[:, h // 2, :], kmean_z[:, h, :], True, True, ['QM%d' % p, 'kmean_z'], ['b7'])
                tt('dve', scm, psel, bc(fbm[:, qt, :].unsqueeze(1), [128, 8, 8]), ALU.add, ['b7', 'fbm'], ['scm'])
                for h in range(8):
                    Sx.op('dve', lambda e, h=h: e.max(out=topm[:, h, :], in_=scm[:, h, :]), ['scm'], ['topm'])
                tt('dve', selm, scm, bc(topm[:, :, 2:3], [128, 8, 8]), ALU.is_ge, ['scm', 'topm'], ['selm'])

            mbc = [0]

            def moba_units(qt):
                p = PAR[qt]
                QM = QMs[p]
                own = qt // 2
                use_sel = own >= 4
                units = []
                for h in range(8):
                    hp, base = h // 2, 64 * (h % 2)
                    blocks = list(range(own + 1))
                    groups = [blocks[i:i + 2] for i in range(0, len(blocks), 2)]
                    hst = {}
                    for gi, grp in enumerate(groups):
                        tiles = []
                        for n in grp:
                            for kt in (2 * n, 2 * n + 1):
                                if kt <= qt:
                                    tiles.append((n, kt))
                        st = {}

                        def qk(hp=hp, base=base, tiles=tiles, st=st):
                            ps_i = next_ps()
                            pk = 'b%d' % ps_i
                            for j, (n, kt) in enumerate(tiles):
                                diag = (kt == qt)
                                mm(bank(ps_i)[:, j * 128:(j + 1) * 128], kmT[base:base + 64, hp, kt * 128:(kt + 1) * 128],
                                   QM[base:base + 64, hp, :], True, not diag, ['kmT', 'QM%d' % p], [pk], skip=True)
                                if diag:
                                    mm(bank(ps_i)[:, j * 128:(j + 1) * 128], identb, trib4[:, 0:128], False, True,
                                       ['identb', 'trib4'], [pk], skip=True)
                            pt, ptk = next_pt()
                            W = 128 * len(tiles)
                            act(pt[:, 0:W], bank(ps_i)[:, 0:W], AF.Exp, [pk], [ptk])
                            st['pt'] = (pt, ptk)

                        def pv(h=h, gi=gi, grp=grp, tiles=tiles, st=st, hst=hst, ngroups=len(groups)):
                            pt, ptk = st['pt']
                            ok = 'oacc%d' % h
                            if not use_sel:
                                if gi == 0:
                                    mbc[0] ^= 1
                                    hst['bk'] = 5 + mbc[0]
                                bk = hst['bk']
                                bkk = 'b%d' % bk
                                pm = bank(bk)[:, 0:65]
                                for j, (n, kt) in enumerate(tiles):
                                    mm(pm, pt[:, j * 128:(j + 1) * 128], VM[:, kt, h, :], (gi == 0 and j == 0),
                                       (gi == ngroups - 1 and j == len(tiles) - 1), [ptk, 'VM'], [bkk])
                                if gi == ngroups - 1:
                                    cp('dve', oacc[:, h, :], pm, [bkk], [ok])
                                return
                            mbc[0] ^= 1
                            bk = 5 + mbc[0]
                            bkk = 'b%d' % bk
                            for bi, n in enumerate(grp):
                                pm = bank(bk)[:, bi * 65:(bi + 1) * 65]
                                tl = [(j, kt) for j, (nn, kt) in enumerate(tiles) if nn == n]
                                for jj, (j, kt) in enumerate(tl):
                                    mm(pm, pt[:, j * 128:(j + 1) * 128], VM[:, kt, h, :], jj == 0, jj == len(tl) - 1, [ptk, 'VM'], [bkk])
                            for bi, n in enumerate(grp):
                                pm = bank(bk)[:, bi * 65:(bi + 1) * 65]
                                if n == 0:
                                    ts('dve', oacc[:, h, :], pm, selm[:, h, n:n + 1], None, ALU.mult, None, [bkk, 'selm'], [ok])
                                elif n != own:
                                    stt(oacc[:, h, :], pm, selm[:, h, n:n + 1], oacc[:, h, :], ALU.mult, ALU.add, [bkk, 'selm', ok], [ok])
                                else:
                                    tt('dve', oacc[:, h, :], pm, oacc[:, h, :], ALU.add, [bkk, ok], [ok])
                        units.append((qk, pv))
                return units

            def finish_nsa(qt):
                p = PAR[qt]
                tsl = slice(qt * 128, (qt + 1) * 128)
                cp('act', y_b, yaccs[p].rearrange("p h d -> p (h d)"), ['yacc%d_0' % p, 'yacc%d_1' % p], ['y_b'])
                for c in range(4):
                    tr(ptb[:, c * 128:(c + 1) * 128], y_b[:, c * 128:(c + 1) * 128], identb, ['y_b', 'identb'], ['b2'])
                cp('act', yTn[:, :, tsl], ptb[:, 0:512].rearrange("p (a b) -> p a b", a=4), ['b2'], ['yTn%d' % qt])

            def finish_moba(qt):
                tsl = slice(qt * 128, (qt + 1) * 128)
                oks = ['oacc%d' % h for h in range(8)]
                ts('dve', rsm, oacc[:, :, 64], 1e-30, None, ALU.max, None, oks, ['rsm'])
                recip(rsm, rsm, ['rsm'], ['rsm'])
                tt('dve', y_b2.rearrange("p (h d) -> p h d", h=8), oacc[:, :, 0:64], bc(rsm.unsqueeze(2), [128, 8, 64]), ALU.mult,
                   oks + ['rsm'], ['y_b2'])
                for c in range(4):
                    tr(ptb[:, c * 128:(c + 1) * 128], y_b2[:, c * 128:(c + 1) * 128], identb, ['y_b2', 'identb'], ['b2'])
                cp('act', yTm[:, :, tsl], ptb[:, 0:512].rearrange("p (a b) -> p a b", a=4), ['b2'], ['yTm%d' % qt])

            DEPTH = 3

            def run_units(units, hooks):
                n = len(units)
                for i in range(n + DEPTH):
                    for fn in hooks.pop(i, []):
                        fn()
                    if i < n:
                        units[i][0]()
                    if i - DEPTH >= 0:
                        units[i - DEPTH][1]()
                for i in sorted(hooks):
                    for fn in hooks[i]:
                        fn()

            qlist = list(range(NT) if qts is None else qts)
            front_a(qlist[0])
            front_b1(qlist[0])
            front_b2(qlist[0])
            cmp_sel(qlist[0])
            for qi, qt in enumerate(qlist):
                nxt = qlist[qi + 1] if qi + 1 < len(qlist) else None
                prev = qlist[qi - 1] if qi > 0 else None
                if qt // 2 >= 4:
                    moba_sel(qt)
                win = nsa_units(qt, 0, 'win') + nsa_units(qt, 1, 'win')
                mob = moba_units(qt)
                sel = nsa_units(qt, 0, 'sel') + nsa_units(qt, 1, 'sel')
                units = win + mob + sel
                nW, nM = len(win), len(mob)
                hooks = {}

                def addh(i, fn):
                    hooks.setdefault(i, []).append(fn)
                if prev is not None:
                    addh(1, lambda prev=prev: finish_nsa(prev))
                addh(nW + nM + DEPTH, lambda qt=qt: finish_moba(qt))
                if nxt is not None:
                    addh(nW + 1, lambda nxt=nxt: (front_a(nxt), front_b1(nxt)))
                    addh(nW + nM, lambda nxt=nxt: front_b2(nxt))
                    base = nW + nM + 1
                    addh(base, lambda nxt=nxt: cs1(nxt, 0))
                    addh(base + 2, lambda nxt=nxt: cs2(nxt, 0))
                    addh(base + 3, lambda nxt=nxt: cs1(nxt, 1))
                    addh(base + 5, lambda nxt=nxt: cs3(nxt, 0))
                    addh(base + 5, lambda nxt=nxt: cs2(nxt, 1))
                    addh(base + 8, lambda nxt=nxt: cs3(nxt, 1))
                run_units(units, hooks)
            finish_nsa(qlist[-1])
            QA, QM, sig, yacc = QAs[0], QMs[0], sigs[0], yaccs[0]

            checkpoint('D', [('yTn', yTn), ('yTm', yTm), ('QA', QA), ('QM', QM), ('sig', sig), ('impn', impn), ('yacc', yacc), ('oacc', oacc), ('selm', selm)])
            AR.release(ATT)
            Sx.barrier()
            xres = AR.alloc(128, [NT, D], F32)
            XE = AR.mark()
            mT = AR.alloc(128, [8, 1024], BF16)
            wE = [(AR.alloc(128, [8, 128], BF16), AR.alloc(128, [8, 128], BF16),
                   AR.alloc(128, [4, 128], BF16), AR.alloc(128, [4, 128], BF16)) for _ in range(2)]
            wo = AR.alloc(128, [8, 512], BF16)
            sga = AR.alloc(128, [512], BF16)
            sgb = AR.alloc(128, [512], BF16)
            m1 = AR.alloc(128, [512], F32)
            m2 = AR.alloc(128, [512], F32)
            xst = [AR.alloc(128, [512], F32) for _ in range(2)]
            yTn_all = ['yTn%d' % t for t in range(NT)]
            yTm_all = ['yTm%d' % t for t in range(NT)]
            for half in range(2):
                for c in range(8):
                    wa, wbb, wun, wum = wE[c % 2]
                    wk = 'wE%d' % (c % 2)
                    dma_cast(wk, wa, win_d[:, 2840 + c * 128:2840 + (c + 1) * 128].rearrange("(c p) n -> p c n", p=128), writes=[wk])
                    dma_cast(wk, wbb, win_d[:, 3864 + c * 128:3864 + (c + 1) * 128].rearrange("(c p) n -> p c n", p=128), writes=[wk])
                    dma_cast(wk, wun, wupn_d[:, c * 128:(c + 1) * 128].rearrange("(c p) n -> p c n", p=128), writes=[wk])
                    dma_cast(wk, wum, wupm_d[:, c * 128:(c + 1) * 128].rearrange("(c p) n -> p c n", p=128), writes=[wk])
                    for tg in range(2):
                        t0 = half * 1024 + tg * 512
                        cs = slice(t0, t0 + 512)
                        hk = ['hT%d' % t for t in range(t0 // 128, t0 // 128 + 4)]
                        ynk = ['yTn%d' % t for t in range(t0 // 128, t0 // 128 + 4)]
                        ymk = ['yTm%d' % t for t in range(t0 // 128, t0 // 128 + 4)]
                        for kc in range(8):
                            mm(bank(0), wa[:, kc, :], hT[:, kc, cs], kc == 0, kc == 7, hk + [wk], ['b0'])
                        for kc in range(8):
                            mm(bank(1), wbb[:, kc, :], hT[:, kc, cs], kc == 0, kc == 7, hk + [wk], ['b1'])
                        for kc in range(4):
                            mm(bank(3), wun[:, kc, :], yTn[:, kc, cs], kc == 0, kc == 3, ynk + [wk], ['b3'])
                        for kc in range(4):
                            mm(bank(4), wum[:, kc, :], yTm[:, kc, cs], kc == 0, kc == 3, ymk + [wk], ['b4'])
                        act(sga, bank(0), AF.Sigmoid, ['b0'], ['sga'])
                        act(sgb, bank(1), AF.Sigmoid, ['b1'], ['sgb'])
                        tt('dve', m1, bank(3), sga, ALU.mult, ['b3', 'sga'], ['m1'])
                        tt('dve', m2, bank(4), sgb, ALU.mult, ['b4', 'sgb'], ['m2'])
                        tt('dve', mT[:, c, tg * 512:(tg + 1) * 512], m1, m2, ALU.add, ['m1', 'm2'], ['mT'])
                for ch in range(2):
                    dma_cast('wo', wo, wout_d[:, ch * 512:(ch + 1) * 512].rearrange("(c p) n -> p c n", p=128), writes=['wo'])
                    for tl in range(8):
                        t = half * 8 + tl
                        xk = 'xst%d' % (tl % 2)
                        dma_sp(xk, xst[tl % 2], x_d[b, t * 128:(t + 1) * 128, ch * 512:(ch + 1) * 512], writes=[xk])
                        pj = next_pj()
                        pk = 'b%d' % pj
                        for kc in range(8):
                            mm(bank(pj), mT[:, kc, tl * 128:(tl + 1) * 128], wo[:, kc, :], kc == 0, kc == 7, ['mT', 'wo'], [pk])
                        tt('dve', xres[:, t, ch * 512:(ch + 1) * 512], bank(pj), xst[tl % 2], ALU.add, [pk, xk], ['xres%d' % t])

            checkpoint('E', [('xres', xres)])
            AR.release(XE)
            Sx.barrier()
            xn_sqs = [AR.alloc(128, [1024], F32) for _ in range(2)]
            xn_bs = [AR.alloc(128, [1024], BF16) for _ in range(2)]
            xn_b = None
            wfo = AR.alloc(128, [NFC, 512], BF16)
            wgu = [(AR.alloc(128, [8, 128], BF16), AR.alloc(128, [8, 128], BF16)) for _ in range(2)]
            sgt = AR.alloc(128, [512], BF16)
            ost = [AR.alloc(128, [512], F32) for _ in range(2)]
            pst = AR.alloc(128, [256], F32)
            pstb = AR.alloc(128, [256], BF16)
            sgp = AR.alloc(128, [512], F32)
            sv = AR.mark()
            AR.release(PERSIST)
            h2T = AR.alloc(128, [8, 1024], BF16)
            actT = AR.alloc(128, [NFC, 1024], BF16)
            pT = AR.alloc(128, [2, 1024], BF16)
            assert AR.off <= ATT, (AR.off, ATT)
            AR.off = sv
            wpg = wfo[:, 0:16, :].rearrange("p (k c) n -> p k (c n)", k=8)
            wpp = AR.alloc(128, [2, 1024], BF16)

            for half in range(2):
                for tl in range(8):
                    t = half * 8 + tl
                    norm_transpose(xres[:, t, :], gvec["ffn"], h2T[:, :, tl * 128:(tl + 1) * 128], 'xres%d' % t, 'h2T', xn_b, 2)
                for fc in range(NFC):
                    wg_, wu_ = wgu[fc % 2]
                    wk = 'wgu%d' % (fc % 2)
                    dma_cast(wk, wg_, wfi_d[:, fc * 128:(fc + 1) * 128].rearrange("(c p) n -> p c n", p=128), writes=[wk])
                    dma_cast(wk, wu_, wfi_d[:, DFF + fc * 128:DFF + (fc + 1) * 128].rearrange("(c p) n -> p c n", p=128), writes=[wk])
                    for tg in range(2):
                        cs = slice(tg * 512, (tg + 1) * 512)
                        bg, bu = (0, 1) if tg == 0 else (3, 4)
                        for kc in range(8):
                            mm(bank(bg), wg_[:, kc, :], h2T[:, kc, cs], kc == 0, kc == 7, ['h2T', wk], ['b%d' % bg])
                        for kc in range(8):
                            mm(bank(bu), wu_[:, kc, :], h2T[:, kc, cs], kc == 0, kc == 7, ['h2T', wk], ['b%d' % bu])
                        act(sgt, bank(bg), AF.Silu, ['b%d' % bg], ['sgt'])
                        tt('dve', actT[:, fc, cs], bank(bu), sgt, ALU.mult, ['b%d' % bu, 'sgt'], ['actT'])
                for ch in range(2):
                    wfo_src = wfo_d[:, ch * 512:(ch + 1) * 512].rearrange("(c p) n -> p c n", p=128)
                    dma_cast('wfoA', wfo[:, 0:11, :], wfo_src[:, 0:11, :], writes=['wfoA'])
                    dma_cast('wfoB', wfo[:, 11:NFC, :], wfo_src[:, 11:NFC, :], writes=['wfoB'])
                    for tl in range(8):
                        t = half * 8 + tl
                        pj = 5 + (tl % 2)
                        pk = 'b%d' % pj
                        for fc in range(NFC):
                            mm(bank(pj), actT[:, fc, tl * 128:(tl + 1) * 128], wfo[:, fc, :], fc == 0, fc == NFC - 1,
                               ['actT', 'wfoA' if fc < 11 else 'wfoB'], [pk])
                        xr = xres[:, t, ch * 512:(ch + 1) * 512]
                        tt('dve', xr, bank(pj), xr, ALU.add, [pk, 'xres%d' % t], ['xres%d' % t])
                for tl in range(8):
                    t = half * 8 + tl
                    norm_transpose(xres[:, t, :], gvec["ple"], h2T[:, :, tl * 128:(tl + 1) * 128], 'xres%d' % t, 'h2T', xn_b, 2)
                    dma_sp('pst', pst, p_d[b, t * 128:(t + 1) * 128, :], writes=['pst'])
                    cp('dve', pstb, pst, ['pst'], ['pstb'])
                    for c in range(2):
                        tr(bank_bf(7)[:, c * 128:(c + 1) * 128], pstb[:, c * 128:(c + 1) * 128], identb, ['pstb', 'identb'], ['b7'])
                    cp('act', pT[:, :, tl * 128:(tl + 1) * 128], bank_bf(7)[:, 0:256].rearrange("p (a b) -> p a b", a=2), ['b7'], ['pT'])
                dma_cast('wfoA', wpg, wpg_d.rearrange("(c p) n -> p c n", p=128), writes=['wfoA', 'wfoB'])
                dma_cast('wpp', wpp, wpp_d.rearrange("(c p) n -> p c n", p=128), writes=['wpp'])
                for tl in range(8):
                    t = half * 8 + tl
                    tls = slice(tl * 128, (tl + 1) * 128)
                    for ch in range(2):
                        ccs = slice(ch * 512, (ch + 1) * 512)
                        bg, bp = (0, 1) if ch == 0 else (3, 4)
                        for kc in range(8):
                            mm(bank(bg), h2T[:, kc, tls], wpg[:, kc, ccs], kc == 0, kc == 7, ['h2T', 'wfoA', 'wfoB'], ['b%d' % bg])
                        for kc in range(2):
                            mm(bank(bp), pT[:, kc, tls], wpp[:, kc, ccs], kc == 0, kc == 1, ['pT', 'wpp'], ['b%d' % bp])
                        act(sgp, bank(bg), AF.Sigmoid, ['b%d' % bg], ['sgp'])
                        ob = ost[ch]
                        ok = 'ost%d' % ch
                        tt('dve', ob, bank(bp), sgp, ALU.mult, ['b%d' % bp, 'sgp'], [ok])
                        tt('dve', ob, ob, xres[:, t, ccs], ALU.add, [ok, 'xres%d' % t], [ok])
                        dma_sp('out%d' % ch, out_d[b, t * 128:(t + 1) * 128, ccs], ob, reads=[ok], writes=['outd'])
        except _Stop:
            pass
        Sx.barrier()

        keys = Sx.sem_keys()
        sems = {k: es.enter_context(nc.semaphore(k.replace(':', '_'))) for k in keys}
        with nc.Block() as block:
            run = Sx.runner(sems)
            block.sync(run('sp'))
            block.tensor(run('pe'))
            block.scalar(run('act'))
            block.vector(run('dve'))
            block.gpsimd(run('pool'))
    build_program.last_dbg = dbg_outs
    return nc


def _consts():
    c = {}
    c["c_ident"] = np.eye(128, dtype=np.float32)
    k = np.arange(128)[:, None]
    t = np.arange(128)[None, :]
    c["c_tri"] = (k <= t).astype(np.float32)
    c["c_anti"] = (k > t).astype(np.float32)
    c["c_trib4"] = np.tile(np.where(k <= t, 0.0, NEG).astype(np.float32), (1, 4))
    c["c_antib4"] = np.tile(np.where(k > t, 0.0, NEG).astype(np.float32), (1, 4))
    cc = np.arange(127)[:, None]
    tt_ = np.arange(S)[None, :]
    c["c_cmask"] = ((16 * cc + 31) <= tt_).astype(np.float32)
    tl = np.arange(128)[:, None, None]
    qt = np.arange(NT)[None, :, None]
    j = np.arange(32)[None, None, :]
    cur = (qt * 128 + tl) // 64
    forced = (j == 0) | (j == cur) | (j == cur - 1)
    fb = np.where(forced, 1e9, np.where(j > cur, -1e9, 0.0)).astype(np.float32)
    c["c_fbs"] = np.ascontiguousarray(fb.reshape(128, NT * 32))
    n = np.arange(8)[None, None, :]
    own = (qt // 2)
    fm = np.where(n >= own, -1e9, 0.0).astype(np.float32) + np.zeros((128, 1, 1), np.float32)
    c["c_fbm"] = np.ascontiguousarray(fm.reshape(128, NT * 8))
    cs = np.arange(127) * 16
    ce = cs + 31
    jj = np.arange(32)
    ov = ((cs[:, None] <= jj[None, :] * 64 + 63) & (ce[:, None] >= jj[None, :] * 64)).astype(np.float32)
    c["c_ovl"] = np.concatenate([ov, np.ones((127, 1), np.float32)], axis=1)
    half = 8
    c["c_invf"] = (500000.0 ** (-np.arange(half, dtype=np.float32) / half)).astype(np.float32).reshape(1, 8)
    c["c_emat"] = (np.arange(S)[None, :] // 64 == np.arange(32)[:, None]).astype(np.float32)
    return c


_NC_CACHE = {}


def make_in_maps(inputs, n_cores=8):
    consts = _consts()
    f = lambda a: np.ascontiguousarray(np.asarray(a))
    shared = dict(consts)
    shared["g_mix"] = f(inputs["g_mix"][0]).reshape(8, 128)
    shared["g_ffn"] = f(inputs["g_ffn"][0]).reshape(8, 128)
    shared["g_ple"] = f(inputs["g_ple"][0]).reshape(8, 128)
    for n in ("nsa_q_gain", "nsa_kc_gain", "nsa_ks_gain", "nsa_kw_gain", "moba_q_gain", "moba_k_gain"):
        shared[n] = f(inputs[n][0]).reshape(1, 64)
    for n in ("w_in", "nsa_pe_k", "nsa_pe_v", "nsa_ck_w1", "nsa_ck_w2", "nsa_cv_w1", "nsa_cv_w2", "w_up_nsa", "w_up_moba",
              "w_out", "w_ffn_in", "w_ffn_out", "w_ple_gate", "w_ple_proj"):
        shared[n] = f(inputs[n][0])
    x = np.asarray(inputs["x"])
    p = np.asarray(inputs["p"])[0]
    pos = np.asarray(inputs["positions"]).astype(np.int32)
    in_maps = []
    for c in range(n_cores):
        m = dict(shared)
        m["x"] = f(x[c * NB:(c + 1) * NB])
        m["p"] = f(p[c * NB:(c + 1) * NB])
        m["pos"] = f(pos[c * NB:(c + 1) * NB]).reshape(NB, NT, 128)
        in_maps.append(m)
    return in_maps


def kernel(**inputs):
    n_cores = 8
    if "nc" not in _NC_CACHE:
        _NC_CACHE["nc"] = build_program()
    nc = _NC_CACHE["nc"]
    in_maps = make_in_maps(inputs, n_cores)
    res = run_bass_kernel_spmd(nc, in_maps, core_ids=list(range(n_cores)))
    out = np.concatenate([np.asarray(r["out"]) for r in res.results], axis=0)
    return out.astype(np.float32)
```
